# Optimizing a Trainium2 kernel written in Bass

```python
import jax, jax.numpy as jnp
from jax import lax
import numpy as np

D_MODEL = 1024
BATCH = 4
SEQ = 8192
DEPTH = 1
DEC_BATCH = 32
DEC_SEQ = 16
PAST_LEN = 2048

CHUNK = 64
QBLK = 128
N_HEADS = 8
QK_NOPE = 64
QK_ROPE = 32
V_DIM = 64
Q_LORA = 384
KV_LORA = 256
MLA_WIDTH = N_HEADS * V_DIM
RG_WIDTH = D_MODEL - MLA_WIDTH
RG_BLOCKS = 8
RG_BLOCK_DIM = RG_WIDTH // RG_BLOCKS
RG_CONV = 4
RG_C = 8.0
FF_DIM = 2816
FF_CONV = 3
ROPE_THETA = 10000.0
EPS = 1e-6
NEG = -1e30
SCALE = (QK_NOPE + QK_ROPE) ** -0.5
IN_COLS = Q_LORA + KV_LORA + QK_ROPE + 2 * RG_WIDTH

kernel_name = "hymba_mla_rglru_convffn_stream_step"


def rmsnorm(x, g):
    x32 = x.astype(jnp.float32)
    y = x32 * lax.rsqrt(jnp.mean(x32 * x32, axis=-1, keepdims=True) + EPS)
    return y.astype(x.dtype) * g


def rope(x, pos):
    half = QK_ROPE // 2
    inv = ROPE_THETA ** (-jnp.arange(half, dtype=jnp.float32) / half)
    ang = pos.astype(jnp.float32)[:, None] * inv[None, :]
    cos = jnp.cos(ang).astype(x.dtype)
    sin = jnp.sin(ang).astype(x.dtype)
    if x.ndim == 4:
        cos = cos[:, None, :]
        sin = sin[:, None, :]
    x1 = x[..., :half]
    x2 = x[..., half:]
    return jnp.concatenate([x1 * cos - x2 * sin, x2 * cos + x1 * sin], axis=-1)


def causal_dwconv(x, buf, w, b):
    width = w.shape[0]
    t = x.shape[1]
    xp = jnp.concatenate([buf.astype(x.dtype), x], axis=1)
    out = b
    for k in range(width):
        out = out + xp[:, k:k + t] * w[k]
    return out, xp[:, xp.shape[1] - (width - 1):]


def mla_block(q_lat, q_rope, c_kv, k_rope, q_pos, k_pos):
    s = jnp.einsum("bqhr,bkr->bhqk", q_lat, c_kv) + jnp.einsum("bqhe,bke->bhqk", q_rope, k_rope)
    s = s.astype(jnp.float32) * SCALE
    mask = (k_pos[None, :] // CHUNK) <= (q_pos[:, None] // CHUNK)
    s = jnp.where(mask[None, None], s, NEG)
    p = jax.nn.softmax(s, axis=-1).astype(c_kv.dtype)
    return jnp.einsum("bhqk,bkr->bqhr", p, c_kv)


def mla_attend(q_lat, q_rope, c_kv, k_rope, q_pos, k_pos):
    b, t = q_lat.shape[0], q_lat.shape[1]
    if t > QBLK and t % QBLK == 0:
        nb = t // QBLK
        ql = jnp.moveaxis(q_lat.reshape(b, nb, QBLK, N_HEADS, KV_LORA), 1, 0)
        qr = jnp.moveaxis(q_rope.reshape(b, nb, QBLK, N_HEADS, QK_ROPE), 1, 0)
        qp = q_pos.reshape(nb, QBLK)

        def one_block(args):
            a, r, p = args
            return mla_block(a, r, c_kv, k_rope, p, k_pos)

        o = lax.map(one_block, (ql, qr, qp))
        return jnp.moveaxis(o, 0, 1).reshape(b, t, N_HEADS, KV_LORA)
    return mla_block(q_lat, q_rope, c_kv, k_rope, q_pos, k_pos)


def linear_scan(a, u, h0):
    u = u.at[:, 0].add(a[:, 0] * h0)

    def comb(l, r):
        al, ul = l
        ar, ur = r
        return al * ar, ar * ul + ur

    _, h = lax.associative_scan(comb, (a, u), axis=1)
    return h


def layer(x, pos, past_ckv, past_krope, past_pos, h0, rg_buf, ff_buf, p):
    (norm_mix_g, w_in, q_norm_g, w_uq, kv_norm_g, w_uk, w_uv, w_rg_conv, b_rg_conv,
     w_rg_a, b_rg_a, w_rg_i, b_rg_i, rg_lambda, w_out, norm_ffn_g, w_ffn_up,
     w_ffn_conv, b_ffn_conv, w_ffn_down) = p
    b, t, _ = x.shape
    xn = rmsnorm(x, norm_mix_g)
    proj = xn @ w_in
    o1 = Q_LORA
    o2 = o1 + KV_LORA
    o3 = o2 + QK_ROPE
    o4 = o3 + RG_WIDTH
    c_q = proj[..., :o1]
    c_kv_raw = proj[..., o1:o2]
    k_rope_raw = proj[..., o2:o3]
    rg_x = proj[..., o3:o4]
    rg_gate = proj[..., o4:]

    q = jnp.einsum("btr,rhd->bthd", rmsnorm(c_q, q_norm_g), w_uq)
    q_nope = q[..., :QK_NOPE]
    q_rope = rope(q[..., QK_NOPE:], pos)
    c_kv = rmsnorm(c_kv_raw, kv_norm_g)
    k_rope = rope(k_rope_raw, pos)
    if past_ckv is None:
        all_ckv, all_krope, k_pos = c_kv, k_rope, pos
    else:
        all_ckv = jnp.concatenate([past_ckv.astype(c_kv.dtype), c_kv], axis=1)
        all_krope = jnp.concatenate([past_krope.astype(k_rope.dtype), k_rope], axis=1)
        k_pos = jnp.concatenate([past_pos, pos])
    q_lat = jnp.einsum("bthn,rhn->bthr", q_nope, w_uk)
    o_lat = mla_attend(q_lat, q_rope, all_ckv, all_krope, pos, k_pos)
    o_mla = jnp.einsum("bthr,rhv->bthv", o_lat, w_uv).reshape(b, t, MLA_WIDTH)

    xc, new_rg_buf = causal_dwconv(rg_x, rg_buf, w_rg_conv, b_rg_conv)
    xb = xc.reshape(b, t, RG_BLOCKS, RG_BLOCK_DIM)
    r = jax.nn.sigmoid(jnp.einsum("btnc,ncd->btnd", xb, w_rg_a).reshape(b, t, RG_WIDTH) + b_rg_a)
    i = jax.nn.sigmoid(jnp.einsum("btnc,ncd->btnd", xb, w_rg_i).reshape(b, t, RG_WIDTH) + b_rg_i)
    log_a = -RG_C * r.astype(jnp.float32) * jax.nn.softplus(-rg_lambda.astype(jnp.float32))
    a = jnp.exp(log_a)
    u = jnp.sqrt(-jnp.expm1(2.0 * log_a)) * (i * xc).astype(jnp.float32)
    h = linear_scan(a, u, h0.astype(jnp.float32))
    o_rg = h.astype(x.dtype) * jax.nn.gelu(rg_gate)

    x = x + jnp.concatenate([o_mla, o_rg], axis=-1) @ w_out

    up = rmsnorm(x, norm_ffn_g) @ w_ffn_up
    upc, new_ff_buf = causal_dwconv(up, ff_buf, w_ffn_conv, b_ffn_conv)
    x = x + (jax.nn.gelu(upc[..., :FF_DIM]) * upc[..., FF_DIM:]) @ w_ffn_down
    return x, c_kv, k_rope, h[:, -1].astype(h0.dtype), new_rg_buf, new_ff_buf


def setup_inputs(seed: int = 0) -> dict:
    key = jax.random.key(seed)
    ks = jax.random.split(key, 32)

    def nrm(k, shape, scale):
        return jax.random.normal(k, shape, jnp.float32) * scale

    u = jax.random.uniform(ks[20], (DEPTH, RG_WIDTH), jnp.float32, 0.9, 0.999)
    s = u ** (1.0 / RG_C)
    rg_lambda = jnp.log(s / (1.0 - s))
    return {
        "x_prompt": nrm(ks[0], (BATCH, SEQ, D_MODEL), 1.0),
        "x_sample": nrm(ks[1], (DEC_BATCH, DEC_SEQ, D_MODEL), 1.0),
        "cache_ckv": nrm(ks[2], (DEPTH, DEC_BATCH, PAST_LEN, KV_LORA), 1.0),
        "cache_krope": nrm(ks[3], (DEPTH, DEC_BATCH, PAST_LEN, QK_ROPE), 1.0),
        "state_rg_h": nrm(ks[4], (DEPTH, DEC_BATCH, RG_WIDTH), 0.5),
        "state_rg_conv": nrm(ks[5], (DEPTH, DEC_BATCH, RG_CONV - 1, RG_WIDTH), 1.0),
        "state_ffn_conv": nrm(ks[6], (DEPTH, DEC_BATCH, FF_CONV - 1, 2 * FF_DIM), 1.0),
        "norm_mix_g": 1.0 + nrm(ks[7], (DEPTH, D_MODEL), 0.02),
        "w_in": nrm(ks[8], (DEPTH, D_MODEL, IN_COLS), D_MODEL ** -0.5),
        "q_norm_g": 1.0 + nrm(ks[9], (DEPTH, Q_LORA), 0.02),
        "w_uq": nrm(ks[10], (DEPTH, Q_LORA, N_HEADS, QK_NOPE + QK_ROPE), Q_LORA ** -0.5),
        "kv_norm_g": 1.0 + nrm(ks[11], (DEPTH, KV_LORA), 0.02),
        "w_uk": nrm(ks[12], (DEPTH, KV_LORA, N_HEADS, QK_NOPE), KV_LORA ** -0.5),
        "w_uv": nrm(ks[13], (DEPTH, KV_LORA, N_HEADS, V_DIM), KV_LORA ** -0.5),
        "w_rg_conv": nrm(ks[14], (DEPTH, RG_CONV, RG_WIDTH), RG_CONV ** -0.5),
        "b_rg_conv": nrm(ks[15], (DEPTH, RG_WIDTH), 0.02),
        "w_rg_a": nrm(ks[16], (DEPTH, RG_BLOCKS, RG_BLOCK_DIM, RG_BLOCK_DIM), RG_BLOCK_DIM ** -0.5),
        "b_rg_a": nrm(ks[17], (DEPTH, RG_WIDTH), 0.02),
        "w_rg_i": nrm(ks[18], (DEPTH, RG_BLOCKS, RG_BLOCK_DIM, RG_BLOCK_DIM), RG_BLOCK_DIM ** -0.5),
        "b_rg_i": nrm(ks[19], (DEPTH, RG_WIDTH), 0.02),
        "rg_lambda": rg_lambda,
        "w_out": nrm(ks[21], (DEPTH, D_MODEL, D_MODEL), D_MODEL ** -0.5),
        "norm_ffn_g": 1.0 + nrm(ks[22], (DEPTH, D_MODEL), 0.02),
        "w_ffn_up": nrm(ks[23], (DEPTH, D_MODEL, 2 * FF_DIM), D_MODEL ** -0.5),
        "w_ffn_conv": nrm(ks[24], (DEPTH, FF_CONV, 2 * FF_DIM), FF_CONV ** -0.5),
        "b_ffn_conv": nrm(ks[25], (DEPTH, 2 * FF_DIM), 0.02),
        "w_ffn_down": nrm(ks[26], (DEPTH, FF_DIM, D_MODEL), FF_DIM ** -0.5),
        "final_norm_g": 1.0 + nrm(ks[27], (D_MODEL,), 0.02),
    }


def reference(x_prompt, x_sample, cache_ckv, cache_krope, state_rg_h, state_rg_conv, state_ffn_conv,
              norm_mix_g, w_in, q_norm_g, w_uq, kv_norm_g, w_uk, w_uv, w_rg_conv, b_rg_conv,
              w_rg_a, b_rg_a, w_rg_i, b_rg_i, rg_lambda, w_out, norm_ffn_g, w_ffn_up,
              w_ffn_conv, b_ffn_conv, w_ffn_down, final_norm_g):
    bp, sp = x_prompt.shape[0], x_prompt.shape[1]
    sd = x_sample.shape[1]
    past = cache_ckv.shape[2]
    pos_p = jnp.arange(sp, dtype=jnp.int32)
    past_pos = jnp.arange(past, dtype=jnp.int32)
    pos_s = past + jnp.arange(sd, dtype=jnp.int32)
    yp, ys = x_prompt, x_sample
    pc, pk, ph, prb, pfb = [], [], [], [], []
    sc, sk, sh, srb, sfb = [], [], [], [], []
    for l in range(DEPTH):
        p = (norm_mix_g[l], w_in[l], q_norm_g[l], w_uq[l], kv_norm_g[l], w_uk[l], w_uv[l],
             w_rg_conv[l], b_rg_conv[l], w_rg_a[l], b_rg_a[l], w_rg_i[l], b_rg_i[l], rg_lambda[l],
             w_out[l], norm_ffn_g[l], w_ffn_up[l], w_ffn_conv[l], b_ffn_conv[l], w_ffn_down[l])
        h0_p = jnp.zeros((bp, RG_WIDTH), state_rg_h.dtype)
        rb_p = jnp.zeros((bp, RG_CONV - 1, RG_WIDTH), x_prompt.dtype)
        fb_p = jnp.zeros((bp, FF_CONV - 1, 2 * FF_DIM), x_prompt.dtype)
        yp, c1, k1, h1, r1, f1 = layer(yp, pos_p, None, None, None, h0_p, rb_p, fb_p, p)
        ys, c2, k2, h2, r2, f2 = layer(ys, pos_s, cache_ckv[l], cache_krope[l], past_pos,
                                       state_rg_h[l], state_rg_conv[l], state_ffn_conv[l], p)
        pc.append(c1); pk.append(k1); ph.append(h1); prb.append(r1); pfb.append(f1)
        sc.append(c2); sk.append(k2); sh.append(h2); srb.append(r2); sfb.append(f2)
    y_prompt = rmsnorm(yp, final_norm_g)
    y_sample = rmsnorm(ys, final_norm_g)
    p_ckv = jnp.stack(pc)
    p_krope = jnp.stack(pk)
    p_rg_h = jnp.stack(ph)
    p_rg_conv = jnp.stack(prb)
    p_ffn_conv = jnp.stack(pfb)
    s_ckv = jnp.stack(sc)
    s_krope = jnp.stack(sk)
    s_rg_h = jnp.stack(sh)
    s_rg_conv = jnp.stack(srb)
    s_ffn_conv = jnp.stack(sfb)
    return (y_prompt, y_sample, p_ckv, p_krope, p_rg_h, p_rg_conv, p_ffn_conv,
            s_ckv, s_krope, s_rg_h, s_rg_conv, s_ffn_conv)
```

```python
import numpy as np
from contextlib import ExitStack
import concourse.bass as bass
import concourse.mybir as mybir
from concourse.bass_utils import run_bass_kernel_spmd

F32 = mybir.dt.float32
BF16 = mybir.dt.bfloat16
AF = mybir.ActivationFunctionType
ALU = mybir.AluOpType

D = 1024
NQ = 384
NKV = 256
NRO = 32
RGW = 512
FF = 2816
NH = 8
INC = 1696
SEQ = 8192
HALF = 4096
NTILE = 64
NF = 66
EPS = 1e-6
SCALE = 96.0 ** -0.5
NEGB = -30000.0
NPAIR = 22

C_RGW = 0
C_RGB = 16
C_BA = 20
C_BI = 24
C_LAM = 28
C_FCW = 32
C_FCB = 164
C_GMIX = 208
C_GFFN = 216
C_GQ = 224
C_KB = 227
C_FLAG = 228
C_SMASK = 229
C_NCOL = 240


class Res:
    __slots__ = ("name", "w", "r", "sem", "cnt", "excl", "multi")

    def __init__(self, name, excl=False, multi=False):
        self.name = name
        self.excl = excl
        self.multi = multi
        self.w = []
        self.r = {}
        self.sem = None
        self.cnt = 0


class Ev:
    __slots__ = ("kind", "eng", "idx", "res", "val", "grp")

    def __init__(self, kind, eng, idx, res=None, val=None, grp=None):
        self.kind = kind
        self.eng = eng
        self.idx = idx
        self.res = res
        self.val = val
        self.grp = grp


class Prog:
    ENG = ["pe", "act", "dve", "pool", "sp"]

    def __init__(self, nc, es):
        self.nc = nc
        self.es = es
        self.ops = []
        self.anchors = []
        self.rec = None
        self.engobj = dict(pe=nc.tensor, act=nc.scalar, dve=nc.vector, pool=nc.gpsimd, sp=nc.sync)

    def record(self, f):
        self.rec = []
        f()
        r = self.rec
        self.rec = None
        return r

    def replay(self, lst, n):
        while n > 0 and lst:
            self.add(*lst.pop(0))
            n -= 1

    def replay_step(self, lst, max_ops=12):
        tw = {}
        tr = {}
        n = 0
        while lst and n < max_ops:
            (eng, fn, reads, writes, dma, grp) = lst[0]
            conflict = False
            for w in writes:
                for d in (tw, tr):
                    e = d.get(id(w))
                    if e and (e - {eng}):
                        conflict = True
            for r in reads:
                e = tw.get(id(r))
                if e and (e - {eng}):
                    conflict = True
                if r.excl:
                    e = tr.get(id(r))
                    if e and (e - {eng}):
                        conflict = True
            if conflict and n > 0:
                break
            self.add(*lst.pop(0))
            n += 1
            for w in writes:
                tw.setdefault(id(w), set()).add(eng)
            for r in reads:
                tr.setdefault(id(r), set()).add(eng)

    def replay_sched(self, lst, max_ops=10, window=400):
        tw = {}
        tr = {}
        pw = set()
        pr = set()
        n = 0
        i = 0
        scanned = 0
        while i < len(lst) and n < max_ops and scanned < window:
            (eng, fn, reads, writes, dma, grp) = lst[i]
            scanned += 1
            wid = [id(w) for w in writes] + [id(r) for r in reads if r.excl]
            rid = [id(r) for r in reads if not r.excl]
            ready = True
            for x in wid:
                if x in pw or x in pr:
                    ready = False
                    break
            if ready:
                for x in rid:
                    if x in pw:
                        ready = False
                        break
            if ready:
                for x in wid:
                    e = tw.get(x)
                    if e and (e - {eng}):
                        ready = False
                    e = tr.get(x)
                    if e and (e - {eng}):
                        ready = False
                for x in rid:
                    e = tw.get(x)
                    if e and (e - {eng}):
                        ready = False
            if ready:
                self.add(*lst.pop(i))
                n += 1
                for x in wid:
                    tw.setdefault(x, set()).add(eng)
                for x in rid:
                    tr.setdefault(x, set()).add(eng)
            else:
                pw.update(wid)
                pr.update(rid)
                i += 1

    def add(self, eng, fn, reads=(), writes=(), dma=None, grp=None):
        if self.rec is not None:
            self.rec.append((eng, fn, reads, writes, dma, grp))
            return None
        deps = []
        ex = [r for r in reads if r.excl and r not in writes]
        if ex:
            reads = [r for r in reads if not r.excl]
            writes = list(writes) + ex
        for r in reads:
            deps += r.w
        for w in writes:
            ds = (list(w.r.values()) if w.multi else w.w + list(w.r.values()))
            if w.excl:
                ds = [d for d in ds if d.eng != eng]
            deps += ds
        idx = len(self.ops)
        if dma is not None:
            if dma.sem is None:
                dma.sem = self.es.enter_context(self.nc.semaphore("d_" + dma.name))
                self.anchors.append(dma)
            dma.cnt += 16
            ev = Ev("d", eng, idx, dma, dma.cnt, grp)
            if grp is not None:
                grp.append(ev)
        else:
            ev = Ev("e", eng, idx)
        dd = []
        for d in deps:
            if d.kind == "e" and d.eng == "pe" and eng == "pe":
                continue
            if grp is not None and d.grp is grp:
                continue
            dd.append(d)
        self.ops.append((eng, fn, dd, ev))
        for w in writes:
            if w.multi:
                w.w = w.w + [ev]
            else:
                w.w = [ev]
                w.r = {}
        for r in reads:
            if r in writes:
                continue
            key = eng if ev.kind == "e" else ("d", id(dma))
            r.r[key] = ev
        return ev

    def close_group(self, grp):
        if not grp:
            return
        tot = grp[0].res.cnt
        for ev in grp:
            ev.val = tot

    def emit(self):
        nc = self.nc
        esem = {e: self.es.enter_context(nc.semaphore("e_" + e)) for e in self.ENG}
        need = set()
        for (eng, fn, dd, ev) in self.ops:
            for d in dd:
                if d.kind == "e":
                    need.add(d.idx)
        ms = {}
        cnt = {e: 0 for e in self.ENG}
        for i, (eng, fn, dd, ev) in enumerate(self.ops):
            if i in need:
                cnt[eng] += 1
                ms[i] = cnt[eng]
        seen = {e: {} for e in self.ENG}
        for i, (eng, fn, dd, ev) in enumerate(self.ops):
            E = self.engobj[eng]
            waits = {}
            for d in dd:
                if d.kind == "e":
                    key = d.eng
                    val = ms[d.idx]
                    sem = esem[d.eng]
                else:
                    key = ("d", id(d.res))
                    val = d.val
                    sem = d.res.sem
                if seen[eng].get(key, 0) >= val:
                    continue
                if key not in waits or waits[key][1] < val:
                    waits[key] = (sem, val)
            for key, (sem, val) in waits.items():
                seen[eng][key] = val
                E.wait_ge(sem, val)
            ins = fn()
            if i in need:
                ins.then_inc(esem[eng], 1)
            if ev.kind == "d":
                ins.then_inc(ev.res.sem, 16)
        for a in self.anchors:
            nc.sync.wait_ge(a.sem, a.cnt)
        self.stats = dict(cnt)
        self.stats["nops"] = len(self.ops)


class _Stop(Exception):
    pass


_STOP = [None]


def _ck(name):
    if _STOP[0] == name:
        raise _Stop()


def build_program():
    nc = bass.Bass("TRN2", target_bir_lowering=False)
    es = ExitStack()
    P = Prog(nc, es)

    def din(name, shape, dt=F32):
        return nc.dram_tensor(name, list(shape), dt, kind="ExternalInput").ap()

    def dout(name, shape, dt=F32):
        return nc.dram_tensor(name, list(shape), dt, kind="ExternalOutput").ap()

    xp = din("xp", [NF * 128, D])
    tabd = din("tabd", [NF, 128, 32])
    cossd = din("cossd", [64, 16])
    sinsd = din("sinsd", [64, 16])
    cst = din("cst", [128, C_NCOL])
    gfin_d = din("gfin", [128, D])
    gkv_d = din("gkv", [128, NKV])
    ident_d = din("ident", [128, 128])
    w_in_d = din("w_in", [128, 8, INC])
    w_uq_d = din("w_uq", [128, 3, 768])
    w_out_d = din("w_out", [128, 8, D])
    w_ukt_d = din("w_ukt", [128, 8, 256])
    w_uv_d = din("w_uv", [128, 2, 512])
    w_rga_d = din("w_rga", [128, 4, 128])
    w_rgi_d = din("w_rgi", [128, 4, 128])
    w_up_d = din("w_up", [128, 8, 2 * FF])
    w_dn_d = din("w_dn", [128, NPAIR, D])
    xs_d = din("xs", [64, D])
    cckv_d = din("cckv", [NTILE, 128, NKV])
    ckr_d = din("ckr", [NTILE, 128, NRO])
    srh_d = din("srh", [128, 4, 4])
    src_d = din("src", [128, 4, 4, 3])
    sfc_d = din("sfc", [128, 44, 4, 2])

    y_d = dout("y", [HALF, D])
    pckv_d = dout("pckv", [HALF, NKV])
    pkr_d = dout("pkr", [HALF, NRO])
    prh_d = dout("prh", [128, 4])
    prc_d = dout("prc", [128, 4, 3])
    pfc_d = dout("pfc", [128, 44, 2])
    ys_d = dout("ys", [64, D])
    sckv_d = dout("sckv", [64, NKV])
    skr_d = dout("skr", [64, NRO])
    srh_o = dout("srho", [128, 4, 4])
    src_o = dout("srco", [128, 4, 4, 3])
    sfc_o = dout("sfco", [128, 44, 4, 2])

    wup_s = nc.dram_tensor("wup_s", [128, 8, 2 * FF], BF16, kind="Internal").ap()
    wdn_s = nc.dram_tensor("wdn_s", [128, NPAIR, D], BF16, kind="Internal").ap()
    wrg_s = nc.dram_tensor("wrg_s", [128, 8, 1024], BF16, kind="Internal").ap()
    wout_s = nc.dram_tensor("wout_s", [128, 8, D], BF16, kind="Internal").ap()

    def sb(name, shape, dt):
        return es.enter_context(nc.sbuf_tensor(name, list(shape), dt))

    W_IN = sb("W_IN", [128, 8, 672], BF16)
    W_UQ = sb("W_UQ", [128, 3, 768], BF16)
    W_UKT = sb("W_UKT", [128, 8, 256], BF16)
    W_UV = sb("W_UV", [128, 2, 512], BF16)
    W_RGA = sb("W_RGA", [128, 4, 128], BF16)
    W_RGI = sb("W_RGI", [128, 4, 128], BF16)
    IDB = sb("IDB", [128, 128], BF16)
    GFIN = sb("GFIN", [128, D], F32)
    GKV = sb("GKV", [128, NKV], F32)
    TAB = sb("TAB", [128, 2, 32], F32)
    COSS = sb("COSS", [64, 16], F32)
    SINS = sb("SINS", [64, 16], F32)
    CST = sb("CST", [128, C_NCOL], F32)
    EPSB = sb("EPSB", [128, 1], F32)
    ONEB = sb("ONEB", [128, 1], F32)
    CL = sb("CL", [128, 8], F32)
    NB = sb("NB", [128, 8], F32)
    KTC = sb("KTC", [128, 2, NF * 128], BF16)
    KTR = sb("KTR", [128, NF * 128], BF16)
    V = sb("V", [128, NF, 257], BF16)
    KTCN = sb("KTCN", [128, 2, 64], BF16)
    KTRN = sb("KTRN", [128, 64], BF16)
    VN = sb("VN", [64, 257], BF16)
    XB = sb("XB", [128, 2, D], F32)
    XN = sb("XN", [128, D], BF16)
    XNT2 = [sb("XNT0", [128, 8, 256], BF16)]
    XNT2.append(XNT2[0])
    QTCH = sb("QTCH", [128, 2, 8, 2], BF16)
    QTRH = sb("QTRH", [128, 8, 2], BF16)
    ONH = sb("ONH", [16, 256], BF16)
    OTH = sb("OTH", [128, 2, 8, 2], BF16)
    CATMH = sb("CATMH", [128, 4, 2], BF16)
    CATRH = sb("CATRH", [128, 4, 2], BF16)
    XH2 = sb("XH2", [128, 8, 2], BF16)
    HG = sb("HG", [128, 4, 2], F32)
    HL = sb("HL", [128, 4, 2], F32)
    PTH = sb("PTH", [128, 64], BF16)
    RCPH = sb("RCPH", [16, 1], F32)
    XN2T = sb("XN2T", [128, 8, 258], BF16)
    CKV = sb("CKV", [128, 1, NKV], F32)
    KR = sb("KR", [128, 1, NRO], F32)
    KRB4 = sb("KRB4", [128, 2, 4, NRO], BF16)
    KRB = KRB4[:, 0, 0, :]
    ST = sb("ST", [128, 32], F32)
    T1 = sb("T1", [128, 8, 32], F32)
    T2 = sb("T2", [128, 8, 32], F32)
    CQN = sb("CQN", [128, NQ], BF16)
    CQNT = sb("CQNT", [128, 3, 128], BF16)
    QNT = sb("QNT", [128, 4, 128], BF16)
    QTC2 = [sb("QTC%d" % i, [128, 2, 8, 128], BF16) for i in range(2)]
    QR = sb("QR", [128, 8, 32], BF16)
    QTR2 = [sb("QTR%d" % i, [128, 8, 128], BF16) for i in range(2)]
    NPT = 2
    PT = [sb("PT%d" % i, [128, 512], BF16) for i in range(NPT)]
    PTD = [sb("PTD%d" % i, [128, 4, 128], BF16) for i in range(1)]
    ON = sb("ON", [128, 4, 256], BF16)
    OT2 = [sb("OT%d" % i, [128, 2, 8, 128], BF16) for i in range(2)]
    RCP = sb("RCP", [128, 8], F32)
    OML = sb("OML", [128, 512], BF16)
    CATM = sb("CATM", [128, 4, 256], BF16)
    CATR = [sb("CATR%d" % i, [128, 4, 256], BF16) for i in range(2)]
    XTMP = sb("XTMP", [128, D], F32)
    RX = sb("RX", [128, 4, 4 * 19 + 200], F32)
    RGT = {n: sb("RG_" + n, [128, 256], F32) for n in ["XC", "R", "I", "A", "M", "G"]}
    XCB = sb("XCB", [128, 256], BF16)
    HST = sb("HST", [128, 4, 4], F32)
    WU = [sb("WU%d" % i, [128, 8, 256], BF16) for i in range(2)]
    WD = [sb("WD%d" % i, [128, D], BF16) for i in range(2)]
    UA = [sb("UA%d" % i, [128, 72], F32) for i in range(1)]
    UBf = [sb("UB%d" % i, [128, 72], F32) for i in range(1)]
    ACA = [sb("ACA%d" % i, [128, 256], F32) for i in range(2)]
    ACB = [sb("ACB%d" % i, [128, 256], F32) for i in range(2)]
    GA = [sb("GA%d" % i, [128, 256], F32) for i in range(1)]
    HT = [sb("HT%d" % i, [128, 256], BF16) for i in range(2)]
    UH = [sb("UH%d" % i, [128, 44, 4, 2], F32) for i in range(2)]

    PS = [es.enter_context(nc.psum_tensor("PS%d" % i, [128, 512], F32)) for i in range(8)]

    def psb(b):
        return PS[b][:].bitcast(BF16)

    R = {}

    def res(name):
        if name not in R:
            R[name] = Res(name)
        return R[name]

    PSR = [Res("ps%d" % b, excl=True) for b in range(8)]

    def psr(b, lo=0, hi=512):
        return [PSR[b]]

    KVR = [res("kv%d" % i) for i in range(NF)]
    SETUP = res("setup")
    sgrp = []

    def MM(out, lhsT, rhs, start, stop, reads, writes):
        P.add("pe", lambda: nc.tensor.matmul(out, lhsT, rhs, start=start, stop=stop), reads, writes)

    def TR(out, in_, ident, reads, writes):
        P.add("pe", lambda: nc.tensor.transpose(out, in_, ident), reads, writes)

    def ACT(out, in_, func, reads, writes, bias=None, scale=None, accum=None):
        kw = {}
        if bias is not None:
            kw["bias"] = bias
        if scale is not None:
            kw["scale"] = scale
        if accum is not None:
            kw["accum_out"] = accum
        P.add("act", lambda: nc.scalar.activation(out=out, in_=in_, func=func, **kw), reads, writes)

    def TS(eng, out, in0, s1, s2, op0, op1, reads, writes):
        e = nc.vector if eng == "dve" else nc.gpsimd
        if op1 is None:
            P.add(eng, lambda: e.tensor_scalar(out=out, in0=in0, scalar1=s1, scalar2=None, op0=op0), reads, writes)
        else:
            P.add(eng, lambda: e.tensor_scalar(out=out, in0=in0, scalar1=s1, scalar2=s2, op0=op0, op1=op1),
                  reads, writes)

    def STT(out, in0, scalar, in1, op0, op1, reads, writes):
        P.add("dve", lambda: nc.vector.scalar_tensor_tensor(out=out, in0=in0, scalar=scalar, in1=in1,
                                                           op0=op0, op1=op1), reads, writes)

    def TT(eng, out, in0, in1, op, reads, writes):
        e = nc.vector if eng == "dve" else nc.gpsimd
        P.add(eng, lambda: e.tensor_tensor(out=out, in0=in0, in1=in1, op=op), reads, writes)

    def CP(eng, out, in_, reads, writes):
        if eng == "act":
            P.add("act", lambda: nc.scalar.copy(out=out, in_=in_), reads, writes)
        else:
            e = nc.vector if eng == "dve" else nc.gpsimd
            P.add(eng, lambda: e.tensor_copy(out=out, in_=in_), reads, writes)

    def DMA(q, out, in_, reads, writes, anchor, grp=None, slow=False):
        e = nc.sync if q == "sp" else nc.gpsimd
        if slow:
            P.add(q, lambda: e.dma_start(out=out, in_=in_, allow_slow_non_contiguous=True), reads, writes,
                  dma=anchor, grp=grp)
        else:
            P.add(q, lambda: e.dma_start(out=out, in_=in_), reads, writes, dma=anchor, grp=grp)

    def MEMSET(eng, ap, val, writes):
        e = nc.vector if eng == "dve" else nc.gpsimd
        P.add(eng, lambda: e.memset(ap, val), (), writes)

    rW = res("weights")
    rC = res("consts")
    for (dst, src) in [(GFIN, gfin_d), (GKV, gkv_d), (COSS, cossd), (SINS, sinsd),
                       (CST, cst)]:
        DMA("sp", dst[:], src, (), [rC], SETUP, grp=sgrp)
    SETUP_P = res("setup_p")
    sgrp_p = []
    for (dst, src) in [(W_UKT, w_ukt_d), (W_UV, w_uv_d), (W_RGA, w_rga_d), (W_RGI, w_rgi_d),
                       (IDB, ident_d)]:
        DMA("pool", dst[:], src, (), [rW], SETUP_P, grp=sgrp_p)
    P.close_group(sgrp)
    P.close_group(sgrp_p)

    STG = KTC[:].rearrange("p a s -> p (a s)")[:, 0:16384].bitcast(F32).rearrange("p (a s) -> p a s", a=4)
    STGB = V[:].rearrange("p a s -> p (a s)")[:, 0:2 * 2048].rearrange("p (a s) -> p a s", a=2)
    rS = [res("stg%d" % i) for i in range(4)]
    rSB = [res("stgb%d" % i) for i in range(2)]
    rWU = res("wup_s")
    rWU.multi = True
    rWD = res("wdn_s")
    rWD.multi = True
    si = 0
    rWRG = res("wrg_s")
    rWRG.multi = True
    bi_ = 0
    for k in range(8):
        sl = si % 4
        si += 1
        bl = bi_ % 2
        bi_ += 1
        DMA("sp", STG[:, sl, 0:INC], w_in_d[:, k, :], (), [rS[sl]], rS[sl])
        TS("dve", W_IN[:, k, :], STG[:, sl, 0:672], CST[:, C_GMIX + k:C_GMIX + k + 1], None, ALU.mult, None,
           [rS[sl], rC], [rW])
        sv = STG[:, sl, 672:INC].rearrange("p (t c n) -> p c t n", t=2, c=4)
        dv = STGB[:, bl, 0:1024].rearrange("p (c t n) -> p c t n", c=4, t=2)
        TS("dve", dv, sv, CST[:, C_GMIX + k:C_GMIX + k + 1], None, ALU.mult, None, [rS[sl], rC], [rSB[bl]])
        DMA("sp", wrg_s[:, k, :], STGB[:, bl, 0:1024], [rSB[bl]], [rWRG], rSB[bl])
    for k in range(3):
        sl = si % 4
        si += 1
        DMA("sp", STG[:, sl, 0:768], w_uq_d[:, k, :], (), [rS[sl]], rS[sl])
        TS("dve", W_UQ[:, k, :], STG[:, sl, 0:768], CST[:, C_GQ + k:C_GQ + k + 1], None, ALU.mult, None,
           [rS[sl], rC], [rW])
    for k in range(8):
        for (c0, c1) in [(0, 2048), (2048, 4096), (4096, 2 * FF)]:
            sl = si % 4
            si += 1
            bl = bi_ % 2
            bi_ += 1
            n = c1 - c0
            DMA("sp", STG[:, sl, 0:n], w_up_d[:, k, c0:c1], (), [rS[sl]], rS[sl])
            TS("dve", STGB[:, bl, 0:n], STG[:, sl, 0:n], CST[:, C_GFFN + k:C_GFFN + k + 1], None, ALU.mult, None,
               [rS[sl], rC], [rSB[bl]])
            DMA("sp", wup_s[:, k, c0:c1], STGB[:, bl, 0:n], [rSB[bl]], [rWU], rSB[bl])
    for j in range(0, NPAIR, 2):
        bl = bi_ % 2
        bi_ += 1
        DMA("pool", STGB[:, bl, :], w_dn_d[:, j:j + 2, :].rearrange("p a d -> p (a d)"), (), [rSB[bl]],
            res("stgbl%d" % bl))
        DMA("sp", wdn_s[:, j:j + 2, :].rearrange("p a d -> p (a d)"), STGB[:, bl, :], [rSB[bl]], [rWD], rSB[bl])

    rWO = res("wout_s")
    rWO.multi = True
    for k in range(0, 8, 2):
        bl = bi_ % 2
        bi_ += 1
        DMA("pool", STGB[:, bl, :], w_out_d[:, k:k + 2, :].rearrange("p a d -> p (a d)"), (), [rSB[bl]],
            res("stgbl%d" % bl))
        DMA("sp", wout_s[:, k:k + 2, :].rearrange("p a d -> p (a d)"), STGB[:, bl, :], [rSB[bl]], [rWO], rSB[bl])

    rT = res("setup_tmp")
    ACT(ST[:, 0:4], CST[:, C_LAM:C_LAM + 4], AF.Exp, [rC], [rT], scale=-1.0)
    MEMSET("pool", EPSB[:], EPS, [rC])
    MEMSET("pool", ONEB[:], 1.0, [rC])
    ACT(ST[:, 4:8], ST[:, 0:4], AF.Ln, [rT, rC], [rT], bias=ONEB[:, 0:1])
    TS("dve", CL[:, 0:4], ST[:, 4:8], -8.0, None, ALU.mult, None, [rT], [rC])
    TS("dve", CL[:, 4:8], ST[:, 4:8], -16.0, None, ALU.mult, None, [rT], [rC])
    TS("dve", NB[:, 0:4], CST[:, C_BA:C_BA + 4], -1.0, None, ALU.mult, None, [rC], [rC])
    TS("dve", NB[:, 4:8], CST[:, C_BI:C_BI + 4], -1.0, None, ALU.mult, None, [rC], [rC])

    rPTD = [res("ptd0")]
    MEMSET("pool", PTD[0][:], 0.0, [rPTD[0]])
    rQT2 = [res("qt0"), res("qt1")]
    for i in range(2):
        MEMSET("pool", QTR2[i][:], 0.0, [rQT2[i]])
    rQTH = res("qth")
    MEMSET("pool", QTRH[:], 0.0, [rQTH])
    rHSTc = [res("hst%d" % c) for c in range(4)]
    rHST = rHSTc
    MEMSET("pool", HST[:], 0.0, rHSTc)
    rRXc = [res("rxh%d" % c) for c in range(4)]
    rRXH = rRXc
    MEMSET("pool", RX[:], 0.0, rRXc)
    rUH = [res("uh0"), res("uh1")]
    MEMSET("pool", UH[0][:], 0.0, [rUH[0]])
    MEMSET("pool", UH[1][:], 0.0, [rUH[1]])
    rVN = res("vn")
    MEMSET("pool", V[:, :, 256:257], 1.0, KVR + rS + rSB)
    MEMSET("pool", KTR[:, :], 0.0, KVR)
    MEMSET("pool", KTRN[:, :], 0.0, [rVN])
    MEMSET("pool", VN[:, 256:257], 1.0, [rVN])

    rXB = [res("xb0"), res("xb1")]
    rXN = res("xn")
    rXNT2 = [[res("xnt0_%d" % s_) for s_ in range(2)]]
    rXNT2.append(rXNT2[0])
    rXN2T = [res("xn2t0"), res("xn2t1")]
    rTAB = [res("tab0"), res("tab1")]
    stc = [0]

    def stcol():
        c = 8 + (stc[0] % 24)
        stc[0] += 1
        return c

    def rstd_from(Pn, src_ap, src_res, dim, junk_ap, junk_res):
        c = stcol()
        rr = res("stc%d" % c)
        ACT(junk_ap, src_ap, AF.Square, src_res, [junk_res, rr], accum=ST[:Pn, c:c + 1])
        ACT(ST[:Pn, c:c + 1], ST[:Pn, c:c + 1], AF.Ln, [rr], [rr], scale=1.0 / dim, bias=EPSB[:Pn, 0:1])
        ACT(ST[:Pn, c:c + 1], ST[:Pn, c:c + 1], AF.Exp, [rr], [rr], scale=-0.5)
        return ST[:Pn, c:c + 1], rr

    rXTMP = res("xtmp")

    def stage_a(xsrc, Pn, s, tab_src, use_tmp=False, xi=0):
        XNT, rXNT = XNT2[xi], rXNT2[xi]
        if use_tmp:
            xb, rxb = XTMP[:Pn, :], rXTMP
        else:
            xb, rxb = XB[:Pn, s, :], rXB[s]
        DMA("sp", xb, xsrc, (), [rxb], rxb)
        if tab_src is not None:
            DMA("sp", TAB[:Pn, s, :], tab_src, (), [rTAB[s]], rTAB[s])
        rs, rr = rstd_from(Pn, xb, [rxb], D, XN[:Pn, :], rXN)
        TS("dve", XN[:Pn, :], xb, rs, None, ALU.mult, None, [rxb, rr], [rXN])
        pb = psb(7)
        for k in range(8):
            TR(pb[:, k * 128:k * 128 + Pn], XN[:Pn, k * 128:(k + 1) * 128], IDB[:Pn, :Pn], [rXN, rW], psr(7))
        CP("act", XNT[:, :, s * Pn:(s + 1) * Pn], pb[:, 0:1024].rearrange("p (k t) -> p k t", k=8)[:, :, 0:Pn],
           psr(7), [rXNT[s]])

    rCKV = res("ckv")
    rKR = res("kr")
    rKRB = res("krb4_0")
    rT1 = res("t1")
    rT2 = res("t2")
    rOML = res("oml")

    def kv_side(Pn, s, cos_ap, sin_ap, tab_res, with_q, dst, out_ckv=None, out_kr=None, xi=0):
        XNT, rXNT = XNT2[xi], rXNT2[xi]
        tok = slice(s * Pn, (s + 1) * Pn)
        if with_q:
            for k in range(8):
                MM(PS[6][:Pn, 0:NQ], XNT[:, k, tok], W_IN[:, k, 0:NQ], k == 0, k == 7, [rXNT[s], rW], psr(6))
        if dst is None:
            return
        (ktc_ap, ktr_ap, v_ap, kvres) = dst
        for k in range(8):
            MM(PS[7][:Pn, 0:288], XNT[:, k, tok], W_IN[:, k, NQ:NQ + 288], k == 0, k == 7, [rXNT[s], rW], psr(7))
        rs, rr = rstd_from(Pn, PS[7][:Pn, 0:NKV], psr(7), NKV, OML[:Pn, 0:NKV], rOML)
        STT(CKV[:Pn, 0, :], PS[7][:Pn, 0:NKV], rs, GKV[:Pn, :], ALU.mult, ALU.mult, psr(7) + [rr, rC], [rCKV])
        xr = PS[7][:Pn, 256:288]
        TT("dve", T1[:Pn, 0, :].rearrange("p (a e) -> p a e", a=2), xr.rearrange("p (a e) -> p a e", a=2),
           cos_ap.unsqueeze(1).to_broadcast([Pn, 2, 16]), ALU.mult, psr(7) + tab_res, [rT1])
        TT("dve", T2[:Pn, 0, 0:16], PS[7][:Pn, 272:288], sin_ap, ALU.mult, psr(7) + tab_res, [rT2])
        TT("dve", T2[:Pn, 0, 16:32], PS[7][:Pn, 256:272], sin_ap, ALU.mult, psr(7) + tab_res, [rT2])
        TT("dve", KR[:Pn, 0, 0:16], T1[:Pn, 0, 0:16], T2[:Pn, 0, 0:16], ALU.subtract, [rT1, rT2], [rKR])
        TT("dve", KR[:Pn, 0, 16:32], T1[:Pn, 0, 16:32], T2[:Pn, 0, 16:32], ALU.add, [rT1, rT2], [rKR])
        if out_ckv is not None:
            DMA("sp", out_ckv, CKV[:Pn, 0, :], [rCKV], (), rCKV)
            DMA("sp", out_kr, KR[:Pn, 0, :], [rKR], (), rKR)
        CP("pool", v_ap[:Pn, 0:NKV], CKV[:Pn, 0, :], [rCKV], kvres)
        CP("pool", KRB[:Pn, :], KR[:Pn, 0, :], [rKR], [rKRB])
        pb = psb(7)
        for c in range(2):
            TR(pb[:, c * 128:c * 128 + Pn], v_ap[:Pn, c * 128:(c + 1) * 128], IDB[:Pn, :Pn], kvres + [rW], psr(7))
        TR(pb[0:32, 256:256 + Pn], KRB[:Pn, :], IDB[:Pn, :Pn], [rKRB, rW], psr(7))
        CP("act", ktc_ap, pb[:, 0:256].rearrange("p (c t) -> p c t", c=2)[:, :, 0:Pn], psr(7), kvres)
        CP("act", ktr_ap, pb[0:32, 256:256 + Pn], psr(7), kvres)

    rRG = {n: res("rg_" + n) for n in RGT}
    rXCB = res("xcb")
    rCAT = [res("cat_mla0"), res("cat_mla1")]
    rCATR = [res("cat_rg0"), res("cat_rg1")]
    rWUs = [res("wu0"), res("wu1")]
    wuc = [0]
    rWDs = [res("wd0"), res("wd1")]
    wdc = [0]

    rACA = [res("aca0"), res("aca1")]
    rACB = [res("acb0"), res("acb1")]
    rGA = [res("ga0")]
    rHT = [res("ht0"), res("ht1")]
    rHG = res("hg")
    rHL = res("hl")
    rCATRH = res("catrh")
    RGS = [
        dict(XC=(RGT["XC"], rRG["XC"]), R=(RGT["R"], rRG["R"]), I=(RGT["I"], rRG["I"]), A=(RGT["A"], rRG["A"]),
             M=(RGT["M"], rRG["M"]), G=(RGT["G"], rRG["G"]), XCB=(XCB, rXCB)),
        dict(XC=(ACA[0], rACA[0]), R=(ACA[1], rACA[1]), I=(ACB[0], rACB[0]), A=(ACB[1], rACB[1]),
             M=(GA[0], rGA[0]), G=None, XCB=(HT[0], rHT[0])),
    ]

    def rg_branch(N, nseg, L, full, s_list, par=0, chunks=(0, 1, 2, 3), tset=0, banks=(6, 7), xi=0, wu_slot=None,
                  halo_gate=False):
        XNT, rXNT = XNT2[xi], rXNT2[xi]
        T = RGS[tset]
        (XC_, rXC_), (R_, rR_), (I_, rI_), (A_, rA_), (M_, rM_), (XCB_, rXCB_) = \
            T["XC"], T["R"], T["I"], T["A"], T["M"], T["XCB"]
        bx, bg_ = banks
        W = 3 + L
        rx_reads = [rXNT[s] for s in s_list]
        for c in chunks:
            rRX, rHS = rRXc[c], rHSTc[c]
            if wu_slot is None:
                sl = wuc[0] % 2
                wuc[0] += 1
            else:
                sl = wu_slot
            DMA("sp", WU[sl][:], wrg_s[:, :, c * 256:(c + 1) * 256], [rWRG], [rWUs[sl]], rWUs[sl])
            rxc = RX[:, c, 0:nseg * W].rearrange("p (g w) -> p g w", g=nseg)
            for k in range(8):
                MM(PS[bx][:, 0:N], WU[sl][:, k, 0:128], XNT[:, k, 0:N], k == 0, k == 7,
                   rx_reads + [rWUs[sl]], psr(bx))
            CP("act", rxc[:, :, 3:3 + L], PS[bx][:, 0:N].rearrange("p (g l) -> p g l", g=nseg), psr(bx), [rRX])
            if full:
                (G_, rG_) = T["G"]
                for k in range(8):
                    MM(PS[bg_][:, 0:N], WU[sl][:, k, 128:256], XNT[:, k, 0:N], k == 0, k == 7,
                       rx_reads + [rWUs[sl]], psr(bg_))
                ACT(G_[:, 0:N], PS[bg_][:, 0:N], AF.Gelu_apprx_tanh, psr(bg_), [rG_])
            elif halo_gate:
                for k in range(8):
                    MM(PS[bg_][:, 0:2], WU[sl][:, k, 128:256], XNT[:, k, N - 2:N], k == 0, k == 7,
                       rx_reads + [rWUs[sl]], psr(bg_))
                CP("act", HG[:, c, :], PS[bg_][:, 0:2], psr(bg_), [rHG])
            xc = XC_[:, 0:N].rearrange("p (g l) -> p g l", g=nseg)
            wc = lambda k: CST[:, C_RGW + c * 4 + k:C_RGW + c * 4 + k + 1]
            TS("dve", xc, rxc[:, :, 0:L], wc(0), CST[:, C_RGB + c:C_RGB + c + 1], ALU.mult, ALU.add,
               [rRX, rC], [rXC_])
            for k in range(1, 4):
                STT(xc, rxc[:, :, k:k + L], wc(k), xc, ALU.mult, ALU.add, [rRX, rC, rXC_], [rXC_])
            CP("pool", rxc[:, :, 0:3], rxc[:, :, L:L + 3], [rRX], [rRX])
            CP("act", XCB_[:, 0:N], XC_[:, 0:N], [rXC_], [rXCB_])
            MM(PS[bx][:, 256:256 + N], W_RGA[:, c, :], XCB_[:, 0:N], True, True, [rXCB_, rW], psr(bx))
            MM(PS[bg_][:, 256:256 + N], W_RGI[:, c, :], XCB_[:, 0:N], True, True, [rXCB_, rW], psr(bg_))
            for (X_, rX_, bnk, col) in [(R_, rR_, bx, c), (I_, rI_, bg_, 4 + c)]:
                ACT(X_[:, 0:N], PS[bnk][:, 256:256 + N], AF.Exp, psr(bnk) + [rC], [rX_], scale=-1.0,
                    bias=NB[:, col:col + 1])
                TS("dve", X_[:, 0:N], X_[:, 0:N], 1.0, None, ALU.add, None, [rX_], [rX_])
                P.add("dve", (lambda X_=X_: nc.vector.reciprocal(out=X_[:, 0:N], in_=X_[:, 0:N])), [rX_], [rX_])
            ACT(A_[:, 0:N], R_[:, 0:N], AF.Exp, [rR_, rC], [rA_], scale=CL[:, c:c + 1])
            ACT(M_[:, 0:N], R_[:, 0:N], AF.Exp, [rR_, rC], [rM_], scale=CL[:, 4 + c:5 + c])
            ACT(M_[:, 0:N], M_[:, 0:N], AF.Ln, [rM_], [rM_], scale=-1.0, bias=ONEB[:, 0:1])
            ACT(M_[:, 0:N], M_[:, 0:N], AF.Exp, [rM_], [rM_], scale=0.5)
            TT("dve", I_[:, 0:N], I_[:, 0:N], XC_[:, 0:N], ALU.mult, [rI_, rXC_], [rI_])
            TT("dve", M_[:, 0:N], M_[:, 0:N], I_[:, 0:N], ALU.mult, [rM_, rI_], [rM_])
            for g in range(nseg):
                sl_ = slice(g * L, (g + 1) * L)
                P.add("dve", (lambda g=g, sl_=sl_, c=c: nc.vector.tensor_tensor_scan(
                    out=M_[:, sl_], data0=A_[:, sl_], data1=M_[:, sl_],
                    initial=HST[:, c, g:g + 1], op0=ALU.mult, op1=ALU.add)),
                    [rA_, rM_, rHS], [rM_])
            hv = M_[:, 0:N].rearrange("p (g l) -> p g l", g=nseg)
            CP("pool", HST[:, c, 0:nseg], hv[:, :, L - 1], [rM_], [rHS])
            if full:
                TT("dve", CATR[par][:, c, 0:N], M_[:, 0:N], G_[:, 0:N], ALU.mult, [rM_, rG_], [rCATR[par]])
            elif halo_gate:
                CP("pool", HL[:, c, :], M_[:, N - 2:N], [rM_], [rHL])
        if halo_gate and not full:
            ACT(HG[:, :, :], HG[:, :, :], AF.Gelu_apprx_tanh, [rHG], [rHG])
            TT("dve", CATRH[:, :, :], HL[:, :, :], HG[:, :, :], ALU.mult, [rHL, rHG], [rCATRH])

    rCQN = res("cqn")
    rCQNT = res("cqnt")
    rQNT = res("qnt")
    rQR = res("qr")

    def q_path(Pn, s, cos_ap, sin_ap, tab_res, qi=0):
        if qi == "h":
            QTC, QTR, rQT = QTCH, QTRH, rQTH
        else:
            QTC, QTR, rQT = QTC2[qi], QTR2[qi], rQT2[qi]
        rs, rr = rstd_from(Pn, PS[6][:Pn, 0:NQ], psr(6), NQ, CQN[:Pn, :], rCQN)
        TS("dve", CQN[:Pn, :], PS[6][:Pn, 0:NQ], rs, None, ALU.mult, None, psr(6) + [rr], [rCQN])
        pb = psb(6)
        for k in range(3):
            TR(pb[:, k * 128:k * 128 + Pn], CQN[:Pn, k * 128:(k + 1) * 128], IDB[:Pn, :Pn], [rCQN, rW], psr(6))
        CP("act", CQNT[:, :, 0:Pn], pb[:, 0:384].rearrange("p (k t) -> p k t", k=3)[:, :, 0:Pn], psr(6), [rCQNT])
        for m in range(4):
            for k in range(3):
                MM(PS[6][:, m * 128:m * 128 + Pn], W_UQ[:, k, m * 128:(m + 1) * 128], CQNT[:, k, 0:Pn],
                   k == 0, k == 2, [rCQNT, rW], psr(6))
        CP("dve", QNT[:, :, 0:Pn], PS[6][:, :].rearrange("p (m t) -> p m t", m=4)[:, :, 0:Pn], psr(6), [rQNT])
        for k in range(3):
            MM(PS[7][:Pn, 0:256], CQNT[:, k, 0:Pn], W_UQ[:, k, 512:768], k == 0, k == 2, [rCQNT, rW], psr(7))
        xr = PS[7][:Pn, 0:256].rearrange("p (h a e) -> p h a e", h=8, a=2)
        cb = cos_ap.unsqueeze(1).unsqueeze(1).to_broadcast([Pn, 8, 2, 16])
        sb_ = sin_ap.unsqueeze(1).to_broadcast([Pn, 8, 16])
        t1 = T1[:Pn, :, :].rearrange("p h (a e) -> p h a e", a=2)
        TT("dve", t1, xr, cb, ALU.mult, psr(7) + tab_res, [rT1])
        TT("dve", T2[:Pn, :, 0:16], xr[:, :, 1, :], sb_, ALU.mult, psr(7) + tab_res, [rT2])
        TT("dve", T2[:Pn, :, 16:32], xr[:, :, 0, :], sb_, ALU.mult, psr(7) + tab_res, [rT2])
        TT("dve", QR[:Pn, :, 0:16], T1[:Pn, :, 0:16], T2[:Pn, :, 0:16], ALU.subtract, [rT1, rT2], [rQR])
        TT("dve", QR[:Pn, :, 16:32], T1[:Pn, :, 16:32], T2[:Pn, :, 16:32], ALU.add, [rT1, rT2], [rQR])
        pb7 = psb(7)
        for h in range(8):
            TR(pb7[0:32, h * 128:h * 128 + Pn], QR[:Pn, h, :], IDB[:Pn, :Pn], [rQR, rW], psr(7))
        CP("act", QTR[0:32, :, 0:Pn], pb7[0:32, 0:1024].rearrange("p (h t) -> p h t", h=8)[:, :, 0:Pn], psr(7),
           [rQT])
        bsel = [6, 7]
        bi2 = 0
        for c in range(2):
            for g in range(2):
                b = bsel[bi2 % 2]
                bi2 += 1
                for hh in range(4):
                    h = g * 4 + hh
                    po = (h % 2) * 64
                    MM(PS[b][:, hh * 128:hh * 128 + Pn], W_UKT[:, h, c * 128:(c + 1) * 128],
                       QNT[:, h // 2, 0:Pn], True, True, [rQNT, rW], psr(b))
                CP("dve" if (bi2 % 2) else "act", QTC[:, c, g * 4:(g + 1) * 4, 0:Pn],
                   PS[b][:, :].rearrange("p (h t) -> p h t", h=4)[:, :, 0:Pn], psr(b), [rQT])

    rPT = [res("pt%d" % i) for i in range(NPT)]
    rON = res("on")
    rOT2 = [res("ot0"), res("ot1")]
    ptc = [0]
    scb = [0]
    rcpc = [0]

    def attn_group(qrhs_c, qrhs_r, ncol, keytiles, acc_banks, out_views, rQT, bg=None, bg_step=0, sviews=None,
                   pipelined=True, pt_pool=None, rcp_tile=None):
        nk = len(keytiles)
        sbank = {}

        def scores(ki):
            (kc, kr, va, kp, kvres, bias, isdiag) = keytiles[ki]
            if sviews is None:
                b, co = scb[0] % 2, 0
            else:
                b, co = sviews[scb[0] % len(sviews)]
            scb[0] += 1
            sbank[ki] = (b, co)
            sview = PS[b][:kp, co:co + ncol]
            MM(sview, kc(0), qrhs_c(0), True, False, kvres + [rQT], psr(b))
            MM(sview, kc(1), qrhs_c(1), False, False, kvres + [rQT], psr(b))
            MM(sview, kr, qrhs_r, False, True, kvres + [rQT], psr(b))

        def exp_pv(ki):
            (kc, kr, va, kp, kvres, bias, isdiag) = keytiles[ki]
            b, co = sbank[ki]
            sview = PS[b][:kp, co:co + ncol]
            isap = not isinstance(bias, float)
            rd = psr(b) + ([rC] if isap else [])
            if isdiag:
                pt_t, pt_r = PTD[0], rPTD[0]
                pv = pt_t[:].rearrange("p h t -> p (h t)")
                ACT(pv[0:64, 0:ncol], PS[b][0:64, 0:ncol], AF.Exp, rd, [pt_r], scale=SCALE,
                    bias=(bias[0:64, :] if isap else bias))
                s3 = PS[b][:, 0:512].rearrange("p (h t) -> p h t", h=4)
                ACT(pt_t[64:128, :, 64:128], s3[64:128, :, 64:128], AF.Exp, rd, [pt_r], scale=SCALE,
                    bias=(bias[64:128, :] if isap else bias))
            else:
                if pt_pool is None:
                    d = ptc[0] % NPT
                    ptc[0] += 1
                    pt_t, pt_r = PT[d], rPT[d]
                else:
                    pt_t, pt_r = pt_pool
                pv = pt_t[:]
                ACT(pv[:kp, 0:ncol], sview, AF.Exp, rd, [pt_r], scale=SCALE, bias=bias)
            for ai, (ab, lo, hi) in enumerate(acc_banks):
                MM(PS[ab][:hi - lo, 0:257], pv[:kp, lo:hi], va, ki == 0, ki == nk - 1, [pt_r] + kvres, psr(ab))

        if pipelined:
            scores(0)
        for ki in range(nk):
            if pipelined:
                if ki + 1 < nk:
                    scores(ki + 1)
            else:
                scores(ki)
            exp_pv(ki)
            if bg is not None:
                P.replay_sched(bg, max(16, bg_step))
        for ai, (ab, lo, hi) in enumerate(acc_banks):
            M = hi - lo
            if rcp_tile is None:
                c = rcpc[0] % 8
                rcpc[0] += 1
                rr = res("rcp%d" % c)
                RCP_ = RCP
            else:
                c = 0
                RCP_, rr = rcp_tile[0], rcp_tile[1]
            TS("dve", RCP_[:M, c:c + 1], PS[ab][:M, 256:257], 1e-30, None, ALU.add, None, psr(ab), [rr])
            P.add("dve", (lambda c=c, M=M, RCP_=RCP_: nc.vector.reciprocal(out=RCP_[:M, c:c + 1],
                                                                            in_=RCP_[:M, c:c + 1])), [rr], [rr])
            TS("dve", out_views[ai], PS[ab][:M, 0:256], RCP_[:M, c:c + 1], None, ALU.mult, None, psr(ab) + [rr],
               [rON if rcp_tile is None else rcp_tile[2]])

    def attention_prompt(i, qi, bg=None, drain=True):
        QTC, QTR, rQT, OT, rOTq = QTC2[qi], QTR2[qi], rQT2[qi], OT2[qi], rOT2[qi]
        bg_step = 0
        if bg is not None:
            bg_step = len(bg) // (2 * (i + 1)) + 1
        for g in range(2):
            kts = []
            for kt in range(i + 1):
                kc = (lambda c, kt=kt: KTC[:, c, kt * 128:(kt + 1) * 128])
                bias = CST[:, C_KB:C_KB + 1] if kt < 2 else 0.0
                kts.append((kc, KTR[:, kt * 128:(kt + 1) * 128], V[:, kt, :], 128, [KVR[kt]], bias, kt == i))
            qc = (lambda c, g=g: QTC[:, c, g * 4:(g + 1) * 4, :])
            qr = QTR[:, g * 4:(g + 1) * 4, :]
            accs = [(2 + hh, hh * 128, (hh + 1) * 128) for hh in range(4)]
            outs = [ON[:, hh, :] for hh in range(4)]
            attn_group(qc, qr, 512, kts, accs, outs, rQT, bg, bg_step)
            for c in range(2):
                pb = psb(c)
                for hh in range(4):
                    TR(pb[:, hh * 128:(hh + 1) * 128], ON[:, hh, c * 128:(c + 1) * 128], IDB[:, :], [rON, rW],
                       psr(c))
                CP("act" if c == 0 else "dve", OT[:, c, g * 4:(g + 1) * 4, :],
                   pb[:, 0:512].rearrange("p (h t) -> p h t", h=4), psr(c), [rOTq])
        if bg is not None and drain:
            while bg:
                P.replay_sched(bg, 10)

    rXH = res("xn2t_halo")

    def o_mla_and_out(Pn, OT, rOT, xb, rxb, catm, rcatm, catr, rcatr, xn2_dst, rxn2):
        for h in range(8):
            for c in range(2):
                MM(PS[6][:Pn, h * 64:(h + 1) * 64], OT[:, c, h, 0:Pn], W_UV[:, c, h * 64:(h + 1) * 64], c == 0, c == 1,
                   [rOT, rW], psr(6))
        CP("act", OML[:Pn, :], PS[6][:Pn, 0:512], psr(6), [rOML])
        pb = psb(7)
        for k in range(4):
            TR(pb[:, k * 128:k * 128 + Pn], OML[:Pn, k * 128:(k + 1) * 128], IDB[:Pn, :Pn], [rOML, rW], psr(7))
        CP("dve", catm, pb[:, 0:512].rearrange("p (k t) -> p k t", k=4)[:, :, 0:Pn], psr(7), [rcatm])
        for k in range(8):
            sl = wdc[0] % 2
            wdc[0] += 1
            DMA("sp", WD[sl][:], wout_s[:, k, :], [rWO], [rWDs[sl]], rWDs[sl])
            lhs = catm[:, k, :] if k < 4 else catr[:, k - 4, :]
            for n in range(2):
                MM(PS[6 + n][:Pn, :], lhs, WD[sl][:, n * 512:(n + 1) * 512], k == 0, k == 7,
                   [rcatm, rcatr, rWDs[sl]], psr(6 + n))
        for n in range(2):
            TT("dve", xb[:, n * 512:(n + 1) * 512], PS[6 + n][:Pn, :], xb[:, n * 512:(n + 1) * 512], ALU.add,
               psr(6 + n) + [rxb], [rxb])
        rs, rr = rstd_from(Pn, xb, [rxb], D, XN[:Pn, :], rXN)
        TS("dve", XN[:Pn, :], xb, rs, None, ALU.mult, None, [rxb, rr], [rXN])
        pb = psb(7)
        for k in range(8):
            TR(pb[:, k * 128:k * 128 + Pn], XN[:Pn, k * 128:(k + 1) * 128], IDB[:Pn, :Pn], [rXN, rW], psr(7))
        CP("act", xn2_dst, pb[:, 0:1024].rearrange("p (k t) -> p k t", k=8)[:, :, 0:Pn], psr(7), rxn2)

    rUA = [res("ua0")]
    rUB = [res("ub0")]

    def ffn_prompt(nsub, out_rows, save_state):
        N = 128 * nsub
        NW = N + 2
        Pn = 128
        xr = [rXN2T[s] for s in range(nsub)] + [rXH]
        up_banks = [(0, 1), (6, 7)]
        wus = {}

        def up(j):
            sl = wuc[0] % 2
            wuc[0] += 1
            DMA("sp", WU[sl][:], wup_s[:, :, j * 256:(j + 1) * 256], [rWU], [rWUs[sl]], rWUs[sl])
            (ba, bb) = up_banks[j % 2]
            for k in range(8):
                MM(PS[ba][:, 0:NW], WU[sl][:, k, 0:128], XN2T[:, k, 0:NW], k == 0, k == 7, [rWUs[sl]] + xr, psr(ba))
            for k in range(8):
                MM(PS[bb][:, 0:NW], WU[sl][:, k, 128:256], XN2T[:, k, 0:NW], k == 0, k == 7, [rWUs[sl]] + xr, psr(bb))

        def e1(j):
            (ba, bb) = up_banks[j % 2]
            sl = j % 2
            for (isb, bank, AC, rAC) in [(0, ba, ACA[sl], rACA[sl]), (1, bb, ACB[sl], rACB[sl])]:
                ch = j + 22 * isb
                wc = lambda k: CST[:, C_FCW + ch * 3 + k:C_FCW + ch * 3 + k + 1]
                ACT(AC[:, 0:N], PS[bank][:, 2:NW], AF.Identity, psr(bank) + [rC], [rAC], scale=wc(2),
                    bias=CST[:, C_FCB + ch:C_FCB + ch + 1])
                STT(AC[:, 0:N], PS[bank][:, 1:N + 1], wc(1), AC[:, 0:N], ALU.mult, ALU.add, psr(bank) + [rC, rAC], [rAC])
                STT(AC[:, 0:N], PS[bank][:, 0:N], wc(0), AC[:, 0:N], ALU.mult, ALU.add, psr(bank) + [rC, rAC], [rAC])
                if save_state:
                    CP("act", UH[0][:, ch, 0, :], PS[bank][:, N:NW], psr(bank), [rUH[0]])

        def e2(j):
            sl = j % 2
            ACT(GA[0][:, 0:N], ACA[sl][:, 0:N], AF.Gelu_apprx_tanh, [rACA[sl]], [rGA[0]])
            TT("pool", HT[sl][:, 0:N], GA[0][:, 0:N], ACB[sl][:, 0:N], ALU.mult, [rGA[0], rACB[sl]], [rHT[sl]])

        def down(j):
            sl = wdc[0] % 2
            wdc[0] += 1
            hs = j % 2
            DMA("sp", WD[sl][:], wdn_s[:, j, :], [rWD], [rWDs[sl]], rWDs[sl])
            for s in range(nsub):
                for n in range(2):
                    ab = 2 + s * 2 + n
                    MM(PS[ab][:Pn, :], HT[hs][:, s * Pn:(s + 1) * Pn], WD[sl][:, n * 512:(n + 1) * 512],
                       j == 0, j == NPAIR - 1, [rHT[hs], rWDs[sl]], psr(ab))

        up(0)
        up(1)
        e1(0)
        for j in range(NPAIR):
            if j + 2 < NPAIR:
                up(j + 2)
            if j + 1 < NPAIR:
                e1(j + 1)
            e2(j)
            down(j)
        for s in range(nsub):
            for n in range(2):
                ab = 2 + s * 2 + n
                TT("dve", XB[:Pn, s, n * 512:(n + 1) * 512], PS[ab][:Pn, :], XB[:Pn, s, n * 512:(n + 1) * 512],
                   ALU.add, psr(ab) + [rXB[s]], [rXB[s]])
            rs, rr = rstd_from(Pn, XB[:Pn, s, :], [rXB[s]], D, XN[:Pn, :], rXN)
            STT(XB[:Pn, s, :], XB[:Pn, s, :], rs, GFIN[:Pn, :], ALU.mult, ALU.mult, [rXB[s], rr, rC], [rXB[s]])
            DMA("sp", out_rows(s), XB[:Pn, s, :], [rXB[s]], (), rXB[s])

    def ffn(N, nseg, L, nsub, Pn, uh_prev, uh_next, out_rows, only_halo=False):
        Wd = 2 + L
        xr = [rXN2T[s] for s in range(nsub)]
        up_banks = [(0, 1), (6, 7)]
        wus = {}

        def up(j):
            sl = wuc[0] % 2
            wuc[0] += 1
            wus[j] = sl
            DMA("sp", WU[sl][:], wup_s[:, :, j * 256:(j + 1) * 256], [rWU], [rWUs[sl]], rWUs[sl])
            (ba, bb) = up_banks[j % 2]
            for k in range(8):
                MM(PS[ba][:, 0:N], WU[sl][:, k, 0:128], XN2T[:, k, 0:N], k == 0, k == 7, [rWUs[sl]] + xr, psr(ba, 0, N))
            for k in range(8):
                MM(PS[bb][:, 0:N], WU[sl][:, k, 128:256], XN2T[:, k, 0:N], k == 0, k == 7, [rWUs[sl]] + xr,
                   psr(bb, 0, N))

        def elem(j):
            (ba, bb) = up_banks[j % 2]
            for (isb, bank, U_, rU, AC, rAC) in [(0, ba, UA[0], rUA[0], ACA[0], rACA[0]),
                                                 (1, bb, UBf[0], rUB[0], ACB[0], rACB[0])]:
                ch = j + 22 * isb
                uv = U_[:, 0:nseg * Wd].rearrange("p (g w) -> p g w", g=nseg)
                CP("act", uv[:, :, 2:2 + L], PS[bank][:, 0:N].rearrange("p (g l) -> p g l", g=nseg),
                   psr(bank, 0, N), [rU])
                CP("pool", uv[:, :, 0:2], UH[uh_prev][:, ch, 0:nseg, :], [rUH[uh_prev]], [rU])
                CP("pool", UH[uh_next][:, ch, 0:nseg, :], uv[:, :, L:L + 2], [rU], [rUH[uh_next]])
                if only_halo:
                    continue
                av = AC[:, 0:N].rearrange("p (g l) -> p g l", g=nseg)
                wc = lambda k: CST[:, C_FCW + ch * 3 + k:C_FCW + ch * 3 + k + 1]
                TS("dve", av, uv[:, :, 2:2 + L], wc(2), CST[:, C_FCB + ch:C_FCB + ch + 1], ALU.mult, ALU.add,
                   [rU, rC], [rAC])
                STT(av, uv[:, :, 1:1 + L], wc(1), av, ALU.mult, ALU.add, [rU, rC, rAC], [rAC])
                STT(av, uv[:, :, 0:L], wc(0), av, ALU.mult, ALU.add, [rU, rC, rAC], [rAC])
            if only_halo:
                return
            ACT(GA[0][:, 0:N], ACA[0][:, 0:N], AF.Gelu_apprx_tanh, [rACA[0]], [rGA[0]])
            hs = j % 2
            TT("pool", HT[hs][:, 0:N], GA[0][:, 0:N], ACB[0][:, 0:N], ALU.mult, [rGA[0], rACB[0]], [rHT[hs]])

        def down(j):
            sl = wdc[0] % 2
            wdc[0] += 1
            hs = j % 2
            DMA("sp", WD[sl][:], wdn_s[:, j, :], [rWD], [rWDs[sl]], rWDs[sl])
            for s in range(nsub):
                for n in range(2):
                    ab = 2 + s * 2 + n
                    MM(PS[ab][:Pn, :], HT[hs][:, s * Pn:(s + 1) * Pn], WD[sl][:, n * 512:(n + 1) * 512],
                       j == 0, j == NPAIR - 1, [rHT[hs], rWDs[sl]], psr(ab))

        if only_halo:
            for j in range(NPAIR):
                up(j)
                elem(j)
            return
        up(0)
        for j in range(NPAIR):
            if j + 1 < NPAIR:
                up(j + 1)
            elem(j)
            down(j)
        for s in range(nsub):
            for n in range(2):
                ab = 2 + s * 2 + n
                TT("dve", XB[:Pn, s, n * 512:(n + 1) * 512], PS[ab][:Pn, :], XB[:Pn, s, n * 512:(n + 1) * 512],
                   ALU.add, psr(ab) + [rXB[s]], [rXB[s]])
            rs, rr = rstd_from(Pn, XB[:Pn, s, :], [rXB[s]], D, XN[:Pn, :], rXN)
            STT(XB[:Pn, s, :], XB[:Pn, s, :], rs, GFIN[:Pn, :], ALU.mult, ALU.mult, [rXB[s], rr, rC], [rXB[s]])
            DMA("sp", out_rows(s), XB[:Pn, s, :], [rXB[s]], (), rXB[s])

    try:
        _ck('setup')
        def xrows(i):
            return xp[i * 128:(i + 1) * 128, :]

        def kvdst(i):
            return (KTC[:, :, i * 128:(i + 1) * 128], KTR[0:32, i * 128:(i + 1) * 128], V[:, i, :], [KVR[i]])

        def tab(s, Pn=128):
            return (TAB[:Pn, s, 0:16], TAB[:Pn, s, 16:32], [rTAB[s]])

        def pre_stage(tiles):
            for s, f in enumerate(tiles):
                stage_a(xrows(f), 128, s, tabd[f], use_tmp=True)

        def pre_rg(par, other=False):
            if other:
                rg_branch(256, 1, 256, False, [0, 1], par=par, halo_gate=True)
            else:
                rg_branch(256, 1, 256, True, [0, 1], par=par)

        def pre_pair(tiles, par, other=False):
            for s, f in enumerate(tiles):
                stage_a(xrows(f), 128, s, tabd[f], use_tmp=True)
            if other:
                rg_branch(256, 1, 256, False, [0, 1], par=par, halo_gate=True)
            else:
                rg_branch(256, 1, 256, True, [0, 1], par=par)

        def kside_pair(tiles):
            for s, f in enumerate(tiles):
                c_, s_, tr_ = tab(s)
                kv_side(128, s, c_, s_, tr_, False, kvdst(f))

        def front(f, s, qi, oi):
            c_, s_, tr_ = tab(s)
            kv_side(128, s, c_, s_, tr_, True, kvdst(f),
                    pckv_d[oi * 128:(oi + 1) * 128, :], pkr_d[oi * 128:(oi + 1) * 128, :])
            q_path(128, s, c_, s_, tr_, qi)

        def xload(tiles):
            for s, f in enumerate(tiles):
                DMA("sp", XB[:, s, :], xrows(f), (), [rXB[s]], rXB[s])

        def back(s, qi, par):
            tok = slice(s * 128, (s + 1) * 128)
            o_mla_and_out(128, OT2[qi], rOT2[qi], XB[:, s, :], rXB[s], CATM[:, :, tok], rCAT[s],
                          CATR[par][:, :, tok], rCATR[par], XN2T[:, :, 2 + s * 128:2 + (s + 1) * 128], [rXN2T[s]])

        rPTH = res("pth")
        rRCPH = res("rcph")
        rONH = res("onh")
        rOTH = res("oth")
        rCATMH = res("catmh")
        rXH2 = res("xh2")

        def halo(f):
            stage_a(xp[f * 128 + 126:f * 128 + 128, :], 2, 0, tabd[f][126:128, :], use_tmp=True)
            c_, s_, tr_ = tab(0, 2)
            kv_side(2, 0, c_, s_, tr_, True, None)
            q_path(2, 0, c_, s_, tr_, "h")
            groups = [[0, 1]] + [list(range(a, min(a + 4, f + 1))) for a in range(2, f + 1, 4)]
            for gi, grp_ in enumerate(groups):
                for j, kt in enumerate(grp_):
                    sv = PS[6][:, 16 * j:16 * j + 16]
                    MM(sv, KTC[:, 0, kt * 128:(kt + 1) * 128], QTCH[:, 0, :, :], True, False, [KVR[kt], rQTH], psr(6))
                    MM(sv, KTC[:, 1, kt * 128:(kt + 1) * 128], QTCH[:, 1, :, :], False, False, [KVR[kt], rQTH], psr(6))
                    MM(sv, KTR[:, kt * 128:(kt + 1) * 128], QTRH[:, :, :], False, True, [KVR[kt], rQTH], psr(6))
                ncol = 16 * len(grp_)
                if gi == 0:
                    ACT(PTH[:, 0:ncol], PS[6][:, 0:ncol], AF.Exp, psr(6) + [rC], [rPTH], scale=SCALE,
                        bias=CST[:, C_KB:C_KB + 1])
                else:
                    ACT(PTH[:, 0:ncol], PS[6][:, 0:ncol], AF.Exp, psr(6), [rPTH], scale=SCALE, bias=0.0)
                for j, kt in enumerate(grp_):
                    MM(PS[7][:16, 0:257], PTH[:, 16 * j:16 * j + 16], V[:, kt, :], kt == 0, kt == f,
                       [rPTH, KVR[kt]], psr(7))
            TS("dve", RCPH[:16, 0:1], PS[7][:16, 256:257], 1e-30, None, ALU.add, None, psr(7), [rRCPH])
            P.add("dve", (lambda: nc.vector.reciprocal(out=RCPH[:16, 0:1], in_=RCPH[:16, 0:1])), [rRCPH], [rRCPH])
            TS("dve", ONH[:16, :], PS[7][:16, 0:256], RCPH[:16, 0:1], None, ALU.mult, None, psr(7) + [rRCPH], [rONH])
            pb = psb(6)
            for c in range(2):
                TR(pb[:, c * 16:(c + 1) * 16], ONH[:16, c * 128:(c + 1) * 128], IDB[:16, :16], [rONH, rW], psr(6))
            CP("act", OTH[:, :, :, :], pb[:, 0:32].rearrange("p (c h q) -> p c h q", c=2, h=8), psr(6), [rOTH])
            o_mla_and_out(2, OTH, rOTH, XTMP[:2, :], rXTMP, CATMH[:, :, :], rCATMH, CATRH[:, :, :], rCATRH,
                          XH2[:, :, :], [rXH2])

        def catrh_from(par):
            CP("pool", CATRH[:, :, :], CATR[par][:, :, 254:256], [rCATR[par]], [rCATRH])

        pre_pair((0, 1), 1, other=True)
        kside_pair((0, 1))
        TS("pool", HST[:, :, 0:1], HST[:, :, 0:1], CST[:, C_FLAG:C_FLAG + 1], None, ALU.mult, None,
           rHSTc + [rC], rHSTc)
        halo(1)
        TS("pool", XN2T[:, :, 0:2], XH2[:, :, :], CST[:, C_FLAG:C_FLAG + 1], None, ALU.mult, None,
           [rXH2, rC], [rXH])
        pre_pair((2, 3), 0)
        xload((2, 3))
        front(2, 0, 0, 0)
        _ck('prefix')
        for k in range(16):
            f0, f1, g0, g1 = 2 + 4 * k, 3 + 4 * k, 4 + 4 * k, 5 + 4 * k
            par, parn = k % 2, (k + 1) % 2
            last = (k == 15)

            def l0a():
                front(f1, 1, 1, 2 * k + 1)

            def l0b():
                pre_pair((g0, g1), parn, other=True)
                if not last:
                    kside_pair((g0, g1))
            la = P.record(l0a)
            ida = set(id(t) for t in la)
            lst = la + P.record(l0b)
            attention_prompt(f0, 0, lst, drain=False)
            while any(id(t) in ida for t in lst):
                P.replay_sched(lst, 10)

            def l1():
                back(0, 0, par)
                if not last:
                    halo(g1)
                    pre_stage((f0 + 4, f1 + 4))
                    front(f0 + 4, 0, 0, 2 * k + 2)
                    pre_rg(parn)
            lst = lst + P.record(l1)
            attention_prompt(f1, 1, lst, drain=True)
            back(1, 1, par)
            base = 2 * k * 128
            ffn_prompt(2, lambda s: y_d[base + s * 128:base + (s + 1) * 128, :], last)
            if not last:
                CP("pool", XN2T[:, :, 0:2], XH2[:, :, :], [rXH2], [rXH])
                xload((f0 + 4, f1 + 4))
            _ck('own1')
        _ck('own')

        DMA("sp", prh_d, HST[:, :, 0], rHSTc, (), res("o_prh"), slow=True)
        DMA("sp", prc_d, RX[:, :, 0:3], rRXc, (), res("o_prc"), slow=True)
        DMA("sp", pfc_d, UH[0][:, :, 0, :], [rUH[0]], (), res("o_pfc"), slow=True)
        uh_cur = 0

        DMA("sp", HST[:, :, :], srh_d, (), rHSTc, res("l_srh"), slow=True)
        rxs = RX[:, :, 0:4 * 19].rearrange("p c (g w) -> p c g w", g=4)
        for c in range(4):
            DMA("sp", rxs[:, c, :, 0:3], src_d[:, c, :, :], (), [rRXc[c]], res("l_src%d" % c), slow=True)
        DMA("sp", UH[uh_cur][:, :, :, :], sfc_d, (), [rUH[uh_cur]], res("l_sfc"), slow=True)
        stg = [(XTMP[:, :].rearrange("p (t d) -> p t d", t=4), rXTMP),
               (XB[:, 0, :].rearrange("p (t d) -> p t d", t=4), rXB[0]),
               (XB[:, 1, :].rearrange("p (t d) -> p t d", t=4), rXB[1])]
        krs = [(T1[:, 0:4, :], rT1), (T2[:, 0:4, :], rT2)]
        rKRB4 = [res("krb4_0"), res("krb4_1")]
        cbanks = [4, 5, 6, 7]
        for g4 in range(NTILE // 4):
            (sv, sr) = stg[g4 % 3]
            (kv_, kr_) = krs[g4 % 2]
            kl = g4 % 2
            DMA("sp", sv, cckv_d[4 * g4:4 * g4 + 4].rearrange("t p d -> p t d"), (), [sr], sr)
            DMA("sp", kv_, ckr_d[4 * g4:4 * g4 + 4].rearrange("t p e -> p t e"), (), [kr_], kr_)
            CP("pool", KRB4[:, kl, :, :], kv_, [kr_], [rKRB4[kl]])
            for j in range(4):
                kt = 4 * g4 + j
                bnk = cbanks[kt % 4]
                CP("pool" if kt % 2 == 0 else "dve", V[:, kt, 0:NKV], sv[:, j, :], [sr], [KVR[kt]])
                pb = psb(bnk)
                for c in range(2):
                    TR(pb[:, c * 128:(c + 1) * 128], V[:, kt, c * 128:(c + 1) * 128], IDB[:, :], [KVR[kt], rW],
                       psr(bnk))
                TR(pb[0:32, 256:384], KRB4[:, kl, j, :], IDB[:, :], [rKRB4[kl], rW], psr(bnk))
                CP("act" if kt % 2 == 0 else "dve", KTC[:, :, kt * 128:(kt + 1) * 128],
                   pb[:, 0:256].rearrange("p (c t) -> p c t", c=2), psr(bnk), [KVR[kt]])
                CP("act", KTR[0:32, kt * 128:(kt + 1) * 128], pb[0:32, 256:384], psr(bnk), [KVR[kt]])

        stage_a(xs_d, 64, 0, None)
        rg_branch(64, 4, 16, True, [0], par=0)
        kv_side(64, 0, COSS[:, :], SINS[:, :], [rC], True, (KTCN[:, :, :], KTRN[0:32, :], VN[:, :], [rVN]),
                sckv_d, skr_d)
        q_path(64, 0, COSS[:, :], SINS[:, :], [rC])
        for sq in range(4):
            kts = []
            for j in range(16):
                kt = sq * 16 + j
                kc = (lambda c, kt=kt: KTC[:, c, kt * 128:(kt + 1) * 128])
                kts.append((kc, KTR[:, kt * 128:(kt + 1) * 128], V[:, kt, :], 128, [KVR[kt]], 0.0, False))
            kts.append(((lambda c: KTCN[:, c, :]), KTRN[:, :], VN[:, :], 64, [rVN],
                        CST[0:64, C_SMASK + sq:C_SMASK + sq + 1], False))
            qc = (lambda c, sq=sq: QTC2[0][:, c, :, sq * 16:(sq + 1) * 16])
            qr = QTR2[0][:, :, sq * 16:(sq + 1) * 16]
            attn_group(qc, qr, 128, kts, [(2 + sq, 0, 128)], [ON[:, sq, :]], rQT2[0])
        for sq in range(4):
            pb = psb(6)
            for c in range(2):
                TR(pb[:, c * 128:(c + 1) * 128], ON[:, sq, c * 128:(c + 1) * 128], IDB[:, :], [rON, rW], psr(6))
            CP("act", OT2[0][:, :, :, sq * 16:(sq + 1) * 16],
               pb[:, 0:256].rearrange("p (c h q) -> p c h q", c=2, h=8), psr(6), [rOT2[0]])
        o_mla_and_out(64, OT2[0], rOT2[0], XB[:64, 0, :], rXB[0], CATM[:, :, 0:64], rCAT[0],
                      CATR[0][:, :, 0:64], rCATR[0], XN2T[:, :, 0:64], [rXN2T[0], rXH])
        ffn(64, 4, 16, 1, 64, uh_cur, 1 - uh_cur, lambda s: ys_d)
        uh_cur = 1 - uh_cur
        DMA("sp", srh_o, HST[:, :, :], rHSTc, (), res("o_srh"), slow=True)
        for c in range(4):
            DMA("sp", src_o[:, c, :, :], rxs[:, c, :, 0:3], [rRXc[c]], (), res("o_src%d" % c), slow=True)
        DMA("sp", sfc_o, UH[uh_cur][:, :, :, :], [rUH[uh_cur]], (), res("o_sfc"), slow=True)
    except _Stop:
        pass

    P.emit()
    return nc, es, P


_CACHE = {}


def _prep_inputs(inp):
    f32 = np.float32
    g = lambda k: np.asarray(inp[k], dtype=f32)
    x_prompt = g("x_prompt")
    x_sample = g("x_sample")
    cache_ckv = g("cache_ckv")[0]
    cache_krope = g("cache_krope")[0]
    st_h = g("state_rg_h")[0]
    st_c = g("state_rg_conv")[0]
    st_f = g("state_ffn_conv")[0]
    w_in = g("w_in")[0]
    w_uq = g("w_uq")[0]
    w_uk = g("w_uk")[0]
    w_uv = g("w_uv")[0]
    w_out = g("w_out")[0]
    w_up = g("w_ffn_up")[0]
    w_dn = g("w_ffn_down")[0]

    def pm(a, k):
        return np.ascontiguousarray(a.reshape(k, 128, -1).transpose(1, 0, 2))

    def fm(v, k):
        return np.ascontiguousarray(v.reshape(k, 128).T)

    shared = {}
    shared["w_in"] = pm(w_in, 8)
    uq = np.concatenate([w_uq[:, :, :64].reshape(NQ, 512), w_uq[:, :, 64:].reshape(NQ, 256)], axis=1)
    shared["w_uq"] = pm(uq, 3)
    shared["w_out"] = pm(w_out, 8)
    ukt = np.zeros((128, 8, 256), f32)
    for h in range(8):
        ukt[(h % 2) * 64:(h % 2) * 64 + 64, h, :] = w_uk[:, h, :].T
    shared["w_ukt"] = ukt
    shared["w_uv"] = pm(w_uv.reshape(256, 512), 2)
    for nm, key in [("w_rga", "w_rg_a"), ("w_rgi", "w_rg_i")]:
        w = g(key)[0]
        bd = np.zeros((128, 4, 128), f32)
        for c in range(4):
            for b in range(2):
                bd[b * 64:(b + 1) * 64, c, b * 64:(b + 1) * 64] = w[2 * c + b]
        shared[nm] = bd
    upi = np.concatenate([w_up[:, :FF].reshape(D, NPAIR, 128), w_up[:, FF:].reshape(D, NPAIR, 128)], axis=2)
    shared["w_up"] = pm(upi.reshape(D, 2 * FF), 8)
    shared["w_dn"] = pm(w_dn, NPAIR)
    shared["gfin"] = np.ascontiguousarray(np.broadcast_to(g("final_norm_g")[None, :], (128, D)))
    shared["gkv"] = np.ascontiguousarray(np.broadcast_to(g("kv_norm_g")[0][None, :], (128, NKV)))
    shared["ident"] = np.eye(128, dtype=f32)

    cst = np.zeros((128, C_NCOL), f32)
    rgw = g("w_rg_conv")[0]
    cst[:, C_RGW:C_RGW + 16] = rgw.reshape(4, 4, 128).transpose(2, 1, 0).reshape(128, 16)
    cst[:, C_RGB:C_RGB + 4] = fm(g("b_rg_conv")[0], 4)
    cst[:, C_BA:C_BA + 4] = fm(g("b_rg_a")[0], 4)
    cst[:, C_BI:C_BI + 4] = fm(g("b_rg_i")[0], 4)
    cst[:, C_LAM:C_LAM + 4] = fm(g("rg_lambda")[0], 4)
    fcw = g("w_ffn_conv")[0]
    cst[:, C_FCW:C_FCW + 132] = fcw.reshape(3, 44, 128).transpose(2, 1, 0).reshape(128, 132)
    cst[:, C_FCB:C_FCB + 44] = fm(g("b_ffn_conv")[0], 44)
    cst[:, C_GMIX:C_GMIX + 8] = fm(g("norm_mix_g")[0], 8)
    cst[:, C_GFFN:C_GFFN + 8] = fm(g("norm_ffn_g")[0], 8)
    cst[:, C_GQ:C_GQ + 3] = fm(g("q_norm_g")[0], 3)
    for sq in range(4):
        cst[:, C_SMASK + sq] = NEGB
        cst[sq * 16:(sq + 1) * 16, C_SMASK + sq] = 0.0

    half_ = NRO // 2
    inv = (10000.0 ** (-np.arange(half_, dtype=f32) / half_)).astype(f32)

    def tables(pos):
        ang = pos.astype(f32)[:, None] * inv[None, :]
        return np.cos(ang).astype(f32), np.sin(ang).astype(f32)

    pos_s = (2048 + (np.arange(64) % 16)).astype(np.int64)
    cs_s, sn_s = tables(pos_s)

    in_maps = []
    for core in range(8):
        b, half = core // 2, core % 2
        m = dict(shared)
        zpad = np.zeros((256, D), f32)
        if half == 1:
            m["xp"] = np.concatenate([x_prompt[b], zpad], axis=0)
            pos = np.arange(NF * 128, dtype=np.int64)
        else:
            m["xp"] = np.concatenate([zpad, x_prompt[b]], axis=0)
            pos = np.arange(NF * 128, dtype=np.int64) - 256
        cs, sn = tables(np.abs(pos))
        sn = sn * np.sign(pos)[:, None].astype(f32)
        m["tabd"] = np.ascontiguousarray(np.concatenate([cs, sn], axis=1).reshape(NF, 128, 32))
        m["cossd"] = cs_s
        m["sinsd"] = sn_s
        c2 = cst.copy()
        c2[:, C_KB] = 0.0 if half == 1 else NEGB
        c2[:, C_FLAG] = float(half)
        m["cst"] = c2
        sl = slice(4 * core, 4 * core + 4)
        m["xs"] = np.ascontiguousarray(x_sample[sl].reshape(64, D))
        m["cckv"] = np.ascontiguousarray(cache_ckv[sl].reshape(NTILE, 128, NKV))
        m["ckr"] = np.ascontiguousarray(cache_krope[sl].reshape(NTILE, 128, NRO))
        m["srh"] = np.ascontiguousarray(st_h[sl].reshape(4, 4, 128).transpose(2, 1, 0))
        m["src"] = np.ascontiguousarray(st_c[sl].reshape(4, 3, 4, 128).transpose(3, 2, 0, 1))
        m["sfc"] = np.ascontiguousarray(st_f[sl].reshape(4, 2, 44, 128).transpose(3, 2, 0, 1))
        in_maps.append(m)
    return in_maps


def kernel(**inputs):
    if "prog" not in _CACHE:
        _CACHE["prog"] = build_program()
    nc, es, P = _CACHE["prog"]
    in_maps = _prep_inputs(inputs)
    res = run_bass_kernel_spmd(nc, in_maps, core_ids=list(range(8)))
    r = res.results
    f32 = np.float32
    B = 4
    y_prompt = np.zeros((B, SEQ, D), f32)
    p_ckv = np.zeros((1, B, SEQ, NKV), f32)
    p_krope = np.zeros((1, B, SEQ, NRO), f32)
    p_rg_h = np.zeros((1, B, RGW), f32)
    p_rg_conv = np.zeros((1, B, 3, RGW), f32)
    p_ffn_conv = np.zeros((1, B, 2, 2 * FF), f32)
    y_sample = np.zeros((32, 16, D), f32)
    s_ckv = np.zeros((1, 32, 16, NKV), f32)
    s_krope = np.zeros((1, 32, 16, NRO), f32)
    s_rg_h = np.zeros((1, 32, RGW), f32)
    s_rg_conv = np.zeros((1, 32, 3, RGW), f32)
    s_ffn_conv = np.zeros((1, 32, 2, 2 * FF), f32)
    for core in range(8):
        b, half = core // 2, core % 2
        o = r[core]
        for oi in range(32):
            st = 4 * (oi // 2) + 2 * half + (oi % 2)
            ts = slice(st * 128, (st + 1) * 128)
            os_ = slice(oi * 128, (oi + 1) * 128)
            y_prompt[b, ts] = o["y"][os_]
            p_ckv[0, b, ts] = o["pckv"][os_]
            p_krope[0, b, ts] = o["pkr"][os_]
        if half == 0:
            p_rg_h[0, b] = o["prh"].T.reshape(RGW)
            p_rg_conv[0, b] = o["prc"].transpose(2, 1, 0).reshape(3, RGW)
        else:
            p_ffn_conv[0, b] = o["pfc"].transpose(2, 1, 0).reshape(2, 2 * FF)
        sl = slice(4 * core, 4 * core + 4)
        y_sample[sl] = o["ys"].reshape(4, 16, D)
        s_ckv[0, sl] = o["sckv"].reshape(4, 16, NKV)
        s_krope[0, sl] = o["skr"].reshape(4, 16, NRO)
        s_rg_h[0, sl] = o["srho"].transpose(2, 1, 0).reshape(4, RGW)
        s_rg_conv[0, sl] = o["srco"].transpose(2, 3, 1, 0).reshape(4, 3, RGW)
        s_ffn_conv[0, sl] = o["sfco"].transpose(2, 3, 1, 0).reshape(4, 2, 2 * FF)
    return (y_prompt, y_sample, p_ckv, p_krope, p_rg_h, p_rg_conv, p_ffn_conv,
            s_ckv, s_krope, s_rg_h, s_rg_conv, s_ffn_conv)
```

```python
import numpy as np
from contextlib import ExitStack
import concourse.bass as bass
import concourse.mybir as mybir
from concourse.bass_utils import run_bass_kernel_spmd

F32 = mybir.dt.float32
BF16 = mybir.dt.bfloat16
AF = mybir.ActivationFunctionType
ALU = mybir.AluOpType

D = 1024
NQ = 384
NKV = 256
NRO = 32
RGW = 512
FF = 2816
NH = 8
INC = 1696
SEQ = 8192
HALF = 4096
NTILE = 64
NF = 66
EPS = 1e-6
SCALE = 96.0 ** -0.5
NEGB = -30000.0
NPAIR = 22

C_RGW = 0
C_RGB = 16
C_BA = 20
C_BI = 24
C_LAM = 28
C_FCW = 32
C_FCB = 164
C_GMIX = 208
C_GFFN = 216
C_GQ = 224
C_KB = 227
C_FLAG = 228
C_SMASK = 229
C_NCOL = 240


class Res:
    __slots__ = ("name", "w", "r", "sem", "cnt", "excl", "multi")

    def __init__(self, name, excl=False, multi=False):
        self.name = name
        self.excl = excl
        self.multi = multi
        self.w = []
        self.r = {}
        self.sem = None
        self.cnt = 0


class Ev:
    __slots__ = ("kind", "eng", "idx", "res", "val", "grp")

    def __init__(self, kind, eng, idx, res=None, val=None, grp=None):
        self.kind = kind
        self.eng = eng
        self.idx = idx
        self.res = res
        self.val = val
        self.grp = grp


class Prog:
    ENG = ["pe", "act", "dve", "pool", "sp"]

    def __init__(self, nc, es):
        self.nc = nc
        self.es = es
        self.ops = []
        self.anchors = []
        self.rec = None
        self.engobj = dict(pe=nc.tensor, act=nc.scalar, dve=nc.vector, pool=nc.gpsimd, sp=nc.sync)

    def record(self, f):
        self.rec = []
        f()
        r = self.rec
        self.rec = None
        return r

    def replay(self, lst, n):
        while n > 0 and lst:
            self.add(*lst.pop(0))
            n -= 1

    def replay_step(self, lst, max_ops=12):
        tw = {}
        tr = {}
        n = 0
        while lst and n < max_ops:
            (eng, fn, reads, writes, dma, grp) = lst[0]
            conflict = False
            for w in writes:
                for d in (tw, tr):
                    e = d.get(id(w))
                    if e and (e - {eng}):
                        conflict = True
            for r in reads:
                e = tw.get(id(r))
                if e and (e - {eng}):
                    conflict = True
                if r.excl:
                    e = tr.get(id(r))
                    if e and (e - {eng}):
                        conflict = True
            if conflict and n > 0:
                break
            self.add(*lst.pop(0))
            n += 1
            for w in writes:
                tw.setdefault(id(w), set()).add(eng)
            for r in reads:
                tr.setdefault(id(r), set()).add(eng)

    def replay_sched(self, lst, max_ops=10, window=400):
        tw = {}
        tr = {}
        pw = set()
        pr = set()
        n = 0
        i = 0
        scanned = 0
        while i < len(lst) and n < max_ops and scanned < window:
            (eng, fn, reads, writes, dma, grp) = lst[i]
            scanned += 1
            wid = [id(w) for w in writes] + [id(r) for r in reads if r.excl]
            rid = [id(r) for r in reads if not r.excl]
            ready = True
            for x in wid:
                if x in pw or x in pr:
                    ready = False
                    break
            if ready:
                for x in rid:
                    if x in pw:
                        ready = False
                        break
            if ready:
                for x in wid:
                    e = tw.get(x)
                    if e and (e - {eng}):
                        ready = False
                    e = tr.get(x)
                    if e and (e - {eng}):
                        ready = False
                for x in rid:
                    e = tw.get(x)
                    if e and (e - {eng}):
                        ready = False
            if ready:
                self.add(*lst.pop(i))
                n += 1
                for x in wid:
                    tw.setdefault(x, set()).add(eng)
                for x in rid:
                    tr.setdefault(x, set()).add(eng)
            else:
                pw.update(wid)
                pr.update(rid)
                i += 1

    def add(self, eng, fn, reads=(), writes=(), dma=None, grp=None):
        if self.rec is not None:
            self.rec.append((eng, fn, reads, writes, dma, grp))
            return None
        deps = []
        ex = [r for r in reads if r.excl and r not in writes]
        if ex:
            reads = [r for r in reads if not r.excl]
            writes = list(writes) + ex
        for r in reads:
            deps += r.w
        for w in writes:
            ds = (list(w.r.values()) if w.multi else w.w + list(w.r.values()))
            if w.excl:
                ds = [d for d in ds if d.eng != eng]
            deps += ds
        idx = len(self.ops)
        if dma is not None:
            if dma.sem is None:
                dma.sem = self.es.enter_context(self.nc.semaphore("d_" + dma.name))
                self.anchors.append(dma)
            dma.cnt += 16
            ev = Ev("d", eng, idx, dma, dma.cnt, grp)
            if grp is not None:
                grp.append(ev)
        else:
            ev = Ev("e", eng, idx)
        dd = []
        for d in deps:
            if d.kind == "e" and d.eng == "pe" and eng == "pe":
                continue
            if grp is not None and d.grp is grp:
                continue
            dd.append(d)
        self.ops.append((eng, fn, dd, ev))
        for w in writes:
            if w.multi:
                w.w = w.w + [ev]
            else:
                w.w = [ev]
                w.r = {}
        for r in reads:
            if r in writes:
                continue
            key = eng if ev.kind == "e" else ("d", id(dma))
            r.r[key] = ev
        return ev

    def close_group(self, grp):
        if not grp:
            return
        tot = grp[0].res.cnt
        for ev in grp:
            ev.val = tot

    def emit(self):
        nc = self.nc
        esem = {e: self.es.enter_context(nc.semaphore("e_" + e)) for e in self.ENG}
        need = set()
        for (eng, fn, dd, ev) in self.ops:
            for d in dd:
                if d.kind == "e":
                    need.add(d.idx)
        ms = {}
        cnt = {e: 0 for e in self.ENG}
        for i, (eng, fn, dd, ev) in enumerate(self.ops):
            if i in need:
                cnt[eng] += 1
                ms[i] = cnt[eng]
        seen = {e: {} for e in self.ENG}
        for i, (eng, fn, dd, ev) in enumerate(self.ops):
            E = self.engobj[eng]
            waits = {}
            for d in dd:
                if d.kind == "e":
                    key = d.eng
                    val = ms[d.idx]
                    sem = esem[d.eng]
                else:
                    key = ("d", id(d.res))
                    val = d.val
                    sem = d.res.sem
                if seen[eng].get(key, 0) >= val:
                    continue
                if key not in waits or waits[key][1] < val:
                    waits[key] = (sem, val)
            for key, (sem, val) in waits.items():
                seen[eng][key] = val
                E.wait_ge(sem, val)
            ins = fn()
            if i in need:
                ins.then_inc(esem[eng], 1)
            if ev.kind == "d":
                ins.then_inc(ev.res.sem, 16)
        for a in self.anchors:
            nc.sync.wait_ge(a.sem, a.cnt)
        self.stats = dict(cnt)
        self.stats["nops"] = len(self.ops)


class _Stop(Exception):
    pass


_STOP = [None]


def _ck(name):
    if _STOP[0] == name:
        raise _Stop()


def build_program():
    nc = bass.Bass("TRN2", target_bir_lowering=False)
    es = ExitStack()
    P = Prog(nc, es)

    def din(name, shape, dt=F32):
        return nc.dram_tensor(name, list(shape), dt, kind="ExternalInput").ap()

    def dout(name, shape, dt=F32):
        return nc.dram_tensor(name, list(shape), dt, kind="ExternalOutput").ap()

    xp = din("xp", [NF * 128, D])
    tabd = din("tabd", [NF, 128, 32])
    cossd = din("cossd", [64, 16])
    sinsd = din("sinsd", [64, 16])
    cst = din("cst", [128, C_NCOL])
    gfin_d = din("gfin", [128, D])
    gkv_d = din("gkv", [128, NKV])
    ident_d = din("ident", [128, 128])
    w_in_d = din("w_in", [128, 8, INC])
    w_uq_d = din("w_uq", [128, 3, 768])
    w_out_d = din("w_out", [128, 8, D])
    w_ukt_d = din("w_ukt", [128, 8, 256])
    w_uv_d = din("w_uv", [128, 2, 512])
    w_rga_d = din("w_rga", [128, 4, 128])
    w_rgi_d = din("w_rgi", [128, 4, 128])
    w_up_d = din("w_up", [128, 8, 2 * FF])
    w_dn_d = din("w_dn", [128, NPAIR, D])
    xs_d = din("xs", [64, D])
    cckv_d = din("cckv", [NTILE, 128, NKV])
    ckr_d = din("ckr", [NTILE, 128, NRO])
    srh_d = din("srh", [128, 4, 4])
    src_d = din("src", [128, 4, 4, 3])
    sfc_d = din("sfc", [128, 44, 4, 2])

    y_d = dout("y", [HALF, D])
    pckv_d = dout("pckv", [HALF, NKV])
    pkr_d = dout("pkr", [HALF, NRO])
    prh_d = dout("prh", [128, 4])
    prc_d = dout("prc", [128, 4, 3])
    pfc_d = dout("pfc", [128, 44, 2])
    ys_d = dout("ys", [64, D])
    sckv_d = dout("sckv", [64, NKV])
    skr_d = dout("skr", [64, NRO])
    srh_o = dout("srho", [128, 4, 4])
    src_o = dout("srco", [128, 4, 4, 3])
    sfc_o = dout("sfco", [128, 44, 4, 2])

    wup_s = nc.dram_tensor("wup_s", [128, 8, 2 * FF], BF16, kind="Internal").ap()
    wdn_s = nc.dram_tensor("wdn_s", [128, NPAIR, D], BF16, kind="Internal").ap()
    wrg_s = nc.dram_tensor("wrg_s", [128, 8, 1024], BF16, kind="Internal").ap()
    wout_s = nc.dram_tensor("wout_s", [128, 8, D], BF16, kind="Internal").ap()

    def sb(name, shape, dt):
        return es.enter_context(nc.sbuf_tensor(name, list(shape), dt))

    W_IN = sb("W_IN", [128, 8, 672], BF16)
    W_UQ = sb("W_UQ", [128, 3, 768], BF16)
    W_UKT = sb("W_UKT", [128, 8, 256], BF16)
    W_UV = sb("W_UV", [128, 2, 512], BF16)
    W_RGA = sb("W_RGA", [128, 4, 128], BF16)
    W_RGI = sb("W_RGI", [128, 4, 128], BF16)
    IDB = sb("IDB", [128, 128], BF16)
    GFIN = sb("GFIN", [128, D], F32)
    GKV = sb("GKV", [128, NKV], F32)
    TAB = sb("TAB", [128, 2, 32], F32)
    COSS = sb("COSS", [64, 16], F32)
    SINS = sb("SINS", [64, 16], F32)
    CST = sb("CST", [128, C_NCOL], F32)
    EPSB = sb("EPSB", [128, 1], F32)
    ONEB = sb("ONEB", [128, 1], F32)
    CL = sb("CL", [128, 8], F32)
    NB = sb("NB", [128, 8], F32)
    KTC = sb("KTC", [128, 2, NF * 128], BF16)
    KTR = sb("KTR", [128, NF * 128], BF16)
    V = sb("V", [128, NF, 257], BF16)
    KTCN = sb("KTCN", [128, 2, 64], BF16)
    KTRN = sb("KTRN", [128, 64], BF16)
    VN = sb("VN", [64, 257], BF16)
    XB = sb("XB", [128, 2, D], F32)
    XN = sb("XN", [128, D], BF16)
    XNT2 = [sb("XNT0", [128, 8, 256], BF16)]
    XNT2.append(XNT2[0])
    QTCH = sb("QTCH", [128, 2, 8, 2], BF16)
    QTRH = sb("QTRH", [128, 8, 2], BF16)
    ONH = sb("ONH", [16, 256], BF16)
    OTH = sb("OTH", [128, 2, 8, 2], BF16)
    CATMH = sb("CATMH", [128, 4, 2], BF16)
    CATRH = sb("CATRH", [128, 4, 2], BF16)
    XH2 = sb("XH2", [128, 8, 2], BF16)
    HG = sb("HG", [128, 4, 2], F32)
    HL = sb("HL", [128, 4, 2], F32)
    PTH = sb("PTH", [128, 64], BF16)
    RCPH = sb("RCPH", [16, 1], F32)
    XN2T = sb("XN2T", [128, 8, 258], BF16)
    CKV = sb("CKV", [128, 1, NKV], F32)
    KR = sb("KR", [128, 1, NRO], F32)
    KRB4 = sb("KRB4", [128, 2, 4, NRO], BF16)
    KRB = KRB4[:, 0, 0, :]
    ST = sb("ST", [128, 32], F32)
    T1 = sb("T1", [128, 8, 32], F32)
    T2 = sb("T2", [128, 8, 32], F32)
    CQN = sb("CQN", [128, NQ], BF16)
    CQNT = sb("CQNT", [128, 3, 128], BF16)
    QNT = sb("QNT", [128, 4, 128], BF16)
    QTC2 = [sb("QTC%d" % i, [128, 2, 8, 128], BF16) for i in range(2)]
    QR = sb("QR", [128, 8, 32], BF16)
    QTR2 = [sb("QTR%d" % i, [128, 8, 128], BF16) for i in range(2)]
    NPT = 2
    PT = [sb("PT%d" % i, [128, 512], BF16) for i in range(NPT)]
    PTD = [sb("PTD%d" % i, [128, 4, 128], BF16) for i in range(1)]
    ON = sb("ON", [128, 4, 256], BF16)
    OT2 = [sb("OT%d" % i, [128, 2, 8, 128], BF16) for i in range(2)]
    RCP = sb("RCP", [128, 8], F32)
    OML = sb("OML", [128, 512], BF16)
    CATM = sb("CATM", [128, 4, 256], BF16)
    CATR = [sb("CATR%d" % i, [128, 4, 256], BF16) for i in range(2)]
    XTMP = sb("XTMP", [128, D], F32)
    RX = sb("RX", [128, 4, 4 * 19 + 200], F32)
    RGT = {n: sb("RG_" + n, [128, 256], F32) for n in ["XC", "R", "I", "A", "M", "G"]}
    XCB = sb("XCB", [128, 256], BF16)
    HST = sb("HST", [128, 4, 4], F32)
    WU = [sb("WU%d" % i, [128, 8, 256], BF16) for i in range(2)]
    WD = [sb("WD%d" % i, [128, D], BF16) for i in range(2)]
    UA = [sb("UA%d" % i, [128, 72], F32) for i in range(1)]
    UBf = [sb("UB%d" % i, [128, 72], F32) for i in range(1)]
    ACA = [sb("ACA%d" % i, [128, 256], F32) for i in range(2)]
    ACB = [sb("ACB%d" % i, [128, 256], F32) for i in range(2)]
    GA = [sb("GA%d" % i, [128, 256], F32) for i in range(1)]
    HT = [sb("HT%d" % i, [128, 256], BF16) for i in range(2)]
    UH = [sb("UH%d" % i, [128, 44, 4, 2], F32) for i in range(2)]

    PS = [es.enter_context(nc.psum_tensor("PS%d" % i, [128, 512], F32)) for i in range(8)]

    def psb(b):
        return PS[b][:].bitcast(BF16)

    R = {}

    def res(name):
        if name not in R:
            R[name] = Res(name)
        return R[name]

    PSR = [Res("ps%d" % b, excl=True) for b in range(8)]

    def psr(b, lo=0, hi=512):
        return [PSR[b]]

    KVR = [res("kv%d" % i) for i in range(NF)]
    SETUP = res("setup")
    sgrp = []

    def MM(out, lhsT, rhs, start, stop, reads, writes):
        P.add("pe", lambda: nc.tensor.matmul(out, lhsT, rhs, start=start, stop=stop), reads, writes)

    def TR(out, in_, ident, reads, writes):
        P.add("pe", lambda: nc.tensor.transpose(out, in_, ident), reads, writes)

    def ACT(out, in_, func, reads, writes, bias=None, scale=None, accum=None):
        kw = {}
        if bias is not None:
            kw["bias"] = bias
        if scale is not None:
            kw["scale"] = scale
        if accum is not None:
            kw["accum_out"] = accum
        P.add("act", lambda: nc.scalar.activation(out=out, in_=in_, func=func, **kw), reads, writes)

    def TS(eng, out, in0, s1, s2, op0, op1, reads, writes):
        e = nc.vector if eng == "dve" else nc.gpsimd
        if op1 is None:
            P.add(eng, lambda: e.tensor_scalar(out=out, in0=in0, scalar1=s1, scalar2=None, op0=op0), reads, writes)
        else:
            P.add(eng, lambda: e.tensor_scalar(out=out, in0=in0, scalar1=s1, scalar2=s2, op0=op0, op1=op1),
                  reads, writes)

    def STT(out, in0, scalar, in1, op0, op1, reads, writes):
        P.add("dve", lambda: nc.vector.scalar_tensor_tensor(out=out, in0=in0, scalar=scalar, in1=in1,
                                                           op0=op0, op1=op1), reads, writes)

    def TT(eng, out, in0, in1, op, reads, writes):
        e = nc.vector if eng == "dve" else nc.gpsimd
        P.add(eng, lambda: e.tensor_tensor(out=out, in0=in0, in1=in1, op=op), reads, writes)

    def CP(eng, out, in_, reads, writes):
        if eng == "act":
            P.add("act", lambda: nc.scalar.copy(out=out, in_=in_), reads, writes)
        else:
            e = nc.vector if eng == "dve" else nc.gpsimd
            P.add(eng, lambda: e.tensor_copy(out=out, in_=in_), reads, writes)

    def DMA(q, out, in_, reads, writes, anchor, grp=None, slow=False):
        e = {"sp": nc.sync, "act": nc.scalar, "pool": nc.gpsimd}[q]
        if slow:
            P.add(q, lambda: e.dma_start(out=out, in_=in_, allow_slow_non_contiguous=True), reads, writes,
                  dma=anchor, grp=grp)
        else:
            P.add(q, lambda: e.dma_start(out=out, in_=in_), reads, writes, dma=anchor, grp=grp)

    def MEMSET(eng, ap, val, writes):
        e = nc.vector if eng == "dve" else nc.gpsimd
        P.add(eng, lambda: e.memset(ap, val), (), writes)

    rW = res("weights")
    rC = res("consts")
    for (dst, src) in [(GFIN, gfin_d), (GKV, gkv_d), (COSS, cossd), (SINS, sinsd),
                       (CST, cst)]:
        DMA("sp", dst[:], src, (), [rC], SETUP, grp=sgrp)
    SETUP_P = res("setup_p")
    sgrp_p = []
    for (dst, src) in [(W_UKT, w_ukt_d), (W_UV, w_uv_d), (W_RGA, w_rga_d), (W_RGI, w_rgi_d),
                       (IDB, ident_d)]:
        DMA("pool", dst[:], src, (), [rW], SETUP_P, grp=sgrp_p)
    P.close_group(sgrp)
    P.close_group(sgrp_p)

    rT = res("setup_tmp")
    ACT(ST[:, 0:4], CST[:, C_LAM:C_LAM + 4], AF.Exp, [rC], [rT], scale=-1.0)
    MEMSET("pool", EPSB[:], EPS, [rC])
    MEMSET("pool", ONEB[:], 1.0, [rC])
    ACT(ST[:, 4:8], ST[:, 0:4], AF.Ln, [rT, rC], [rT], bias=ONEB[:, 0:1])
    TS("dve", CL[:, 0:4], ST[:, 4:8], -8.0, None, ALU.mult, None, [rT], [rC])
    TS("dve", CL[:, 4:8], ST[:, 4:8], -16.0, None, ALU.mult, None, [rT], [rC])
    TS("dve", NB[:, 0:4], CST[:, C_BA:C_BA + 4], -1.0, None, ALU.mult, None, [rC], [rC])
    TS("dve", NB[:, 4:8], CST[:, C_BI:C_BI + 4], -1.0, None, ALU.mult, None, [rC], [rC])

    STG = KTC[:].rearrange("p a s -> p (a s)")[:, 0:16384].bitcast(F32).rearrange("p (a s) -> p a s", a=4)
    STGB = V[:].rearrange("p a s -> p (a s)")[:, 0:2 * 2048].rearrange("p (a s) -> p a s", a=2)
    rS = [res("stg%d" % i) for i in range(4)]
    rSB = [res("stgb%d" % i) for i in range(2)]
    rWU = res("wup_s")
    rWU.multi = True
    rWD = res("wdn_s")
    rWD.multi = True
    si = 0
    rWRG = res("wrg_s")
    rWRG.multi = True
    bi_ = 0
    for k in range(8):
        sl = si % 4
        si += 1
        bl = bi_ % 2
        bi_ += 1
        DMA("sp", STG[:, sl, 0:INC], w_in_d[:, k, :], (), [rS[sl]], rS[sl])
        TS("dve", W_IN[:, k, :], STG[:, sl, 0:672], CST[:, C_GMIX + k:C_GMIX + k + 1], None, ALU.mult, None,
           [rS[sl], rC], [rW])
        sv = STG[:, sl, 672:INC].rearrange("p (t c n) -> p c t n", t=2, c=4)
        dv = STGB[:, bl, 0:1024].rearrange("p (c t n) -> p c t n", c=4, t=2)
        TS("dve", dv, sv, CST[:, C_GMIX + k:C_GMIX + k + 1], None, ALU.mult, None, [rS[sl], rC], [rSB[bl]])
        DMA("act", wrg_s[:, k, :], STGB[:, bl, 0:1024], [rSB[bl]], [rWRG], rSB[bl])
    for k in range(3):
        sl = si % 4
        si += 1
        DMA("sp", STG[:, sl, 0:768], w_uq_d[:, k, :], (), [rS[sl]], rS[sl])
        TS("dve", W_UQ[:, k, :], STG[:, sl, 0:768], CST[:, C_GQ + k:C_GQ + k + 1], None, ALU.mult, None,
           [rS[sl], rC], [rW])
    for k in range(8):
        for (c0, c1) in [(0, 2048), (2048, 4096), (4096, 2 * FF)]:
            sl = si % 4
            si += 1
            bl = bi_ % 2
            bi_ += 1
            n = c1 - c0
            DMA("sp", STG[:, sl, 0:n], w_up_d[:, k, c0:c1], (), [rS[sl]], rS[sl])
            TS("dve", STGB[:, bl, 0:n], STG[:, sl, 0:n], CST[:, C_GFFN + k:C_GFFN + k + 1], None, ALU.mult, None,
               [rS[sl], rC], [rSB[bl]])
            DMA("act", wup_s[:, k, c0:c1], STGB[:, bl, 0:n], [rSB[bl]], [rWU], rSB[bl])
    for j in range(0, NPAIR, 2):
        bl = bi_ % 2
        bi_ += 1
        DMA("pool", STGB[:, bl, :], w_dn_d[:, j:j + 2, :].rearrange("p a d -> p (a d)"), (), [rSB[bl]],
            res("stgbl%d" % bl))
        DMA("act", wdn_s[:, j:j + 2, :].rearrange("p a d -> p (a d)"), STGB[:, bl, :], [rSB[bl]], [rWD], rSB[bl])

    rWO = res("wout_s")
    rWO.multi = True
    for k in range(0, 8, 2):
        bl = bi_ % 2
        bi_ += 1
        DMA("pool", STGB[:, bl, :], w_out_d[:, k:k + 2, :].rearrange("p a d -> p (a d)"), (), [rSB[bl]],
            res("stgbl%d" % bl))
        DMA("act", wout_s[:, k:k + 2, :].rearrange("p a d -> p (a d)"), STGB[:, bl, :], [rSB[bl]], [rWO], rSB[bl])

    rPTD = [res("ptd0")]
    MEMSET("pool", PTD[0][:], 0.0, [rPTD[0]])
    rQT2 = [res("qt0"), res("qt1")]
    for i in range(2):
        MEMSET("pool", QTR2[i][:], 0.0, [rQT2[i]])
    rQTH = res("qth")
    MEMSET("pool", QTRH[:], 0.0, [rQTH])
    rHSTc = [res("hst%d" % c) for c in range(4)]
    rHST = rHSTc
    MEMSET("pool", HST[:], 0.0, rHSTc)
    rRXc = [res("rxh%d" % c) for c in range(4)]
    rRXH = rRXc
    MEMSET("pool", RX[:], 0.0, rRXc)
    rUH = [res("uh0"), res("uh1")]
    MEMSET("pool", UH[0][:], 0.0, [rUH[0]])
    MEMSET("pool", UH[1][:], 0.0, [rUH[1]])
    rVN = res("vn")
    MEMSET("pool", V[:, :, 256:257], 1.0, KVR + rS + rSB)
    MEMSET("pool", KTR[:, :], 0.0, KVR)
    MEMSET("pool", KTRN[:, :], 0.0, [rVN])
    MEMSET("pool", VN[:, 256:257], 1.0, [rVN])

    rXB = [res("xb0"), res("xb1")]
    rXN = res("xn")
    rXNT2 = [[res("xnt0_%d" % s_) for s_ in range(2)]]
    rXNT2.append(rXNT2[0])
    rXN2T = [res("xn2t0"), res("xn2t1")]
    rTAB = [res("tab0"), res("tab1")]
    stc = [0]

    def stcol():
        c = 8 + (stc[0] % 24)
        stc[0] += 1
        return c

    def rstd_from(Pn, src_ap, src_res, dim, junk_ap, junk_res):
        c = stcol()
        rr = res("stc%d" % c)
        ACT(junk_ap, src_ap, AF.Square, src_res, [junk_res, rr], accum=ST[:Pn, c:c + 1])
        ACT(ST[:Pn, c:c + 1], ST[:Pn, c:c + 1], AF.Ln, [rr], [rr], scale=1.0 / dim, bias=EPSB[:Pn, 0:1])
        ACT(ST[:Pn, c:c + 1], ST[:Pn, c:c + 1], AF.Exp, [rr], [rr], scale=-0.5)
        return ST[:Pn, c:c + 1], rr

    rXTMP = res("xtmp")

    def stage_a(xsrc, Pn, s, tab_src, use_tmp=False, xi=0):
        XNT, rXNT = XNT2[xi], rXNT2[xi]
        if use_tmp:
            xb, rxb = XTMP[:Pn, :], rXTMP
        else:
            xb, rxb = XB[:Pn, s, :], rXB[s]
        DMA("sp", xb, xsrc, (), [rxb], rxb)
        if tab_src is not None:
            DMA("sp", TAB[:Pn, s, :], tab_src, (), [rTAB[s]], rTAB[s])
        rs, rr = rstd_from(Pn, xb, [rxb], D, XN[:Pn, :], rXN)
        TS("dve", XN[:Pn, :], xb, rs, None, ALU.mult, None, [rxb, rr], [rXN])
        pb = psb(7)
        for k in range(8):
            TR(pb[:, k * 128:k * 128 + Pn], XN[:Pn, k * 128:(k + 1) * 128], IDB[:Pn, :Pn], [rXN, rW], psr(7))
        CP("act", XNT[:, :, s * Pn:(s + 1) * Pn], pb[:, 0:1024].rearrange("p (k t) -> p k t", k=8)[:, :, 0:Pn],
           psr(7), [rXNT[s]])

    rCKV = res("ckv")
    rKR = res("kr")
    rKRB = res("krb4_0")
    rT1 = res("t1")
    rT2 = res("t2")
    rOML = res("oml")

    def kv_side(Pn, s, cos_ap, sin_ap, tab_res, with_q, dst, out_ckv=None, out_kr=None, xi=0):
        XNT, rXNT = XNT2[xi], rXNT2[xi]
        tok = slice(s * Pn, (s + 1) * Pn)
        if with_q:
            for k in range(8):
                MM(PS[6][:Pn, 0:NQ], XNT[:, k, tok], W_IN[:, k, 0:NQ], k == 0, k == 7, [rXNT[s], rW], psr(6))
        if dst is None:
            return
        (ktc_ap, ktr_ap, v_ap, kvres) = dst
        for k in range(8):
            MM(PS[7][:Pn, 0:288], XNT[:, k, tok], W_IN[:, k, NQ:NQ + 288], k == 0, k == 7, [rXNT[s], rW], psr(7))
        rs, rr = rstd_from(Pn, PS[7][:Pn, 0:NKV], psr(7), NKV, OML[:Pn, 0:NKV], rOML)
        STT(CKV[:Pn, 0, :], PS[7][:Pn, 0:NKV], rs, GKV[:Pn, :], ALU.mult, ALU.mult, psr(7) + [rr, rC], [rCKV])
        xr = PS[7][:Pn, 256:288]
        TT("dve", T1[:Pn, 0, :].rearrange("p (a e) -> p a e", a=2), xr.rearrange("p (a e) -> p a e", a=2),
           cos_ap.unsqueeze(1).to_broadcast([Pn, 2, 16]), ALU.mult, psr(7) + tab_res, [rT1])
        TT("dve", T2[:Pn, 0, 0:16], PS[7][:Pn, 272:288], sin_ap, ALU.mult, psr(7) + tab_res, [rT2])
        TT("dve", T2[:Pn, 0, 16:32], PS[7][:Pn, 256:272], sin_ap, ALU.mult, psr(7) + tab_res, [rT2])
        TT("dve", KR[:Pn, 0, 0:16], T1[:Pn, 0, 0:16], T2[:Pn, 0, 0:16], ALU.subtract, [rT1, rT2], [rKR])
        TT("dve", KR[:Pn, 0, 16:32], T1[:Pn, 0, 16:32], T2[:Pn, 0, 16:32], ALU.add, [rT1, rT2], [rKR])
        if out_ckv is not None:
            DMA("sp", out_ckv, CKV[:Pn, 0, :], [rCKV], (), rCKV)
            DMA("sp", out_kr, KR[:Pn, 0, :], [rKR], (), rKR)
        CP("pool", v_ap[:Pn, 0:NKV], CKV[:Pn, 0, :], [rCKV], kvres)
        CP("pool", KRB[:Pn, :], KR[:Pn, 0, :], [rKR], [rKRB])
        pb = psb(7)
        for c in range(2):
            TR(pb[:, c * 128:c * 128 + Pn], v_ap[:Pn, c * 128:(c + 1) * 128], IDB[:Pn, :Pn], kvres + [rW], psr(7))
        TR(pb[0:32, 256:256 + Pn], KRB[:Pn, :], IDB[:Pn, :Pn], [rKRB, rW], psr(7))
        CP("act", ktc_ap, pb[:, 0:256].rearrange("p (c t) -> p c t", c=2)[:, :, 0:Pn], psr(7), kvres)
        CP("act", ktr_ap, pb[0:32, 256:256 + Pn], psr(7), kvres)

    rRG = {n: res("rg_" + n) for n in RGT}
    rXCB = res("xcb")
    rCAT = [res("cat_mla0"), res("cat_mla1")]
    rCATR = [res("cat_rg0"), res("cat_rg1")]
    rWUs = [res("wu0"), res("wu1")]
    wuc = [0]
    rWDs = [res("wd0"), res("wd1")]
    wdc = [0]

    rACA = [res("aca0"), res("aca1")]
    rACB = [res("acb0"), res("acb1")]
    rGA = [res("ga0")]
    rHT = [res("ht0"), res("ht1")]
    rHG = res("hg")
    rHL = res("hl")
    rCATRH = res("catrh")
    RGS = [
        dict(XC=(RGT["XC"], rRG["XC"]), R=(RGT["R"], rRG["R"]), I=(RGT["I"], rRG["I"]), A=(RGT["A"], rRG["A"]),
             M=(RGT["M"], rRG["M"]), G=(RGT["G"], rRG["G"]), XCB=(XCB, rXCB)),
        dict(XC=(ACA[0], rACA[0]), R=(ACA[1], rACA[1]), I=(ACB[0], rACB[0]), A=(ACB[1], rACB[1]),
             M=(GA[0], rGA[0]), G=None, XCB=(HT[0], rHT[0])),
    ]

    def rg_branch(N, nseg, L, full, s_list, par=0, chunks=(0, 1, 2, 3), tset=0, banks=(6, 7), xi=0, wu_slot=None,
                  halo_gate=False):
        XNT, rXNT = XNT2[xi], rXNT2[xi]
        T = RGS[tset]
        (XC_, rXC_), (R_, rR_), (I_, rI_), (A_, rA_), (M_, rM_), (XCB_, rXCB_) = \
            T["XC"], T["R"], T["I"], T["A"], T["M"], T["XCB"]
        bx, bg_ = banks
        W = 3 + L
        rx_reads = [rXNT[s] for s in s_list]
        for c in chunks:
            rRX, rHS = rRXc[c], rHSTc[c]
            if wu_slot is None:
                sl = wuc[0] % 2
                wuc[0] += 1
            else:
                sl = wu_slot
            DMA("sp", WU[sl][:], wrg_s[:, :, c * 256:(c + 1) * 256], [rWRG], [rWUs[sl]], rWUs[sl])
            rxc = RX[:, c, 0:nseg * W].rearrange("p (g w) -> p g w", g=nseg)
            for k in range(8):
                MM(PS[bx][:, 0:N], WU[sl][:, k, 0:128], XNT[:, k, 0:N], k == 0, k == 7,
                   rx_reads + [rWUs[sl]], psr(bx))
            CP("act", rxc[:, :, 3:3 + L], PS[bx][:, 0:N].rearrange("p (g l) -> p g l", g=nseg), psr(bx), [rRX])
            if full:
                (G_, rG_) = T["G"]
                for k in range(8):
                    MM(PS[bg_][:, 0:N], WU[sl][:, k, 128:256], XNT[:, k, 0:N], k == 0, k == 7,
                       rx_reads + [rWUs[sl]], psr(bg_))
                ACT(G_[:, 0:N], PS[bg_][:, 0:N], AF.Gelu_apprx_tanh, psr(bg_), [rG_])
            elif halo_gate:
                for k in range(8):
                    MM(PS[bg_][:, 0:2], WU[sl][:, k, 128:256], XNT[:, k, N - 2:N], k == 0, k == 7,
                       rx_reads + [rWUs[sl]], psr(bg_))
                CP("act", HG[:, c, :], PS[bg_][:, 0:2], psr(bg_), [rHG])
            xc = XC_[:, 0:N].rearrange("p (g l) -> p g l", g=nseg)
            wc = lambda k: CST[:, C_RGW + c * 4 + k:C_RGW + c * 4 + k + 1]
            TS("dve", xc, rxc[:, :, 0:L], wc(0), CST[:, C_RGB + c:C_RGB + c + 1], ALU.mult, ALU.add,
               [rRX, rC], [rXC_])
            for k in range(1, 4):
                STT(xc, rxc[:, :, k:k + L], wc(k), xc, ALU.mult, ALU.add, [rRX, rC, rXC_], [rXC_])
            CP("pool", rxc[:, :, 0:3], rxc[:, :, L:L + 3], [rRX], [rRX])
            CP("act", XCB_[:, 0:N], XC_[:, 0:N], [rXC_], [rXCB_])
            MM(PS[bx][:, 256:256 + N], W_RGA[:, c, :], XCB_[:, 0:N], True, True, [rXCB_, rW], psr(bx))
            MM(PS[bg_][:, 256:256 + N], W_RGI[:, c, :], XCB_[:, 0:N], True, True, [rXCB_, rW], psr(bg_))
            for (X_, rX_, bnk, col) in [(R_, rR_, bx, c), (I_, rI_, bg_, 4 + c)]:
                ACT(X_[:, 0:N], PS[bnk][:, 256:256 + N], AF.Exp, psr(bnk) + [rC], [rX_], scale=-1.0,
                    bias=NB[:, col:col + 1])
                TS("dve", X_[:, 0:N], X_[:, 0:N], 1.0, None, ALU.add, None, [rX_], [rX_])
                P.add("dve", (lambda X_=X_: nc.vector.reciprocal(out=X_[:, 0:N], in_=X_[:, 0:N])), [rX_], [rX_])
            ACT(A_[:, 0:N], R_[:, 0:N], AF.Exp, [rR_, rC], [rA_], scale=CL[:, c:c + 1])
            ACT(M_[:, 0:N], R_[:, 0:N], AF.Exp, [rR_, rC], [rM_], scale=CL[:, 4 + c:5 + c])
            ACT(M_[:, 0:N], M_[:, 0:N], AF.Ln, [rM_], [rM_], scale=-1.0, bias=ONEB[:, 0:1])
            ACT(M_[:, 0:N], M_[:, 0:N], AF.Exp, [rM_], [rM_], scale=0.5)
            TT("dve", I_[:, 0:N], I_[:, 0:N], XC_[:, 0:N], ALU.mult, [rI_, rXC_], [rI_])
            TT("dve", M_[:, 0:N], M_[:, 0:N], I_[:, 0:N], ALU.mult, [rM_, rI_], [rM_])
            for g in range(nseg):
                sl_ = slice(g * L, (g + 1) * L)
                P.add("dve", (lambda g=g, sl_=sl_, c=c: nc.vector.tensor_tensor_scan(
                    out=M_[:, sl_], data0=A_[:, sl_], data1=M_[:, sl_],
                    initial=HST[:, c, g:g + 1], op0=ALU.mult, op1=ALU.add)),
                    [rA_, rM_, rHS], [rM_])
            hv = M_[:, 0:N].rearrange("p (g l) -> p g l", g=nseg)
            CP("pool", HST[:, c, 0:nseg], hv[:, :, L - 1], [rM_], [rHS])
            if full:
                TT("dve", CATR[par][:, c, 0:N], M_[:, 0:N], G_[:, 0:N], ALU.mult, [rM_, rG_], [rCATR[par]])
            elif halo_gate:
                CP("pool", HL[:, c, :], M_[:, N - 2:N], [rM_], [rHL])
        if halo_gate and not full:
            ACT(HG[:, :, :], HG[:, :, :], AF.Gelu_apprx_tanh, [rHG], [rHG])
            TT("dve", CATRH[:, :, :], HL[:, :, :], HG[:, :, :], ALU.mult, [rHL, rHG], [rCATRH])

    rCQN = res("cqn")
    rCQNT = res("cqnt")
    rQNT = res("qnt")
    rQR = res("qr")

    def q_path(Pn, s, cos_ap, sin_ap, tab_res, qi=0):
        if qi == "h":
            QTC, QTR, rQT = QTCH, QTRH, rQTH
        else:
            QTC, QTR, rQT = QTC2[qi], QTR2[qi], rQT2[qi]
        rs, rr = rstd_from(Pn, PS[6][:Pn, 0:NQ], psr(6), NQ, CQN[:Pn, :], rCQN)
        TS("dve", CQN[:Pn, :], PS[6][:Pn, 0:NQ], rs, None, ALU.mult, None, psr(6) + [rr], [rCQN])
        pb = psb(6)
        for k in range(3):
            TR(pb[:, k * 128:k * 128 + Pn], CQN[:Pn, k * 128:(k + 1) * 128], IDB[:Pn, :Pn], [rCQN, rW], psr(6))
        CP("act", CQNT[:, :, 0:Pn], pb[:, 0:384].rearrange("p (k t) -> p k t", k=3)[:, :, 0:Pn], psr(6), [rCQNT])
        for m in range(4):
            for k in range(3):
                MM(PS[6][:, m * 128:m * 128 + Pn], W_UQ[:, k, m * 128:(m + 1) * 128], CQNT[:, k, 0:Pn],
                   k == 0, k == 2, [rCQNT, rW], psr(6))
        CP("dve", QNT[:, :, 0:Pn], PS[6][:, :].rearrange("p (m t) -> p m t", m=4)[:, :, 0:Pn], psr(6), [rQNT])
        for k in range(3):
            MM(PS[7][:Pn, 0:256], CQNT[:, k, 0:Pn], W_UQ[:, k, 512:768], k == 0, k == 2, [rCQNT, rW], psr(7))
        xr = PS[7][:Pn, 0:256].rearrange("p (h a e) -> p h a e", h=8, a=2)
        cb = cos_ap.unsqueeze(1).unsqueeze(1).to_broadcast([Pn, 8, 2, 16])
        sb_ = sin_ap.unsqueeze(1).to_broadcast([Pn, 8, 16])
        t1 = T1[:Pn, :, :].rearrange("p h (a e) -> p h a e", a=2)
        TT("dve", t1, xr, cb, ALU.mult, psr(7) + tab_res, [rT1])
        TT("dve", T2[:Pn, :, 0:16], xr[:, :, 1, :], sb_, ALU.mult, psr(7) + tab_res, [rT2])
        TT("dve", T2[:Pn, :, 16:32], xr[:, :, 0, :], sb_, ALU.mult, psr(7) + tab_res, [rT2])
        TT("dve", QR[:Pn, :, 0:16], T1[:Pn, :, 0:16], T2[:Pn, :, 0:16], ALU.subtract, [rT1, rT2], [rQR])
        TT("dve", QR[:Pn, :, 16:32], T1[:Pn, :, 16:32], T2[:Pn, :, 16:32], ALU.add, [rT1, rT2], [rQR])
        pb7 = psb(7)
        for h in range(8):
            TR(pb7[0:32, h * 128:h * 128 + Pn], QR[:Pn, h, :], IDB[:Pn, :Pn], [rQR, rW], psr(7))
        CP("act", QTR[0:32, :, 0:Pn], pb7[0:32, 0:1024].rearrange("p (h t) -> p h t", h=8)[:, :, 0:Pn], psr(7),
           [rQT])
        bsel = [6, 7]
        bi2 = 0
        for c in range(2):
            for g in range(2):
                b = bsel[bi2 % 2]
                bi2 += 1
                for hh in range(4):
                    h = g * 4 + hh
                    po = (h % 2) * 64
                    MM(PS[b][:, hh * 128:hh * 128 + Pn], W_UKT[:, h, c * 128:(c + 1) * 128],
                       QNT[:, h // 2, 0:Pn], True, True, [rQNT, rW], psr(b))
                CP("dve" if (bi2 % 2) else "act", QTC[:, c, g * 4:(g + 1) * 4, 0:Pn],
                   PS[b][:, :].rearrange("p (h t) -> p h t", h=4)[:, :, 0:Pn], psr(b), [rQT])

    rPT = [res("pt%d" % i) for i in range(NPT)]
    rON = res("on")
    rOT2 = [res("ot0"), res("ot1")]
    ptc = [0]
    scb = [0]
    rcpc = [0]

    def attn_group(qrhs_c, qrhs_r, ncol, keytiles, acc_banks, out_views, rQT, bg=None, bg_step=0, sviews=None,
                   pipelined=True, pt_pool=None, rcp_tile=None):
        nk = len(keytiles)
        sbank = {}

        def scores(ki):
            (kc, kr, va, kp, kvres, bias, isdiag) = keytiles[ki]
            if sviews is None:
                b, co = scb[0] % 2, 0
            else:
                b, co = sviews[scb[0] % len(sviews)]
            scb[0] += 1
            sbank[ki] = (b, co)
            sview = PS[b][:kp, co:co + ncol]
            MM(sview, kc(0), qrhs_c(0), True, False, kvres + [rQT], psr(b))
            MM(sview, kc(1), qrhs_c(1), False, False, kvres + [rQT], psr(b))
            MM(sview, kr, qrhs_r, False, True, kvres + [rQT], psr(b))

        def exp_pv(ki):
            (kc, kr, va, kp, kvres, bias, isdiag) = keytiles[ki]
            b, co = sbank[ki]
            sview = PS[b][:kp, co:co + ncol]
            isap = not isinstance(bias, float)
            rd = psr(b) + ([rC] if isap else [])
            if isdiag:
                pt_t, pt_r = PTD[0], rPTD[0]
                pv = pt_t[:].rearrange("p h t -> p (h t)")
                ACT(pv[0:64, 0:ncol], PS[b][0:64, 0:ncol], AF.Exp, rd, [pt_r], scale=SCALE,
                    bias=(bias[0:64, :] if isap else bias))
                s3 = PS[b][:, 0:512].rearrange("p (h t) -> p h t", h=4)
                ACT(pt_t[64:128, :, 64:128], s3[64:128, :, 64:128], AF.Exp, rd, [pt_r], scale=SCALE,
                    bias=(bias[64:128, :] if isap else bias))
            else:
                if pt_pool is None:
                    d = ptc[0] % NPT
                    ptc[0] += 1
                    pt_t, pt_r = PT[d], rPT[d]
                else:
                    pt_t, pt_r = pt_pool
                pv = pt_t[:]
                ACT(pv[:kp, 0:ncol], sview, AF.Exp, rd, [pt_r], scale=SCALE, bias=bias)
            for ai, (ab, lo, hi) in enumerate(acc_banks):
                MM(PS[ab][:hi - lo, 0:257], pv[:kp, lo:hi], va, ki == 0, ki == nk - 1, [pt_r] + kvres, psr(ab))

        if pipelined:
            scores(0)
        for ki in range(nk):
            if pipelined:
                if ki + 1 < nk:
                    scores(ki + 1)
            else:
                scores(ki)
            exp_pv(ki)
            if bg is not None:
                P.replay_sched(bg, max(16, bg_step))
        for ai, (ab, lo, hi) in enumerate(acc_banks):
            M = hi - lo
            if rcp_tile is None:
                c = rcpc[0] % 8
                rcpc[0] += 1
                rr = res("rcp%d" % c)
                RCP_ = RCP
            else:
                c = 0
                RCP_, rr = rcp_tile[0], rcp_tile[1]
            TS("dve", RCP_[:M, c:c + 1], PS[ab][:M, 256:257], 1e-30, None, ALU.add, None, psr(ab), [rr])
            P.add("dve", (lambda c=c, M=M, RCP_=RCP_: nc.vector.reciprocal(out=RCP_[:M, c:c + 1],
                                                                            in_=RCP_[:M, c:c + 1])), [rr], [rr])
            TS("dve", out_views[ai], PS[ab][:M, 0:256], RCP_[:M, c:c + 1], None, ALU.mult, None, psr(ab) + [rr],
               [rON if rcp_tile is None else rcp_tile[2]])

    def attention_prompt(i, qi, bg=None, drain=True):
        QTC, QTR, rQT, OT, rOTq = QTC2[qi], QTR2[qi], rQT2[qi], OT2[qi], rOT2[qi]
        bg_step = 0
        if bg is not None:
            bg_step = len(bg) // (2 * (i + 1)) + 1
        for g in range(2):
            kts = []
            for kt in range(i + 1):
                kc = (lambda c, kt=kt: KTC[:, c, kt * 128:(kt + 1) * 128])
                bias = CST[:, C_KB:C_KB + 1] if kt < 2 else 0.0
                kts.append((kc, KTR[:, kt * 128:(kt + 1) * 128], V[:, kt, :], 128, [KVR[kt]], bias, kt == i))
            qc = (lambda c, g=g: QTC[:, c, g * 4:(g + 1) * 4, :])
            qr = QTR[:, g * 4:(g + 1) * 4, :]
            accs = [(2 + hh, hh * 128, (hh + 1) * 128) for hh in range(4)]
            outs = [ON[:, hh, :] for hh in range(4)]
            attn_group(qc, qr, 512, kts, accs, outs, rQT, bg, bg_step)
            for c in range(2):
                pb = psb(c)
                for hh in range(4):
                    TR(pb[:, hh * 128:(hh + 1) * 128], ON[:, hh, c * 128:(c + 1) * 128], IDB[:, :], [rON, rW],
                       psr(c))
                CP("act" if c == 0 else "dve", OT[:, c, g * 4:(g + 1) * 4, :],
                   pb[:, 0:512].rearrange("p (h t) -> p h t", h=4), psr(c), [rOTq])
        if bg is not None and drain:
            while bg:
                P.replay_sched(bg, 10)

    rXH = res("xn2t_halo")

    def o_mla_and_out(Pn, OT, rOT, xb, rxb, catm, rcatm, catr, rcatr, xn2_dst, rxn2):
        for h in range(8):
            for c in range(2):
                MM(PS[6][:Pn, h * 64:(h + 1) * 64], OT[:, c, h, 0:Pn], W_UV[:, c, h * 64:(h + 1) * 64], c == 0, c == 1,
                   [rOT, rW], psr(6))
        CP("act", OML[:Pn, :], PS[6][:Pn, 0:512], psr(6), [rOML])
        pb = psb(7)
        for k in range(4):
            TR(pb[:, k * 128:k * 128 + Pn], OML[:Pn, k * 128:(k + 1) * 128], IDB[:Pn, :Pn], [rOML, rW], psr(7))
        CP("dve", catm, pb[:, 0:512].rearrange("p (k t) -> p k t", k=4)[:, :, 0:Pn], psr(7), [rcatm])
        for k in range(8):
            sl = wdc[0] % 2
            wdc[0] += 1
            DMA("sp", WD[sl][:], wout_s[:, k, :], [rWO], [rWDs[sl]], rWDs[sl])
            lhs = catm[:, k, :] if k < 4 else catr[:, k - 4, :]
            for n in range(2):
                MM(PS[6 + n][:Pn, :], lhs, WD[sl][:, n * 512:(n + 1) * 512], k == 0, k == 7,
                   [rcatm, rcatr, rWDs[sl]], psr(6 + n))
        for n in range(2):
            TT("dve", xb[:, n * 512:(n + 1) * 512], PS[6 + n][:Pn, :], xb[:, n * 512:(n + 1) * 512], ALU.add,
               psr(6 + n) + [rxb], [rxb])
        rs, rr = rstd_from(Pn, xb, [rxb], D, XN[:Pn, :], rXN)
        TS("dve", XN[:Pn, :], xb, rs, None, ALU.mult, None, [rxb, rr], [rXN])
        pb = psb(7)
        for k in range(8):
            TR(pb[:, k * 128:k * 128 + Pn], XN[:Pn, k * 128:(k + 1) * 128], IDB[:Pn, :Pn], [rXN, rW], psr(7))
        CP("act", xn2_dst, pb[:, 0:1024].rearrange("p (k t) -> p k t", k=8)[:, :, 0:Pn], psr(7), rxn2)

    rUA = [res("ua0")]
    rUB = [res("ub0")]

    def ffn_prompt(nsub, out_rows, save_state):
        N = 128 * nsub
        NW = N + 2
        Pn = 128
        xr = [rXN2T[s] for s in range(nsub)] + [rXH]
        up_banks = [(0, 1), (6, 7)]
        wus = {}

        def up(j):
            sl = wuc[0] % 2
            wuc[0] += 1
            DMA("sp", WU[sl][:], wup_s[:, :, j * 256:(j + 1) * 256], [rWU], [rWUs[sl]], rWUs[sl])
            (ba, bb) = up_banks[j % 2]
            for k in range(8):
                MM(PS[ba][:, 0:NW], WU[sl][:, k, 0:128], XN2T[:, k, 0:NW], k == 0, k == 7, [rWUs[sl]] + xr, psr(ba))
            for k in range(8):
                MM(PS[bb][:, 0:NW], WU[sl][:, k, 128:256], XN2T[:, k, 0:NW], k == 0, k == 7, [rWUs[sl]] + xr, psr(bb))

        def e1(j):
            (ba, bb) = up_banks[j % 2]
            sl = j % 2
            for (isb, bank, AC, rAC) in [(0, ba, ACA[sl], rACA[sl]), (1, bb, ACB[sl], rACB[sl])]:
                ch = j + 22 * isb
                wc = lambda k: CST[:, C_FCW + ch * 3 + k:C_FCW + ch * 3 + k + 1]
                ACT(AC[:, 0:N], PS[bank][:, 2:NW], AF.Identity, psr(bank) + [rC], [rAC], scale=wc(2),
                    bias=CST[:, C_FCB + ch:C_FCB + ch + 1])
                STT(AC[:, 0:N], PS[bank][:, 1:N + 1], wc(1), AC[:, 0:N], ALU.mult, ALU.add, psr(bank) + [rC, rAC], [rAC])
                STT(AC[:, 0:N], PS[bank][:, 0:N], wc(0), AC[:, 0:N], ALU.mult, ALU.add, psr(bank) + [rC, rAC], [rAC])
                if save_state:
                    CP("act", UH[0][:, ch, 0, :], PS[bank][:, N:NW], psr(bank), [rUH[0]])

        def e2(j):
            sl = j % 2
            ACT(GA[0][:, 0:N], ACA[sl][:, 0:N], AF.Gelu_apprx_tanh, [rACA[sl]], [rGA[0]])
            TT("pool", HT[sl][:, 0:N], GA[0][:, 0:N], ACB[sl][:, 0:N], ALU.mult, [rGA[0], rACB[sl]], [rHT[sl]])

        def down(j):
            sl = wdc[0] % 2
            wdc[0] += 1
            hs = j % 2
            DMA("sp", WD[sl][:], wdn_s[:, j, :], [rWD], [rWDs[sl]], rWDs[sl])
            for s in range(nsub):
                for n in range(2):
                    ab = 2 + s * 2 + n
                    MM(PS[ab][:Pn, :], HT[hs][:, s * Pn:(s + 1) * Pn], WD[sl][:, n * 512:(n + 1) * 512],
                       j == 0, j == NPAIR - 1, [rHT[hs], rWDs[sl]], psr(ab))

        up(0)
        up(1)
        e1(0)
        for j in range(NPAIR):
            if j + 2 < NPAIR:
                up(j + 2)
            if j + 1 < NPAIR:
                e1(j + 1)
            e2(j)
            down(j)
        for s in range(nsub):
            for n in range(2):
                ab = 2 + s * 2 + n
                TT("dve", XB[:Pn, s, n * 512:(n + 1) * 512], PS[ab][:Pn, :], XB[:Pn, s, n * 512:(n + 1) * 512],
                   ALU.add, psr(ab) + [rXB[s]], [rXB[s]])
            rs, rr = rstd_from(Pn, XB[:Pn, s, :], [rXB[s]], D, XN[:Pn, :], rXN)
            STT(XB[:Pn, s, :], XB[:Pn, s, :], rs, GFIN[:Pn, :], ALU.mult, ALU.mult, [rXB[s], rr, rC], [rXB[s]])
            DMA("sp", out_rows(s), XB[:Pn, s, :], [rXB[s]], (), rXB[s])

    def ffn(N, nseg, L, nsub, Pn, uh_prev, uh_next, out_rows, only_halo=False):
        Wd = 2 + L
        xr = [rXN2T[s] for s in range(nsub)]
        up_banks = [(0, 1), (6, 7)]
        wus = {}

        def up(j):
            sl = wuc[0] % 2
            wuc[0] += 1
            wus[j] = sl
            DMA("sp", WU[sl][:], wup_s[:, :, j * 256:(j + 1) * 256], [rWU], [rWUs[sl]], rWUs[sl])
            (ba, bb) = up_banks[j % 2]
            for k in range(8):
                MM(PS[ba][:, 0:N], WU[sl][:, k, 0:128], XN2T[:, k, 0:N], k == 0, k == 7, [rWUs[sl]] + xr, psr(ba, 0, N))
            for k in range(8):
                MM(PS[bb][:, 0:N], WU[sl][:, k, 128:256], XN2T[:, k, 0:N], k == 0, k == 7, [rWUs[sl]] + xr,
                   psr(bb, 0, N))

        def elem(j):
            (ba, bb) = up_banks[j % 2]
            for (isb, bank, U_, rU, AC, rAC) in [(0, ba, UA[0], rUA[0], ACA[0], rACA[0]),
                                                 (1, bb, UBf[0], rUB[0], ACB[0], rACB[0])]:
                ch = j + 22 * isb
                uv = U_[:, 0:nseg * Wd].rearrange("p (g w) -> p g w", g=nseg)
                CP("act", uv[:, :, 2:2 + L], PS[bank][:, 0:N].rearrange("p (g l) -> p g l", g=nseg),
                   psr(bank, 0, N), [rU])
                CP("pool", uv[:, :, 0:2], UH[uh_prev][:, ch, 0:nseg, :], [rUH[uh_prev]], [rU])
                CP("pool", UH[uh_next][:, ch, 0:nseg, :], uv[:, :, L:L + 2], [rU], [rUH[uh_next]])
                if only_halo:
                    continue
                av = AC[:, 0:N].rearrange("p (g l) -> p g l", g=nseg)
                wc = lambda k: CST[:, C_FCW + ch * 3 + k:C_FCW + ch * 3 + k + 1]
                TS("dve", av, uv[:, :, 2:2 + L], wc(2), CST[:, C_FCB + ch:C_FCB + ch + 1], ALU.mult, ALU.add,
                   [rU, rC], [rAC])
                STT(av, uv[:, :, 1:1 + L], wc(1), av, ALU.mult, ALU.add, [rU, rC, rAC], [rAC])
                STT(av, uv[:, :, 0:L], wc(0), av, ALU.mult, ALU.add, [rU, rC, rAC], [rAC])
            if only_halo:
                return
            ACT(GA[0][:, 0:N], ACA[0][:, 0:N], AF.Gelu_apprx_tanh, [rACA[0]], [rGA[0]])
            hs = j % 2
            TT("pool", HT[hs][:, 0:N], GA[0][:, 0:N], ACB[0][:, 0:N], ALU.mult, [rGA[0], rACB[0]], [rHT[hs]])

        def down(j):
            sl = wdc[0] % 2
            wdc[0] += 1
            hs = j % 2
            DMA("sp", WD[sl][:], wdn_s[:, j, :], [rWD], [rWDs[sl]], rWDs[sl])
            for s in range(nsub):
                for n in range(2):
                    ab = 2 + s * 2 + n
                    MM(PS[ab][:Pn, :], HT[hs][:, s * Pn:(s + 1) * Pn], WD[sl][:, n * 512:(n + 1) * 512],
                       j == 0, j == NPAIR - 1, [rHT[hs], rWDs[sl]], psr(ab))

        if only_halo:
            for j in range(NPAIR):
                up(j)
                elem(j)
            return
        up(0)
        for j in range(NPAIR):
            if j + 1 < NPAIR:
                up(j + 1)
            elem(j)
            down(j)
        for s in range(nsub):
            for n in range(2):
                ab = 2 + s * 2 + n
                TT("dve", XB[:Pn, s, n * 512:(n + 1) * 512], PS[ab][:Pn, :], XB[:Pn, s, n * 512:(n + 1) * 512],
                   ALU.add, psr(ab) + [rXB[s]], [rXB[s]])
            rs, rr = rstd_from(Pn, XB[:Pn, s, :], [rXB[s]], D, XN[:Pn, :], rXN)
            STT(XB[:Pn, s, :], XB[:Pn, s, :], rs, GFIN[:Pn, :], ALU.mult, ALU.mult, [rXB[s], rr, rC], [rXB[s]])
            DMA("sp", out_rows(s), XB[:Pn, s, :], [rXB[s]], (), rXB[s])

    try:
        _ck('setup')
        def xrows(i):
            return xp[i * 128:(i + 1) * 128, :]

        def kvdst(i):
            return (KTC[:, :, i * 128:(i + 1) * 128], KTR[0:32, i * 128:(i + 1) * 128], V[:, i, :], [KVR[i]])

        def tab(s, Pn=128):
            return (TAB[:Pn, s, 0:16], TAB[:Pn, s, 16:32], [rTAB[s]])

        def pre_stage(tiles):
            for s, f in enumerate(tiles):
                stage_a(xrows(f), 128, s, tabd[f], use_tmp=True)

        def pre_rg(par, other=False):
            if other:
                rg_branch(256, 1, 256, False, [0, 1], par=par, halo_gate=True)
            else:
                rg_branch(256, 1, 256, True, [0, 1], par=par)

        def pre_pair(tiles, par, other=False):
            for s, f in enumerate(tiles):
                stage_a(xrows(f), 128, s, tabd[f], use_tmp=True)
            if other:
                rg_branch(256, 1, 256, False, [0, 1], par=par, halo_gate=True)
            else:
                rg_branch(256, 1, 256, True, [0, 1], par=par)

        def kside_pair(tiles):
            for s, f in enumerate(tiles):
                c_, s_, tr_ = tab(s)
                kv_side(128, s, c_, s_, tr_, False, kvdst(f))

        def front(f, s, qi, oi):
            c_, s_, tr_ = tab(s)
            kv_side(128, s, c_, s_, tr_, True, kvdst(f),
                    pckv_d[oi * 128:(oi + 1) * 128, :], pkr_d[oi * 128:(oi + 1) * 128, :])
            q_path(128, s, c_, s_, tr_, qi)

        def xload(tiles):
            for s, f in enumerate(tiles):
                DMA("sp", XB[:, s, :], xrows(f), (), [rXB[s]], rXB[s])

        def back(s, qi, par):
            tok = slice(s * 128, (s + 1) * 128)
            o_mla_and_out(128, OT2[qi], rOT2[qi], XB[:, s, :], rXB[s], CATM[:, :, tok], rCAT[s],
                          CATR[par][:, :, tok], rCATR[par], XN2T[:, :, 2 + s * 128:2 + (s + 1) * 128], [rXN2T[s]])

        rPTH = res("pth")
        rRCPH = res("rcph")
        rONH = res("onh")
        rOTH = res("oth")
        rCATMH = res("catmh")
        rXH2 = res("xh2")

        def halo(f):
            stage_a(xp[f * 128 + 126:f * 128 + 128, :], 2, 0, tabd[f][126:128, :], use_tmp=True)
            c_, s_, tr_ = tab(0, 2)
            kv_side(2, 0, c_, s_, tr_, True, None)
            q_path(2, 0, c_, s_, tr_, "h")
            groups = [[0, 1]] + [list(range(a, min(a + 4, f + 1))) for a in range(2, f + 1, 4)]
            for gi, grp_ in enumerate(groups):
                for j, kt in enumerate(grp_):
                    sv = PS[6][:, 16 * j:16 * j + 16]
                    MM(sv, KTC[:, 0, kt * 128:(kt + 1) * 128], QTCH[:, 0, :, :], True, False, [KVR[kt], rQTH], psr(6))
                    MM(sv, KTC[:, 1, kt * 128:(kt + 1) * 128], QTCH[:, 1, :, :], False, False, [KVR[kt], rQTH], psr(6))
                    MM(sv, KTR[:, kt * 128:(kt + 1) * 128], QTRH[:, :, :], False, True, [KVR[kt], rQTH], psr(6))
                ncol = 16 * len(grp_)
                if gi == 0:
                    ACT(PTH[:, 0:ncol], PS[6][:, 0:ncol], AF.Exp, psr(6) + [rC], [rPTH], scale=SCALE,
                        bias=CST[:, C_KB:C_KB + 1])
                else:
                    ACT(PTH[:, 0:ncol], PS[6][:, 0:ncol], AF.Exp, psr(6), [rPTH], scale=SCALE, bias=0.0)
                for j, kt in enumerate(grp_):
                    MM(PS[7][:16, 0:257], PTH[:, 16 * j:16 * j + 16], V[:, kt, :], kt == 0, kt == f,
                       [rPTH, KVR[kt]], psr(7))
            TS("dve", RCPH[:16, 0:1], PS[7][:16, 256:257], 1e-30, None, ALU.add, None, psr(7), [rRCPH])
            P.add("dve", (lambda: nc.vector.reciprocal(out=RCPH[:16, 0:1], in_=RCPH[:16, 0:1])), [rRCPH], [rRCPH])
            TS("dve", ONH[:16, :], PS[7][:16, 0:256], RCPH[:16, 0:1], None, ALU.mult, None, psr(7) + [rRCPH], [rONH])
            pb = psb(6)
            for c in range(2):
                TR(pb[:, c * 16:(c + 1) * 16], ONH[:16, c * 128:(c + 1) * 128], IDB[:16, :16], [rONH, rW], psr(6))
            CP("act", OTH[:, :, :, :], pb[:, 0:32].rearrange("p (c h q) -> p c h q", c=2, h=8), psr(6), [rOTH])
            o_mla_and_out(2, OTH, rOTH, XTMP[:2, :], rXTMP, CATMH[:, :, :], rCATMH, CATRH[:, :, :], rCATRH,
                          XH2[:, :, :], [rXH2])

        def catrh_from(par):
            CP("pool", CATRH[:, :, :], CATR[par][:, :, 254:256], [rCATR[par]], [rCATRH])

        pre_pair((0, 1), 1, other=True)
        kside_pair((0, 1))
        TS("pool", HST[:, :, 0:1], HST[:, :, 0:1], CST[:, C_FLAG:C_FLAG + 1], None, ALU.mult, None,
           rHSTc + [rC], rHSTc)
        halo(1)
        TS("pool", XN2T[:, :, 0:2], XH2[:, :, :], CST[:, C_FLAG:C_FLAG + 1], None, ALU.mult, None,
           [rXH2, rC], [rXH])
        pre_pair((2, 3), 0)
        xload((2, 3))
        front(2, 0, 0, 0)
        _ck('prefix')
        for k in range(16):
            f0, f1, g0, g1 = 2 + 4 * k, 3 + 4 * k, 4 + 4 * k, 5 + 4 * k
            par, parn = k % 2, (k + 1) % 2
            last = (k == 15)

            def l0a():
                front(f1, 1, 1, 2 * k + 1)

            def l0b():
                pre_pair((g0, g1), parn, other=True)
                if not last:
                    kside_pair((g0, g1))
            la = P.record(l0a)
            ida = set(id(t) for t in la)
            lst = la + P.record(l0b)
            attention_prompt(f0, 0, lst, drain=False)
            while any(id(t) in ida for t in lst):
                P.replay_sched(lst, 10)

            def l1():
                back(0, 0, par)
                if not last:
                    halo(g1)
                    pre_stage((f0 + 4, f1 + 4))
                    front(f0 + 4, 0, 0, 2 * k + 2)
                    pre_rg(parn)
            lst = lst + P.record(l1)
            attention_prompt(f1, 1, lst, drain=True)
            back(1, 1, par)
            base = 2 * k * 128
            ffn_prompt(2, lambda s: y_d[base + s * 128:base + (s + 1) * 128, :], last)
            if not last:
                CP("pool", XN2T[:, :, 0:2], XH2[:, :, :], [rXH2], [rXH])
                xload((f0 + 4, f1 + 4))
            _ck('own1')
        _ck('own')

        DMA("sp", prh_d, HST[:, :, 0], rHSTc, (), res("o_prh"), slow=True)
        DMA("sp", prc_d, RX[:, :, 0:3], rRXc, (), res("o_prc"), slow=True)
        DMA("sp", pfc_d, UH[0][:, :, 0, :], [rUH[0]], (), res("o_pfc"), slow=True)
        uh_cur = 0

        DMA("sp", HST[:, :, :], srh_d, (), rHSTc, res("l_srh"), slow=True)
        rxs = RX[:, :, 0:4 * 19].rearrange("p c (g w) -> p c g w", g=4)
        for c in range(4):
            DMA("sp", rxs[:, c, :, 0:3], src_d[:, c, :, :], (), [rRXc[c]], res("l_src%d" % c), slow=True)
        DMA("sp", UH[uh_cur][:, :, :, :], sfc_d, (), [rUH[uh_cur]], res("l_sfc"), slow=True)
        stg = [(XTMP[:, :].rearrange("p (t d) -> p t d", t=4), rXTMP),
               (XB[:, 0, :].rearrange("p (t d) -> p t d", t=4), rXB[0]),
               (XB[:, 1, :].rearrange("p (t d) -> p t d", t=4), rXB[1])]
        krs = [(T1[:, 0:4, :], rT1), (T2[:, 0:4, :], rT2)]
        rKRB4 = [res("krb4_0"), res("krb4_1")]
        cbanks = [4, 5, 6, 7]
        for g4 in range(NTILE // 4):
            (sv, sr) = stg[g4 % 3]
            (kv_, kr_) = krs[g4 % 2]
            kl = g4 % 2
            DMA("sp", sv, cckv_d[4 * g4:4 * g4 + 4].rearrange("t p d -> p t d"), (), [sr], sr)
            DMA("sp", kv_, ckr_d[4 * g4:4 * g4 + 4].rearrange("t p e -> p t e"), (), [kr_], kr_)
            CP("pool", KRB4[:, kl, :, :], kv_, [kr_], [rKRB4[kl]])
            for j in range(4):
                kt = 4 * g4 + j
                bnk = cbanks[kt % 4]
                CP("pool" if kt % 2 == 0 else "dve", V[:, kt, 0:NKV], sv[:, j, :], [sr], [KVR[kt]])
                pb = psb(bnk)
                for c in range(2):
                    TR(pb[:, c * 128:(c + 1) * 128], V[:, kt, c * 128:(c + 1) * 128], IDB[:, :], [KVR[kt], rW],
                       psr(bnk))
                TR(pb[0:32, 256:384], KRB4[:, kl, j, :], IDB[:, :], [rKRB4[kl], rW], psr(bnk))
                CP("act" if kt % 2 == 0 else "dve", KTC[:, :, kt * 128:(kt + 1) * 128],
                   pb[:, 0:256].rearrange("p (c t) -> p c t", c=2), psr(bnk), [KVR[kt]])
                CP("act", KTR[0:32, kt * 128:(kt + 1) * 128], pb[0:32, 256:384], psr(bnk), [KVR[kt]])

        stage_a(xs_d, 64, 0, None)
        rg_branch(64, 4, 16, True, [0], par=0)
        kv_side(64, 0, COSS[:, :], SINS[:, :], [rC], True, (KTCN[:, :, :], KTRN[0:32, :], VN[:, :], [rVN]),
                sckv_d, skr_d)
        q_path(64, 0, COSS[:, :], SINS[:, :], [rC])
        for sq in range(4):
            kts = []
            for j in range(16):
                kt = sq * 16 + j
                kc = (lambda c, kt=kt: KTC[:, c, kt * 128:(kt + 1) * 128])
                kts.append((kc, KTR[:, kt * 128:(kt + 1) * 128], V[:, kt, :], 128, [KVR[kt]], 0.0, False))
            kts.append(((lambda c: KTCN[:, c, :]), KTRN[:, :], VN[:, :], 64, [rVN],
                        CST[0:64, C_SMASK + sq:C_SMASK + sq + 1], False))
            qc = (lambda c, sq=sq: QTC2[0][:, c, :, sq * 16:(sq + 1) * 16])
            qr = QTR2[0][:, :, sq * 16:(sq + 1) * 16]
            attn_group(qc, qr, 128, kts, [(2 + sq, 0, 128)], [ON[:, sq, :]], rQT2[0])
        for sq in range(4):
            pb = psb(6)
            for c in range(2):
                TR(pb[:, c * 128:(c + 1) * 128], ON[:, sq, c * 128:(c + 1) * 128], IDB[:, :], [rON, rW], psr(6))
            CP("act", OT2[0][:, :, :, sq * 16:(sq + 1) * 16],
               pb[:, 0:256].rearrange("p (c h q) -> p c h q", c=2, h=8), psr(6), [rOT2[0]])
        o_mla_and_out(64, OT2[0], rOT2[0], XB[:64, 0, :], rXB[0], CATM[:, :, 0:64], rCAT[0],
                      CATR[0][:, :, 0:64], rCATR[0], XN2T[:, :, 0:64], [rXN2T[0], rXH])
        ffn(64, 4, 16, 1, 64, uh_cur, 1 - uh_cur, lambda s: ys_d)
        uh_cur = 1 - uh_cur
        DMA("sp", srh_o, HST[:, :, :], rHSTc, (), res("o_srh"), slow=True)
        for c in range(4):
            DMA("sp", src_o[:, c, :, :], rxs[:, c, :, 0:3], [rRXc[c]], (), res("o_src%d" % c), slow=True)
        DMA("sp", sfc_o, UH[uh_cur][:, :, :, :], [rUH[uh_cur]], (), res("o_sfc"), slow=True)
    except _Stop:
        pass

    P.emit()
    return nc, es, P


_CACHE = {}


def _prep_inputs(inp):
    f32 = np.float32
    g = lambda k: np.asarray(inp[k], dtype=f32)
    x_prompt = g("x_prompt")
    x_sample = g("x_sample")
    cache_ckv = g("cache_ckv")[0]
    cache_krope = g("cache_krope")[0]
    st_h = g("state_rg_h")[0]
    st_c = g("state_rg_conv")[0]
    st_f = g("state_ffn_conv")[0]
    w_in = g("w_in")[0]
    w_uq = g("w_uq")[0]
    w_uk = g("w_uk")[0]
    w_uv = g("w_uv")[0]
    w_out = g("w_out")[0]
    w_up = g("w_ffn_up")[0]
    w_dn = g("w_ffn_down")[0]

    def pm(a, k):
        return np.ascontiguousarray(a.reshape(k, 128, -1).transpose(1, 0, 2))

    def fm(v, k):
        return np.ascontiguousarray(v.reshape(k, 128).T)

    shared = {}
    shared["w_in"] = pm(w_in, 8)
    uq = np.concatenate([w_uq[:, :, :64].reshape(NQ, 512), w_uq[:, :, 64:].reshape(NQ, 256)], axis=1)
    shared["w_uq"] = pm(uq, 3)
    shared["w_out"] = pm(w_out, 8)
    ukt = np.zeros((128, 8, 256), f32)
    for h in range(8):
        ukt[(h % 2) * 64:(h % 2) * 64 + 64, h, :] = w_uk[:, h, :].T
    shared["w_ukt"] = ukt
    shared["w_uv"] = pm(w_uv.reshape(256, 512), 2)
    for nm, key in [("w_rga", "w_rg_a"), ("w_rgi", "w_rg_i")]:
        w = g(key)[0]
        bd = np.zeros((128, 4, 128), f32)
        for c in range(4):
            for b in range(2):
                bd[b * 64:(b + 1) * 64, c, b * 64:(b + 1) * 64] = w[2 * c + b]
        shared[nm] = bd
    upi = np.concatenate([w_up[:, :FF].reshape(D, NPAIR, 128), w_up[:, FF:].reshape(D, NPAIR, 128)], axis=2)
    shared["w_up"] = pm(upi.reshape(D, 2 * FF), 8)
    shared["w_dn"] = pm(w_dn, NPAIR)
    shared["gfin"] = np.ascontiguousarray(np.broadcast_to(g("final_norm_g")[None, :], (128, D)))
    shared["gkv"] = np.ascontiguousarray(np.broadcast_to(g("kv_norm_g")[0][None, :], (128, NKV)))
    shared["ident"] = np.eye(128, dtype=f32)

    cst = np.zeros((128, C_NCOL), f32)
    rgw = g("w_rg_conv")[0]
    cst[:, C_RGW:C_RGW + 16] = rgw.reshape(4, 4, 128).transpose(2, 1, 0).reshape(128, 16)
    cst[:, C_RGB:C_RGB + 4] = fm(g("b_rg_conv")[0], 4)
    cst[:, C_BA:C_BA + 4] = fm(g("b_rg_a")[0], 4)
    cst[:, C_BI:C_BI + 4] = fm(g("b_rg_i")[0], 4)
    cst[:, C_LAM:C_LAM + 4] = fm(g("rg_lambda")[0], 4)
    fcw = g("w_ffn_conv")[0]
    cst[:, C_FCW:C_FCW + 132] = fcw.reshape(3, 44, 128).transpose(2, 1, 0).reshape(128, 132)
    cst[:, C_FCB:C_FCB + 44] = fm(g("b_ffn_conv")[0], 44)
    cst[:, C_GMIX:C_GMIX + 8] = fm(g("norm_mix_g")[0], 8)
    cst[:, C_GFFN:C_GFFN + 8] = fm(g("norm_ffn_g")[0], 8)
    cst[:, C_GQ:C_GQ + 3] = fm(g("q_norm_g")[0], 3)
    for sq in range(4):
        cst[:, C_SMASK + sq] = NEGB
        cst[sq * 16:(sq + 1) * 16, C_SMASK + sq] = 0.0

    half_ = NRO // 2
    inv = (10000.0 ** (-np.arange(half_, dtype=f32) / half_)).astype(f32)

    def tables(pos):
        ang = pos.astype(f32)[:, None] * inv[None, :]
        return np.cos(ang).astype(f32), np.sin(ang).astype(f32)

    pos_s = (2048 + (np.arange(64) % 16)).astype(np.int64)
    cs_s, sn_s = tables(pos_s)

    in_maps = []
    for core in range(8):
        b, half = core // 2, core % 2
        m = dict(shared)
        zpad = np.zeros((256, D), f32)
        if half == 1:
            m["xp"] = np.concatenate([x_prompt[b], zpad], axis=0)
            pos = np.arange(NF * 128, dtype=np.int64)
        else:
            m["xp"] = np.concatenate([zpad, x_prompt[b]], axis=0)
            pos = np.arange(NF * 128, dtype=np.int64) - 256
        cs, sn = tables(np.abs(pos))
        sn = sn * np.sign(pos)[:, None].astype(f32)
        m["tabd"] = np.ascontiguousarray(np.concatenate([cs, sn], axis=1).reshape(NF, 128, 32))
        m["cossd"] = cs_s
        m["sinsd"] = sn_s
        c2 = cst.copy()
        c2[:, C_KB] = 0.0 if half == 1 else NEGB
        c2[:, C_FLAG] = float(half)
        m["cst"] = c2
        sl = slice(4 * core, 4 * core + 4)
        m["xs"] = np.ascontiguousarray(x_sample[sl].reshape(64, D))
        m["cckv"] = np.ascontiguousarray(cache_ckv[sl].reshape(NTILE, 128, NKV))
        m["ckr"] = np.ascontiguousarray(cache_krope[sl].reshape(NTILE, 128, NRO))
        m["srh"] = np.ascontiguousarray(st_h[sl].reshape(4, 4, 128).transpose(2, 1, 0))
        m["src"] = np.ascontiguousarray(st_c[sl].reshape(4, 3, 4, 128).transpose(3, 2, 0, 1))
        m["sfc"] = np.ascontiguousarray(st_f[sl].reshape(4, 2, 44, 128).transpose(3, 2, 0, 1))
        in_maps.append(m)
    return in_maps


def kernel(**inputs):
    if "prog" not in _CACHE:
        _CACHE["prog"] = build_program()
    nc, es, P = _CACHE["prog"]
    in_maps = _prep_inputs(inputs)
    res = run_bass_kernel_spmd(nc, in_maps, core_ids=list(range(8)))
    r = res.results
    f32 = np.float32
    B = 4
    y_prompt = np.zeros((B, SEQ, D), f32)
    p_ckv = np.zeros((1, B, SEQ, NKV), f32)
    p_krope = np.zeros((1, B, SEQ, NRO), f32)
    p_rg_h = np.zeros((1, B, RGW), f32)
    p_rg_conv = np.zeros((1, B, 3, RGW), f32)
    p_ffn_conv = np.zeros((1, B, 2, 2 * FF), f32)
    y_sample = np.zeros((32, 16, D), f32)
    s_ckv = np.zeros((1, 32, 16, NKV), f32)
    s_krope = np.zeros((1, 32, 16, NRO), f32)
    s_rg_h = np.zeros((1, 32, RGW), f32)
    s_rg_conv = np.zeros((1, 32, 3, RGW), f32)
    s_ffn_conv = np.zeros((1, 32, 2, 2 * FF), f32)
    for core in range(8):
        b, half = core // 2, core % 2
        o = r[core]
        for oi in range(32):
            st = 4 * (oi // 2) + 2 * half + (oi % 2)
            ts = slice(st * 128, (st + 1) * 128)
            os_ = slice(oi * 128, (oi + 1) * 128)
            y_prompt[b, ts] = o["y"][os_]
            p_ckv[0, b, ts] = o["pckv"][os_]
            p_krope[0, b, ts] = o["pkr"][os_]
        if half == 0:
            p_rg_h[0, b] = o["prh"].T.reshape(RGW)
            p_rg_conv[0, b] = o["prc"].transpose(2, 1, 0).reshape(3, RGW)
        else:
            p_ffn_conv[0, b] = o["pfc"].transpose(2, 1, 0).reshape(2, 2 * FF)
        sl = slice(4 * core, 4 * core + 4)
        y_sample[sl] = o["ys"].reshape(4, 16, D)
        s_ckv[0, sl] = o["sckv"].reshape(4, 16, NKV)
        s_krope[0, sl] = o["skr"].reshape(4, 16, NRO)
        s_rg_h[0, sl] = o["srho"].transpose(2, 1, 0).reshape(4, RGW)
        s_rg_conv[0, sl] = o["srco"].transpose(2, 3, 1, 0).reshape(4, 3, RGW)
        s_ffn_conv[0, sl] = o["sfco"].transpose(2, 3, 1, 0).reshape(4, 2, 2 * FF)
    return (y_prompt, y_sample, p_ckv, p_krope, p_rg_h, p_rg_conv, p_ffn_conv,
            s_ckv, s_krope, s_rg_h, s_rg_conv, s_ffn_conv)
```

```python
import numpy as np
from contextlib import ExitStack
import concourse.bass as bass
import concourse.mybir as mybir
from concourse.bass_utils import run_bass_kernel_spmd

F32 = mybir.dt.float32
BF16 = mybir.dt.bfloat16
AF = mybir.ActivationFunctionType
ALU = mybir.AluOpType

D = 1024
NQ = 384
NKV = 256
NRO = 32
RGW = 512
FF = 2816
NH = 8
INC = 1696
SEQ = 8192
HALF = 4096
NTILE = 64
NF = 66
EPS = 1e-6
SCALE = 96.0 ** -0.5
NEGB = -30000.0
NPAIR = 22

C_RGW = 0
C_RGB = 16
C_BA = 20
C_BI = 24
C_LAM = 28
C_FCW = 32
C_FCB = 164
C_GMIX = 208
C_GFFN = 216
C_GQ = 224
C_KB = 227
C_FLAG = 228
C_SMASK = 229
C_NCOL = 240


class Res:
    __slots__ = ("name", "w", "r", "sem", "cnt", "excl", "multi")

    def __init__(self, name, excl=False, multi=False):
        self.name = name
        self.excl = excl
        self.multi = multi
        self.w = []
        self.r = {}
        self.sem = None
        self.cnt = 0


class Ev:
    __slots__ = ("kind", "eng", "idx", "res", "val", "grp")

    def __init__(self, kind, eng, idx, res=None, val=None, grp=None):
        self.kind = kind
        self.eng = eng
        self.idx = idx
        self.res = res
        self.val = val
        self.grp = grp


class Prog:
    ENG = ["pe", "act", "dve", "pool", "sp"]

    def __init__(self, nc, es):
        self.nc = nc
        self.es = es
        self.ops = []
        self.anchors = []
        self.rec = None
        self.engobj = dict(pe=nc.tensor, act=nc.scalar, dve=nc.vector, pool=nc.gpsimd, sp=nc.sync)

    def record(self, f):
        self.rec = []
        f()
        r = self.rec
        self.rec = None
        return r

    def replay(self, lst, n):
        while n > 0 and lst:
            self.add(*lst.pop(0))
            n -= 1

    def replay_step(self, lst, max_ops=12):
        tw = {}
        tr = {}
        n = 0
        while lst and n < max_ops:
            (eng, fn, reads, writes, dma, grp) = lst[0]
            conflict = False
            for w in writes:
                for d in (tw, tr):
                    e = d.get(id(w))
                    if e and (e - {eng}):
                        conflict = True
            for r in reads:
                e = tw.get(id(r))
                if e and (e - {eng}):
                    conflict = True
                if r.excl:
                    e = tr.get(id(r))
                    if e and (e - {eng}):
                        conflict = True
            if conflict and n > 0:
                break
            self.add(*lst.pop(0))
            n += 1
            for w in writes:
                tw.setdefault(id(w), set()).add(eng)
            for r in reads:
                tr.setdefault(id(r), set()).add(eng)

    def replay_sched(self, lst, max_ops=10, window=400):
        tw = {}
        tr = {}
        pw = set()
        pr = set()
        n = 0
        i = 0
        scanned = 0
        while i < len(lst) and n < max_ops and scanned < window:
            (eng, fn, reads, writes, dma, grp) = lst[i]
            scanned += 1
            wid = [id(w) for w in writes] + [id(r) for r in reads if r.excl]
            rid = [id(r) for r in reads if not r.excl]
            ready = True
            for x in wid:
                if x in pw or x in pr:
                    ready = False
                    break
            if ready:
                for x in rid:
                    if x in pw:
                        ready = False
                        break
            if ready:
                for x in wid:
                    e = tw.get(x)
                    if e and (e - {eng}):
                        ready = False
                    e = tr.get(x)
                    if e and (e - {eng}):
                        ready = False
                for x in rid:
                    e = tw.get(x)
                    if e and (e - {eng}):
                        ready = False
            if ready:
                self.add(*lst.pop(i))
                n += 1
                for x in wid:
                    tw.setdefault(x, set()).add(eng)
                for x in rid:
                    tr.setdefault(x, set()).add(eng)
            else:
                pw.update(wid)
                pr.update(rid)
                i += 1

    def add(self, eng, fn, reads=(), writes=(), dma=None, grp=None):
        if self.rec is not None:
            self.rec.append((eng, fn, reads, writes, dma, grp))
            return None
        deps = []
        ex = [r for r in reads if r.excl and r not in writes]
        if ex:
            reads = [r for r in reads if not r.excl]
            writes = list(writes) + ex
        for r in reads:
            deps += r.w
        for w in writes:
            ds = (list(w.r.values()) if w.multi else w.w + list(w.r.values()))
            if w.excl:
                ds = [d for d in ds if d.eng != eng]
            deps += ds
        idx = len(self.ops)
        if dma is not None:
            if dma.sem is None:
                dma.sem = self.es.enter_context(self.nc.semaphore("d_" + dma.name))
                self.anchors.append(dma)
            dma.cnt += 16
            ev = Ev("d", eng, idx, dma, dma.cnt, grp)
            if grp is not None:
                grp.append(ev)
        else:
            ev = Ev("e", eng, idx)
        dd = []
        for d in deps:
            if d.kind == "e" and d.eng == "pe" and eng == "pe":
                continue
            if grp is not None and d.grp is grp:
                continue
            dd.append(d)
        self.ops.append((eng, fn, dd, ev))
        for w in writes:
            if w.multi:
                w.w = w.w + [ev]
            else:
                w.w = [ev]
                w.r = {}
        for r in reads:
            if r in writes:
                continue
            key = eng if ev.kind == "e" else ("d", id(dma))
            r.r[key] = ev
        return ev

    def close_group(self, grp):
        if not grp:
            return
        tot = grp[0].res.cnt
        for ev in grp:
            ev.val = tot

    def emit(self):
        nc = self.nc
        esem = {e: self.es.enter_context(nc.semaphore("e_" + e)) for e in self.ENG}
        need = set()
        for (eng, fn, dd, ev) in self.ops:
            for d in dd:
                if d.kind == "e":
                    need.add(d.idx)
        ms = {}
        cnt = {e: 0 for e in self.ENG}
        for i, (eng, fn, dd, ev) in enumerate(self.ops):
            if i in need:
                cnt[eng] += 1
                ms[i] = cnt[eng]
        seen = {e: {} for e in self.ENG}
        for i, (eng, fn, dd, ev) in enumerate(self.ops):
            E = self.engobj[eng]
            waits = {}
            for d in dd:
                if d.kind == "e":
                    key = d.eng
                    val = ms[d.idx]
                    sem = esem[d.eng]
                else:
                    key = ("d", id(d.res))
                    val = d.val
                    sem = d.res.sem
                if seen[eng].get(key, 0) >= val:
                    continue
                if key not in waits or waits[key][1] < val:
                    waits[key] = (sem, val)
            for key, (sem, val) in waits.items():
                seen[eng][key] = val
                E.wait_ge(sem, val)
            ins = fn()
            if i in need:
                ins.then_inc(esem[eng], 1)
            if ev.kind == "d":
                ins.then_inc(ev.res.sem, 16)
        for a in self.anchors:
            nc.sync.wait_ge(a.sem, a.cnt)
        self.stats = dict(cnt)
        self.stats["nops"] = len(self.ops)


class _Stop(Exception):
    pass


_STOP = [None]


def _ck(name):
    if _STOP[0] == name:
        raise _Stop()


def build_program():
    nc = bass.Bass("TRN2", target_bir_lowering=False)
    es = ExitStack()
    P = Prog(nc, es)

    def din(name, shape, dt=F32):
        return nc.dram_tensor(name, list(shape), dt, kind="ExternalInput").ap()

    def dout(name, shape, dt=F32):
        return nc.dram_tensor(name, list(shape), dt, kind="ExternalOutput").ap()

    xp = din("xp", [NF * 128, D])
    tabd = din("tabd", [NF, 128, 32])
    cossd = din("cossd", [64, 16])
    sinsd = din("sinsd", [64, 16])
    cst = din("cst", [128, C_NCOL])
    gfin_d = din("gfin", [128, D])
    gkv_d = din("gkv", [128, NKV])
    ident_d = din("ident", [128, 128])
    w_in_d = din("w_in", [128, 8, INC])
    w_uq_d = din("w_uq", [128, 3, 768])
    w_out_d = din("w_out", [128, 8, D])
    w_ukt_d = din("w_ukt", [128, 8, 256])
    w_uv_d = din("w_uv", [128, 2, 512])
    w_rga_d = din("w_rga", [128, 4, 128])
    w_rgi_d = din("w_rgi", [128, 4, 128])
    w_up_d = din("w_up", [128, 8, 2 * FF])
    w_dn_d = din("w_dn", [128, NPAIR, D])
    xs_d = din("xs", [64, D])
    cckv_d = din("cckv", [NTILE, 128, NKV])
    ckr_d = din("ckr", [NTILE, 128, NRO])
    srh_d = din("srh", [128, 4, 4])
    src_d = din("src", [128, 4, 4, 3])
    sfc_d = din("sfc", [128, 44, 4, 2])

    y_d = dout("y", [HALF, D])
    pckv_d = dout("pckv", [HALF, NKV])
    pkr_d = dout("pkr", [HALF, NRO])
    prh_d = dout("prh", [128, 4])
    prc_d = dout("prc", [128, 4, 3])
    pfc_d = dout("pfc", [128, 44, 2])
    ys_d = dout("ys", [64, D])
    sckv_d = dout("sckv", [64, NKV])
    skr_d = dout("skr", [64, NRO])
    srh_o = dout("srho", [128, 4, 4])
    src_o = dout("srco", [128, 4, 4, 3])
    sfc_o = dout("sfco", [128, 44, 4, 2])

    wup_s = nc.dram_tensor("wup_s", [128, NPAIR, 8, 256], BF16, kind="Internal").ap()
    wdn_s = nc.dram_tensor("wdn_s", [128, NPAIR, D], BF16, kind="Internal").ap()
    wrg_s = nc.dram_tensor("wrg_s", [128, 4, 8, 256], BF16, kind="Internal").ap()
    wout_s = nc.dram_tensor("wout_s", [128, 8, D], BF16, kind="Internal").ap()

    def sb(name, shape, dt):
        return es.enter_context(nc.sbuf_tensor(name, list(shape), dt))

    W_IN = sb("W_IN", [128, 8, 672], BF16)
    W_UQ = sb("W_UQ", [128, 3, 768], BF16)
    W_UKT = sb("W_UKT", [128, 8, 256], BF16)
    W_UV = sb("W_UV", [128, 2, 512], BF16)
    W_RGA = sb("W_RGA", [128, 4, 128], BF16)
    W_RGI = sb("W_RGI", [128, 4, 128], BF16)
    IDB = sb("IDB", [128, 128], BF16)
    GFIN = sb("GFIN", [128, D], F32)
    GKV = sb("GKV", [128, NKV], F32)
    TAB = sb("TAB", [128, 2, 32], F32)
    COSS = sb("COSS", [64, 16], F32)
    SINS = sb("SINS", [64, 16], F32)
    CST = sb("CST", [128, C_NCOL], F32)
    EPSB = sb("EPSB", [128, 1], F32)
    ONEB = sb("ONEB", [128, 1], F32)
    CL = sb("CL", [128, 8], F32)
    NB = sb("NB", [128, 8], F32)
    KTC = sb("KTC", [128, 2, NF * 128], BF16)
    KTR = sb("KTR", [128, NF * 128], BF16)
    V = sb("V", [128, NF, 257], BF16)
    KTCN = sb("KTCN", [128, 2, 64], BF16)
    KTRN = sb("KTRN", [128, 64], BF16)
    VN = sb("VN", [64, 257], BF16)
    XB = sb("XB", [128, 2, D], F32)
    XN = sb("XN", [128, D], BF16)
    XNT2 = [sb("XNT0", [128, 8, 256], BF16)]
    XNT2.append(XNT2[0])
    QTCH = sb("QTCH", [128, 2, 8, 2], BF16)
    QTRH = sb("QTRH", [128, 8, 2], BF16)
    ONH = sb("ONH", [16, 256], BF16)
    OTH = sb("OTH", [128, 2, 8, 2], BF16)
    CATMH = sb("CATMH", [128, 4, 2], BF16)
    CATRH = sb("CATRH", [128, 4, 2], BF16)
    XH2 = sb("XH2", [128, 8, 2], BF16)
    HG = sb("HG", [128, 4, 2], F32)
    HL = sb("HL", [128, 4, 2], F32)
    PTH = sb("PTH", [128, 64], BF16)
    RCPH = sb("RCPH", [16, 1], F32)
    XN2T = sb("XN2T", [128, 8, 258], BF16)
    CKV = sb("CKV", [128, 1, NKV], F32)
    KR = sb("KR", [128, 1, NRO], F32)
    KRB4 = sb("KRB4", [128, 2, 4, NRO], BF16)
    KRB = KRB4[:, 0, 0, :]
    ST = sb("ST", [128, 32], F32)
    T1 = sb("T1", [128, 8, 32], F32)
    T2 = sb("T2", [128, 8, 32], F32)
    CQN = sb("CQN", [128, NQ], BF16)
    CQNT = sb("CQNT", [128, 3, 128], BF16)
    QNT = sb("QNT", [128, 4, 128], BF16)
    QTC2 = [sb("QTC%d" % i, [128, 2, 8, 128], BF16) for i in range(2)]
    QR = sb("QR", [128, 8, 32], BF16)
    QTR2 = [sb("QTR%d" % i, [128, 8, 128], BF16) for i in range(2)]
    NPT = 2
    PT = [sb("PT%d" % i, [128, 512], BF16) for i in range(NPT)]
    PTD = [sb("PTD%d" % i, [128, 4, 128], BF16) for i in range(1)]
    ON = sb("ON", [128, 4, 256], BF16)
    OT2 = [sb("OT%d" % i, [128, 2, 8, 128], BF16) for i in range(2)]
    RCP = sb("RCP", [128, 8], F32)
    OML = sb("OML", [128, 512], BF16)
    CATM = sb("CATM", [128, 4, 256], BF16)
    CATR = [sb("CATR%d" % i, [128, 4, 256], BF16) for i in range(2)]
    XTMP = sb("XTMP", [128, D], F32)
    RX = sb("RX", [128, 4, 4 * 19 + 200], F32)
    RGT = {n: sb("RG_" + n, [128, 256], F32) for n in ["XC", "R", "I", "A", "M", "G"]}
    XCB = sb("XCB", [128, 256], BF16)
    HST = sb("HST", [128, 4, 4], F32)
    WU = [sb("WU%d" % i, [128, 8, 256], BF16) for i in range(2)]
    WD = [sb("WD%d" % i, [128, D], BF16) for i in range(2)]
    UA = [sb("UA%d" % i, [128, 72], F32) for i in range(1)]
    UBf = [sb("UB%d" % i, [128, 72], F32) for i in range(1)]
    ACA = [sb("ACA%d" % i, [128, 256], F32) for i in range(2)]
    ACB = [sb("ACB%d" % i, [128, 256], F32) for i in range(2)]
    GA = [sb("GA%d" % i, [128, 256], F32) for i in range(1)]
    HT = [sb("HT%d" % i, [128, 256], BF16) for i in range(2)]
    UH = [sb("UH%d" % i, [128, 44, 4, 2], F32) for i in range(2)]

    PS = [es.enter_context(nc.psum_tensor("PS%d" % i, [128, 512], F32)) for i in range(8)]

    def psb(b):
        return PS[b][:].bitcast(BF16)

    R = {}

    def res(name):
        if name not in R:
            R[name] = Res(name)
        return R[name]

    PSR = [Res("ps%d" % b, excl=True) for b in range(8)]

    def psr(b, lo=0, hi=512):
        return [PSR[b]]

    KVR = [res("kv%d" % i) for i in range(NF)]
    SETUP = res("setup")
    sgrp = []

    def MM(out, lhsT, rhs, start, stop, reads, writes):
        P.add("pe", lambda: nc.tensor.matmul(out, lhsT, rhs, start=start, stop=stop), reads, writes)

    def TR(out, in_, ident, reads, writes):
        P.add("pe", lambda: nc.tensor.transpose(out, in_, ident), reads, writes)

    def ACT(out, in_, func, reads, writes, bias=None, scale=None, accum=None):
        kw = {}
        if bias is not None:
            kw["bias"] = bias
        if scale is not None:
            kw["scale"] = scale
        if accum is not None:
            kw["accum_out"] = accum
        P.add("act", lambda: nc.scalar.activation(out=out, in_=in_, func=func, **kw), reads, writes)

    def TS(eng, out, in0, s1, s2, op0, op1, reads, writes):
        e = nc.vector if eng == "dve" else nc.gpsimd
        if op1 is None:
            P.add(eng, lambda: e.tensor_scalar(out=out, in0=in0, scalar1=s1, scalar2=None, op0=op0), reads, writes)
        else:
            P.add(eng, lambda: e.tensor_scalar(out=out, in0=in0, scalar1=s1, scalar2=s2, op0=op0, op1=op1),
                  reads, writes)

    def STT(out, in0, scalar, in1, op0, op1, reads, writes):
        P.add("dve", lambda: nc.vector.scalar_tensor_tensor(out=out, in0=in0, scalar=scalar, in1=in1,
                                                           op0=op0, op1=op1), reads, writes)

    def TT(eng, out, in0, in1, op, reads, writes):
        e = nc.vector if eng == "dve" else nc.gpsimd
        P.add(eng, lambda: e.tensor_tensor(out=out, in0=in0, in1=in1, op=op), reads, writes)

    def CP(eng, out, in_, reads, writes):
        if eng == "act":
            P.add("act", lambda: nc.scalar.copy(out=out, in_=in_), reads, writes)
        else:
            e = nc.vector if eng == "dve" else nc.gpsimd
            P.add(eng, lambda: e.tensor_copy(out=out, in_=in_), reads, writes)

    def DMA(q, out, in_, reads, writes, anchor, grp=None, slow=False):
        e = {"sp": nc.sync, "act": nc.scalar, "pool": nc.gpsimd}[q]
        if slow:
            P.add(q, lambda: e.dma_start(out=out, in_=in_, allow_slow_non_contiguous=True), reads, writes,
                  dma=anchor, grp=grp)
        else:
            P.add(q, lambda: e.dma_start(out=out, in_=in_), reads, writes, dma=anchor, grp=grp)

    def MEMSET(eng, ap, val, writes):
        e = nc.vector if eng == "dve" else nc.gpsimd
        P.add(eng, lambda: e.memset(ap, val), (), writes)

    rW = res("weights")
    rC = res("consts")
    for (dst, src) in [(GFIN, gfin_d), (GKV, gkv_d), (COSS, cossd), (SINS, sinsd),
                       (CST, cst)]:
        DMA("sp", dst[:], src, (), [rC], SETUP, grp=sgrp)
    SETUP_P = res("setup_p")
    sgrp_p = []
    for (dst, src) in [(W_UKT, w_ukt_d), (W_UV, w_uv_d), (W_RGA, w_rga_d), (W_RGI, w_rgi_d),
                       (IDB, ident_d)]:
        DMA("pool", dst[:], src, (), [rW], SETUP_P, grp=sgrp_p)
    P.close_group(sgrp)
    P.close_group(sgrp_p)

    rT = res("setup_tmp")
    ACT(ST[:, 0:4], CST[:, C_LAM:C_LAM + 4], AF.Exp, [rC], [rT], scale=-1.0)
    MEMSET("pool", EPSB[:], EPS, [rC])
    MEMSET("pool", ONEB[:], 1.0, [rC])
    ACT(ST[:, 4:8], ST[:, 0:4], AF.Ln, [rT, rC], [rT], bias=ONEB[:, 0:1])
    TS("dve", CL[:, 0:4], ST[:, 4:8], -8.0, None, ALU.mult, None, [rT], [rC])
    TS("dve", CL[:, 4:8], ST[:, 4:8], -16.0, None, ALU.mult, None, [rT], [rC])
    TS("dve", NB[:, 0:4], CST[:, C_BA:C_BA + 4], -1.0, None, ALU.mult, None, [rC], [rC])
    TS("dve", NB[:, 4:8], CST[:, C_BI:C_BI + 4], -1.0, None, ALU.mult, None, [rC], [rC])

    STG = KTC[:].rearrange("p a s -> p (a s)")[:, 0:16384].bitcast(F32).rearrange("p (a s) -> p a s", a=4)
    STGB = V[:].rearrange("p a s -> p (a s)")[:, 0:2 * 2048].rearrange("p (a s) -> p a s", a=2)
    rS = [res("stg%d" % i) for i in range(4)]
    rSB = [res("stgb%d" % i) for i in range(2)]
    rWU = res("wup_s")
    rWU.multi = True
    rWD = res("wdn_s")
    rWD.multi = True
    si = 0
    rWRG = res("wrg_s")
    rWRG.multi = True
    bi_ = 0
    for k in range(8):
        sl = si % 4
        si += 1
        bl = bi_ % 2
        bi_ += 1
        DMA("sp", STG[:, sl, 0:INC], w_in_d[:, k, :], (), [rS[sl]], rS[sl])
        TS("dve", W_IN[:, k, :], STG[:, sl, 0:672], CST[:, C_GMIX + k:C_GMIX + k + 1], None, ALU.mult, None,
           [rS[sl], rC], [rW])
        sv = STG[:, sl, 672:INC].rearrange("p (t c n) -> p c t n", t=2, c=4)
        dv = STGB[:, bl, 0:1024].rearrange("p (c t n) -> p c t n", c=4, t=2)
        TS("dve", dv, sv, CST[:, C_GMIX + k:C_GMIX + k + 1], None, ALU.mult, None, [rS[sl], rC], [rSB[bl]])
        DMA("act", wrg_s[:, :, k, :], STGB[:, bl, 0:1024].rearrange("p (c n) -> p c n", c=4), [rSB[bl]], [rWRG],
            rSB[bl])
    for k in range(3):
        sl = si % 4
        si += 1
        DMA("sp", STG[:, sl, 0:768], w_uq_d[:, k, :], (), [rS[sl]], rS[sl])
        TS("dve", W_UQ[:, k, :], STG[:, sl, 0:768], CST[:, C_GQ + k:C_GQ + k + 1], None, ALU.mult, None,
           [rS[sl], rC], [rW])
    for k in range(8):
        for (c0, c1) in [(0, 2048), (2048, 4096), (4096, 2 * FF)]:
            sl = si % 4
            si += 1
            bl = bi_ % 2
            bi_ += 1
            n = c1 - c0
            DMA("sp", STG[:, sl, 0:n], w_up_d[:, k, c0:c1], (), [rS[sl]], rS[sl])
            TS("dve", STGB[:, bl, 0:n], STG[:, sl, 0:n], CST[:, C_GFFN + k:C_GFFN + k + 1], None, ALU.mult, None,
               [rS[sl], rC], [rSB[bl]])
            DMA("act", wup_s[:, c0 // 256:c1 // 256, k, :], STGB[:, bl, 0:n].rearrange("p (j n) -> p j n", n=256),
                [rSB[bl]], [rWU], rSB[bl])
    for j in range(0, NPAIR, 2):
        bl = bi_ % 2
        bi_ += 1
        DMA("pool", STGB[:, bl, :], w_dn_d[:, j:j + 2, :].rearrange("p a d -> p (a d)"), (), [rSB[bl]],
            res("stgbl%d" % bl))
        DMA("act", wdn_s[:, j:j + 2, :].rearrange("p a d -> p (a d)"), STGB[:, bl, :], [rSB[bl]], [rWD], rSB[bl])

    rWO = res("wout_s")
    rWO.multi = True
    for k in range(0, 8, 2):
        bl = bi_ % 2
        bi_ += 1
        DMA("pool", STGB[:, bl, :], w_out_d[:, k:k + 2, :].rearrange("p a d -> p (a d)"), (), [rSB[bl]],
            res("stgbl%d" % bl))
        DMA("act", wout_s[:, k:k + 2, :].rearrange("p a d -> p (a d)"), STGB[:, bl, :], [rSB[bl]], [rWO], rSB[bl])

    rPTD = [res("ptd0")]
    MEMSET("pool", PTD[0][:], 0.0, [rPTD[0]])
    rQT2 = [res("qt0"), res("qt1")]
    for i in range(2):
        MEMSET("pool", QTR2[i][:], 0.0, [rQT2[i]])
    rQTH = res("qth")
    MEMSET("pool", QTRH[:], 0.0, [rQTH])
    rHSTc = [res("hst%d" % c) for c in range(4)]
    rHST = rHSTc
    MEMSET("pool", HST[:], 0.0, rHSTc)
    rRXc = [res("rxh%d" % c) for c in range(4)]
    rRXH = rRXc
    MEMSET("pool", RX[:], 0.0, rRXc)
    rUH = [res("uh0"), res("uh1")]
    MEMSET("pool", UH[0][:], 0.0, [rUH[0]])
    MEMSET("pool", UH[1][:], 0.0, [rUH[1]])
    rVN = res("vn")
    MEMSET("pool", V[:, :, 256:257], 1.0, KVR + rS + rSB)
    MEMSET("pool", KTR[:, :], 0.0, KVR)
    MEMSET("pool", KTRN[:, :], 0.0, [rVN])
    MEMSET("pool", VN[:, 256:257], 1.0, [rVN])

    rXB = [res("xb0"), res("xb1")]
    rXN = res("xn")
    rXNT2 = [[res("xnt0_%d" % s_) for s_ in range(2)]]
    rXNT2.append(rXNT2[0])
    rXN2T = [res("xn2t0"), res("xn2t1")]
    rTAB = [res("tab0"), res("tab1")]
    stc = [0]

    def stcol():
        c = 8 + (stc[0] % 24)
        stc[0] += 1
        return c

    def rstd_from(Pn, src_ap, src_res, dim, junk_ap, junk_res):
        c = stcol()
        rr = res("stc%d" % c)
        ACT(junk_ap, src_ap, AF.Square, src_res, [junk_res, rr], accum=ST[:Pn, c:c + 1])
        ACT(ST[:Pn, c:c + 1], ST[:Pn, c:c + 1], AF.Ln, [rr], [rr], scale=1.0 / dim, bias=EPSB[:Pn, 0:1])
        ACT(ST[:Pn, c:c + 1], ST[:Pn, c:c + 1], AF.Exp, [rr], [rr], scale=-0.5)
        return ST[:Pn, c:c + 1], rr

    rXTMP = res("xtmp")

    def stage_a(xsrc, Pn, s, tab_src, use_tmp=False, xi=0):
        XNT, rXNT = XNT2[xi], rXNT2[xi]
        if use_tmp:
            xb, rxb = XTMP[:Pn, :], rXTMP
        else:
            xb, rxb = XB[:Pn, s, :], rXB[s]
        DMA("sp", xb, xsrc, (), [rxb], rxb)
        if tab_src is not None:
            DMA("sp", TAB[:Pn, s, :], tab_src, (), [rTAB[s]], rTAB[s])
        rs, rr = rstd_from(Pn, xb, [rxb], D, XN[:Pn, :], rXN)
        TS("dve", XN[:Pn, :], xb, rs, None, ALU.mult, None, [rxb, rr], [rXN])
        pb = psb(7)
        for k in range(8):
            TR(pb[:, k * 128:k * 128 + Pn], XN[:Pn, k * 128:(k + 1) * 128], IDB[:Pn, :Pn], [rXN, rW], psr(7))
        CP("act", XNT[:, :, s * Pn:(s + 1) * Pn], pb[:, 0:1024].rearrange("p (k t) -> p k t", k=8)[:, :, 0:Pn],
           psr(7), [rXNT[s]])

    rCKV = res("ckv")
    rKR = res("kr")
    rKRB = res("krb4_0")
    rT1 = res("t1")
    rT2 = res("t2")
    rOML = res("oml")

    def kv_side(Pn, s, cos_ap, sin_ap, tab_res, with_q, dst, out_ckv=None, out_kr=None, xi=0):
        XNT, rXNT = XNT2[xi], rXNT2[xi]
        tok = slice(s * Pn, (s + 1) * Pn)
        if with_q:
            for k in range(8):
                MM(PS[6][:Pn, 0:NQ], XNT[:, k, tok], W_IN[:, k, 0:NQ], k == 0, k == 7, [rXNT[s], rW], psr(6))
        if dst is None:
            return
        (ktc_ap, ktr_ap, v_ap, kvres) = dst
        for k in range(8):
            MM(PS[7][:Pn, 0:288], XNT[:, k, tok], W_IN[:, k, NQ:NQ + 288], k == 0, k == 7, [rXNT[s], rW], psr(7))
        rs, rr = rstd_from(Pn, PS[7][:Pn, 0:NKV], psr(7), NKV, OML[:Pn, 0:NKV], rOML)
        STT(CKV[:Pn, 0, :], PS[7][:Pn, 0:NKV], rs, GKV[:Pn, :], ALU.mult, ALU.mult, psr(7) + [rr, rC], [rCKV])
        xr = PS[7][:Pn, 256:288]
        TT("dve", T1[:Pn, 0, :].rearrange("p (a e) -> p a e", a=2), xr.rearrange("p (a e) -> p a e", a=2),
           cos_ap.unsqueeze(1).to_broadcast([Pn, 2, 16]), ALU.mult, psr(7) + tab_res, [rT1])
        TT("dve", T2[:Pn, 0, 0:16], PS[7][:Pn, 272:288], sin_ap, ALU.mult, psr(7) + tab_res, [rT2])
        TT("dve", T2[:Pn, 0, 16:32], PS[7][:Pn, 256:272], sin_ap, ALU.mult, psr(7) + tab_res, [rT2])
        TT("dve", KR[:Pn, 0, 0:16], T1[:Pn, 0, 0:16], T2[:Pn, 0, 0:16], ALU.subtract, [rT1, rT2], [rKR])
        TT("dve", KR[:Pn, 0, 16:32], T1[:Pn, 0, 16:32], T2[:Pn, 0, 16:32], ALU.add, [rT1, rT2], [rKR])
        if out_ckv is not None:
            DMA("sp", out_ckv, CKV[:Pn, 0, :], [rCKV], (), rCKV)
            DMA("sp", out_kr, KR[:Pn, 0, :], [rKR], (), rKR)
        CP("pool", v_ap[:Pn, 0:NKV], CKV[:Pn, 0, :], [rCKV], kvres)
        CP("pool", KRB[:Pn, :], KR[:Pn, 0, :], [rKR], [rKRB])
        pb = psb(7)
        for c in range(2):
            TR(pb[:, c * 128:c * 128 + Pn], v_ap[:Pn, c * 128:(c + 1) * 128], IDB[:Pn, :Pn], kvres + [rW], psr(7))
        TR(pb[0:32, 256:256 + Pn], KRB[:Pn, :], IDB[:Pn, :Pn], [rKRB, rW], psr(7))
        CP("act", ktc_ap, pb[:, 0:256].rearrange("p (c t) -> p c t", c=2)[:, :, 0:Pn], psr(7), kvres)
        CP("act", ktr_ap, pb[0:32, 256:256 + Pn], psr(7), kvres)

    rRG = {n: res("rg_" + n) for n in RGT}
    rXCB = res("xcb")
    rCAT = [res("cat_mla0"), res("cat_mla1")]
    rCATR = [res("cat_rg0"), res("cat_rg1")]
    rWUs = [res("wu0"), res("wu1")]
    wuc = [0]
    rWDs = [res("wd0"), res("wd1")]
    wdc = [0]

    rACA = [res("aca0"), res("aca1")]
    rACB = [res("acb0"), res("acb1")]
    rGA = [res("ga0")]
    rHT = [res("ht0"), res("ht1")]
    rHG = res("hg")
    rHL = res("hl")
    rCATRH = res("catrh")
    RGS = [
        dict(XC=(RGT["XC"], rRG["XC"]), R=(RGT["R"], rRG["R"]), I=(RGT["I"], rRG["I"]), A=(RGT["A"], rRG["A"]),
             M=(RGT["M"], rRG["M"]), G=(RGT["G"], rRG["G"]), XCB=(XCB, rXCB)),
        dict(XC=(ACA[0], rACA[0]), R=(ACA[1], rACA[1]), I=(ACB[0], rACB[0]), A=(ACB[1], rACB[1]),
             M=(GA[0], rGA[0]), G=None, XCB=(HT[0], rHT[0])),
    ]

    def rg_branch(N, nseg, L, full, s_list, par=0, chunks=(0, 1, 2, 3), tset=0, banks=(6, 7), xi=0, wu_slot=None,
                  halo_gate=False):
        XNT, rXNT = XNT2[xi], rXNT2[xi]
        T = RGS[tset]
        (XC_, rXC_), (R_, rR_), (I_, rI_), (A_, rA_), (M_, rM_), (XCB_, rXCB_) = \
            T["XC"], T["R"], T["I"], T["A"], T["M"], T["XCB"]
        bx, bg_ = banks
        W = 3 + L
        rx_reads = [rXNT[s] for s in s_list]
        for c in chunks:
            rRX, rHS = rRXc[c], rHSTc[c]
            if wu_slot is None:
                sl = wuc[0] % 2
                wuc[0] += 1
            else:
                sl = wu_slot
            DMA("sp", WU[sl][:], wrg_s[:, c, :, :], [rWRG], [rWUs[sl]], rWUs[sl])
            rxc = RX[:, c, 0:nseg * W].rearrange("p (g w) -> p g w", g=nseg)
            for k in range(8):
                MM(PS[bx][:, 0:N], WU[sl][:, k, 0:128], XNT[:, k, 0:N], k == 0, k == 7,
                   rx_reads + [rWUs[sl]], psr(bx))
            CP("act", rxc[:, :, 3:3 + L], PS[bx][:, 0:N].rearrange("p (g l) -> p g l", g=nseg), psr(bx), [rRX])
            if full:
                (G_, rG_) = T["G"]
                for k in range(8):
                    MM(PS[bg_][:, 0:N], WU[sl][:, k, 128:256], XNT[:, k, 0:N], k == 0, k == 7,
                       rx_reads + [rWUs[sl]], psr(bg_))
                ACT(G_[:, 0:N], PS[bg_][:, 0:N], AF.Gelu_apprx_tanh, psr(bg_), [rG_])
            elif halo_gate:
                for k in range(8):
                    MM(PS[bg_][:, 0:2], WU[sl][:, k, 128:256], XNT[:, k, N - 2:N], k == 0, k == 7,
                       rx_reads + [rWUs[sl]], psr(bg_))
                CP("act", HG[:, c, :], PS[bg_][:, 0:2], psr(bg_), [rHG])
            xc = XC_[:, 0:N].rearrange("p (g l) -> p g l", g=nseg)
            wc = lambda k: CST[:, C_RGW + c * 4 + k:C_RGW + c * 4 + k + 1]
            TS("dve", xc, rxc[:, :, 0:L], wc(0), CST[:, C_RGB + c:C_RGB + c + 1], ALU.mult, ALU.add,
               [rRX, rC], [rXC_])
            for k in range(1, 4):
                STT(xc, rxc[:, :, k:k + L], wc(k), xc, ALU.mult, ALU.add, [rRX, rC, rXC_], [rXC_])
            CP("pool", rxc[:, :, 0:3], rxc[:, :, L:L + 3], [rRX], [rRX])
            CP("act", XCB_[:, 0:N], XC_[:, 0:N], [rXC_], [rXCB_])
            MM(PS[bx][:, 256:256 + N], W_RGA[:, c, :], XCB_[:, 0:N], True, True, [rXCB_, rW], psr(bx))
            MM(PS[bg_][:, 256:256 + N], W_RGI[:, c, :], XCB_[:, 0:N], True, True, [rXCB_, rW], psr(bg_))
            for (X_, rX_, bnk, col) in [(R_, rR_, bx, c), (I_, rI_, bg_, 4 + c)]:
                ACT(X_[:, 0:N], PS[bnk][:, 256:256 + N], AF.Exp, psr(bnk) + [rC], [rX_], scale=-1.0,
                    bias=NB[:, col:col + 1])
                TS("dve", X_[:, 0:N], X_[:, 0:N], 1.0, None, ALU.add, None, [rX_], [rX_])
                P.add("dve", (lambda X_=X_: nc.vector.reciprocal(out=X_[:, 0:N], in_=X_[:, 0:N])), [rX_], [rX_])
            ACT(A_[:, 0:N], R_[:, 0:N], AF.Exp, [rR_, rC], [rA_], scale=CL[:, c:c + 1])
            ACT(M_[:, 0:N], R_[:, 0:N], AF.Exp, [rR_, rC], [rM_], scale=CL[:, 4 + c:5 + c])
            ACT(M_[:, 0:N], M_[:, 0:N], AF.Ln, [rM_], [rM_], scale=-1.0, bias=ONEB[:, 0:1])
            ACT(M_[:, 0:N], M_[:, 0:N], AF.Exp, [rM_], [rM_], scale=0.5)
            TT("dve", I_[:, 0:N], I_[:, 0:N], XC_[:, 0:N], ALU.mult, [rI_, rXC_], [rI_])
            TT("dve", M_[:, 0:N], M_[:, 0:N], I_[:, 0:N], ALU.mult, [rM_, rI_], [rM_])
            for g in range(nseg):
                sl_ = slice(g * L, (g + 1) * L)
                P.add("dve", (lambda g=g, sl_=sl_, c=c: nc.vector.tensor_tensor_scan(
                    out=M_[:, sl_], data0=A_[:, sl_], data1=M_[:, sl_],
                    initial=HST[:, c, g:g + 1], op0=ALU.mult, op1=ALU.add)),
                    [rA_, rM_, rHS], [rM_])
            hv = M_[:, 0:N].rearrange("p (g l) -> p g l", g=nseg)
            CP("pool", HST[:, c, 0:nseg], hv[:, :, L - 1], [rM_], [rHS])
            if full:
                TT("dve", CATR[par][:, c, 0:N], M_[:, 0:N], G_[:, 0:N], ALU.mult, [rM_, rG_], [rCATR[par]])
            elif halo_gate:
                CP("pool", HL[:, c, :], M_[:, N - 2:N], [rM_], [rHL])
        if halo_gate and not full:
            ACT(HG[:, :, :], HG[:, :, :], AF.Gelu_apprx_tanh, [rHG], [rHG])
            TT("dve", CATRH[:, :, :], HL[:, :, :], HG[:, :, :], ALU.mult, [rHL, rHG], [rCATRH])

    rCQN = res("cqn")
    rCQNT = res("cqnt")
    rQNT = res("qnt")
    rQR = res("qr")

    def q_path(Pn, s, cos_ap, sin_ap, tab_res, qi=0):
        if qi == "h":
            QTC, QTR, rQT = QTCH, QTRH, rQTH
        else:
            QTC, QTR, rQT = QTC2[qi], QTR2[qi], rQT2[qi]
        rs, rr = rstd_from(Pn, PS[6][:Pn, 0:NQ], psr(6), NQ, CQN[:Pn, :], rCQN)
        TS("dve", CQN[:Pn, :], PS[6][:Pn, 0:NQ], rs, None, ALU.mult, None, psr(6) + [rr], [rCQN])
        pb = psb(6)
        for k in range(3):
            TR(pb[:, k * 128:k * 128 + Pn], CQN[:Pn, k * 128:(k + 1) * 128], IDB[:Pn, :Pn], [rCQN, rW], psr(6))
        CP("act", CQNT[:, :, 0:Pn], pb[:, 0:384].rearrange("p (k t) -> p k t", k=3)[:, :, 0:Pn], psr(6), [rCQNT])
        for m in range(4):
            for k in range(3):
                MM(PS[6][:, m * 128:m * 128 + Pn], W_UQ[:, k, m * 128:(m + 1) * 128], CQNT[:, k, 0:Pn],
                   k == 0, k == 2, [rCQNT, rW], psr(6))
        CP("dve", QNT[:, :, 0:Pn], PS[6][:, :].rearrange("p (m t) -> p m t", m=4)[:, :, 0:Pn], psr(6), [rQNT])
        for k in range(3):
            MM(PS[7][:Pn, 0:256], CQNT[:, k, 0:Pn], W_UQ[:, k, 512:768], k == 0, k == 2, [rCQNT, rW], psr(7))
        xr = PS[7][:Pn, 0:256].rearrange("p (h a e) -> p h a e", h=8, a=2)
        cb = cos_ap.unsqueeze(1).unsqueeze(1).to_broadcast([Pn, 8, 2, 16])
        sb_ = sin_ap.unsqueeze(1).to_broadcast([Pn, 8, 16])
        t1 = T1[:Pn, :, :].rearrange("p h (a e) -> p h a e", a=2)
        TT("dve", t1, xr, cb, ALU.mult, psr(7) + tab_res, [rT1])
        TT("dve", T2[:Pn, :, 0:16], xr[:, :, 1, :], sb_, ALU.mult, psr(7) + tab_res, [rT2])
        TT("dve", T2[:Pn, :, 16:32], xr[:, :, 0, :], sb_, ALU.mult, psr(7) + tab_res, [rT2])
        TT("dve", QR[:Pn, :, 0:16], T1[:Pn, :, 0:16], T2[:Pn, :, 0:16], ALU.subtract, [rT1, rT2], [rQR])
        TT("dve", QR[:Pn, :, 16:32], T1[:Pn, :, 16:32], T2[:Pn, :, 16:32], ALU.add, [rT1, rT2], [rQR])
        pb7 = psb(7)
        for h in range(8):
            TR(pb7[0:32, h * 128:h * 128 + Pn], QR[:Pn, h, :], IDB[:Pn, :Pn], [rQR, rW], psr(7))
        CP("act", QTR[0:32, :, 0:Pn], pb7[0:32, 0:1024].rearrange("p (h t) -> p h t", h=8)[:, :, 0:Pn], psr(7),
           [rQT])
        bsel = [6, 7]
        bi2 = 0
        for c in range(2):
            for g in range(2):
                b = bsel[bi2 % 2]
                bi2 += 1
                for hh in range(4):
                    h = g * 4 + hh
                    po = (h % 2) * 64
                    MM(PS[b][:, hh * 128:hh * 128 + Pn], W_UKT[:, h, c * 128:(c + 1) * 128],
                       QNT[:, h // 2, 0:Pn], True, True, [rQNT, rW], psr(b))
                CP("dve" if (bi2 % 2) else "act", QTC[:, c, g * 4:(g + 1) * 4, 0:Pn],
                   PS[b][:, :].rearrange("p (h t) -> p h t", h=4)[:, :, 0:Pn], psr(b), [rQT])

    rPT = [res("pt%d" % i) for i in range(NPT)]
    rON = res("on")
    rOT2 = [res("ot0"), res("ot1")]
    ptc = [0]
    scb = [0]
    rcpc = [0]

    def attn_group(qrhs_c, qrhs_r, ncol, keytiles, acc_banks, out_views, rQT, bg=None, bg_step=0, sviews=None,
                   pipelined=True, pt_pool=None, rcp_tile=None):
        nk = len(keytiles)
        sbank = {}

        def scores(ki):
            (kc, kr, va, kp, kvres, bias, isdiag) = keytiles[ki]
            if sviews is None:
                b, co = scb[0] % 2, 0
            else:
                b, co = sviews[scb[0] % len(sviews)]
            scb[0] += 1
            sbank[ki] = (b, co)
            sview = PS[b][:kp, co:co + ncol]
            MM(sview, kc(0), qrhs_c(0), True, False, kvres + [rQT], psr(b))
            MM(sview, kc(1), qrhs_c(1), False, False, kvres + [rQT], psr(b))
            MM(sview, kr, qrhs_r, False, True, kvres + [rQT], psr(b))

        def exp_pv(ki):
            (kc, kr, va, kp, kvres, bias, isdiag) = keytiles[ki]
            b, co = sbank[ki]
            sview = PS[b][:kp, co:co + ncol]
            isap = not isinstance(bias, float)
            rd = psr(b) + ([rC] if isap else [])
            if isdiag:
                pt_t, pt_r = PTD[0], rPTD[0]
                pv = pt_t[:].rearrange("p h t -> p (h t)")
                ACT(pv[0:64, 0:ncol], PS[b][0:64, 0:ncol], AF.Exp, rd, [pt_r], scale=SCALE,
                    bias=(bias[0:64, :] if isap else bias))
                s3 = PS[b][:, 0:512].rearrange("p (h t) -> p h t", h=4)
                ACT(pt_t[64:128, :, 64:128], s3[64:128, :, 64:128], AF.Exp, rd, [pt_r], scale=SCALE,
                    bias=(bias[64:128, :] if isap else bias))
            else:
                if pt_pool is None:
                    d = ptc[0] % NPT
                    ptc[0] += 1
                    pt_t, pt_r = PT[d], rPT[d]
                else:
                    pt_t, pt_r = pt_pool
                pv = pt_t[:]
                ACT(pv[:kp, 0:ncol], sview, AF.Exp, rd, [pt_r], scale=SCALE, bias=bias)
            for ai, (ab, lo, hi) in enumerate(acc_banks):
                MM(PS[ab][:hi - lo, 0:257], pv[:kp, lo:hi], va, ki == 0, ki == nk - 1, [pt_r] + kvres, psr(ab))

        if pipelined:
            scores(0)
        for ki in range(nk):
            if pipelined:
                if ki + 1 < nk:
                    scores(ki + 1)
            else:
                scores(ki)
            exp_pv(ki)
            if bg is not None:
                P.replay_sched(bg, max(16, bg_step))
        for ai, (ab, lo, hi) in enumerate(acc_banks):
            M = hi - lo
            if rcp_tile is None:
                c = rcpc[0] % 8
                rcpc[0] += 1
                rr = res("rcp%d" % c)
                RCP_ = RCP
            else:
                c = 0
                RCP_, rr = rcp_tile[0], rcp_tile[1]
            TS("dve", RCP_[:M, c:c + 1], PS[ab][:M, 256:257], 1e-30, None, ALU.add, None, psr(ab), [rr])
            P.add("dve", (lambda c=c, M=M, RCP_=RCP_: nc.vector.reciprocal(out=RCP_[:M, c:c + 1],
                                                                            in_=RCP_[:M, c:c + 1])), [rr], [rr])
            TS("dve", out_views[ai], PS[ab][:M, 0:256], RCP_[:M, c:c + 1], None, ALU.mult, None, psr(ab) + [rr],
               [rON if rcp_tile is None else rcp_tile[2]])

    def attention_prompt(i, qi, bg=None, drain=True):
        QTC, QTR, rQT, OT, rOTq = QTC2[qi], QTR2[qi], rQT2[qi], OT2[qi], rOT2[qi]
        bg_step = 0
        if bg is not None:
            bg_step = len(bg) // (2 * (i + 1)) + 1
        for g in range(2):
            kts = []
            for kt in range(i + 1):
                kc = (lambda c, kt=kt: KTC[:, c, kt * 128:(kt + 1) * 128])
                bias = CST[:, C_KB:C_KB + 1] if kt < 2 else 0.0
                kts.append((kc, KTR[:, kt * 128:(kt + 1) * 128], V[:, kt, :], 128, [KVR[kt]], bias, kt == i))
            qc = (lambda c, g=g: QTC[:, c, g * 4:(g + 1) * 4, :])
            qr = QTR[:, g * 4:(g + 1) * 4, :]
            accs = [(2 + hh, hh * 128, (hh + 1) * 128) for hh in range(4)]
            outs = [ON[:, hh, :] for hh in range(4)]
            attn_group(qc, qr, 512, kts, accs, outs, rQT, bg, bg_step)
            for c in range(2):
                pb = psb(c)
                for hh in range(4):
                    TR(pb[:, hh * 128:(hh + 1) * 128], ON[:, hh, c * 128:(c + 1) * 128], IDB[:, :], [rON, rW],
                       psr(c))
                CP("act" if c == 0 else "dve", OT[:, c, g * 4:(g + 1) * 4, :],
                   pb[:, 0:512].rearrange("p (h t) -> p h t", h=4), psr(c), [rOTq])
        if bg is not None and drain:
            while bg:
                P.replay_sched(bg, 10)

    rXH = res("xn2t_halo")

    def o_mla_and_out(Pn, OT, rOT, xb, rxb, catm, rcatm, catr, rcatr, xn2_dst, rxn2):
        for h in range(8):
            for c in range(2):
                MM(PS[6][:Pn, h * 64:(h + 1) * 64], OT[:, c, h, 0:Pn], W_UV[:, c, h * 64:(h + 1) * 64], c == 0, c == 1,
                   [rOT, rW], psr(6))
        CP("act", OML[:Pn, :], PS[6][:Pn, 0:512], psr(6), [rOML])
        pb = psb(7)
        for k in range(4):
            TR(pb[:, k * 128:k * 128 + Pn], OML[:Pn, k * 128:(k + 1) * 128], IDB[:Pn, :Pn], [rOML, rW], psr(7))
        CP("dve", catm, pb[:, 0:512].rearrange("p (k t) -> p k t", k=4)[:, :, 0:Pn], psr(7), [rcatm])
        for k in range(8):
            sl = wdc[0] % 2
            wdc[0] += 1
            DMA("sp", WD[sl][:], wout_s[:, k, :], [rWO], [rWDs[sl]], rWDs[sl])
            lhs = catm[:, k, :] if k < 4 else catr[:, k - 4, :]
            for n in range(2):
                MM(PS[6 + n][:Pn, :], lhs, WD[sl][:, n * 512:(n + 1) * 512], k == 0, k == 7,
                   [rcatm, rcatr, rWDs[sl]], psr(6 + n))
        for n in range(2):
            TT("dve", xb[:, n * 512:(n + 1) * 512], PS[6 + n][:Pn, :], xb[:, n * 512:(n + 1) * 512], ALU.add,
               psr(6 + n) + [rxb], [rxb])
        rs, rr = rstd_from(Pn, xb, [rxb], D, XN[:Pn, :], rXN)
        TS("dve", XN[:Pn, :], xb, rs, None, ALU.mult, None, [rxb, rr], [rXN])
        pb = psb(7)
        for k in range(8):
            TR(pb[:, k * 128:k * 128 + Pn], XN[:Pn, k * 128:(k + 1) * 128], IDB[:Pn, :Pn], [rXN, rW], psr(7))
        CP("act", xn2_dst, pb[:, 0:1024].rearrange("p (k t) -> p k t", k=8)[:, :, 0:Pn], psr(7), rxn2)

    rUA = [res("ua0")]
    rUB = [res("ub0")]

    def ffn_prompt(nsub, out_rows, save_state):
        N = 128 * nsub
        NW = N + 2
        Pn = 128
        xr = [rXN2T[s] for s in range(nsub)] + [rXH]
        up_banks = [(0, 1), (6, 7)]
        wus = {}

        def up(j):
            sl = wuc[0] % 2
            wuc[0] += 1
            DMA("sp", WU[sl][:], wup_s[:, j, :, :], [rWU], [rWUs[sl]], rWUs[sl])
            (ba, bb) = up_banks[j % 2]
            for k in range(8):
                MM(PS[ba][:, 0:NW], WU[sl][:, k, 0:128], XN2T[:, k, 0:NW], k == 0, k == 7, [rWUs[sl]] + xr, psr(ba))
            for k in range(8):
                MM(PS[bb][:, 0:NW], WU[sl][:, k, 128:256], XN2T[:, k, 0:NW], k == 0, k == 7, [rWUs[sl]] + xr, psr(bb))

        def e1(j):
            (ba, bb) = up_banks[j % 2]
            sl = j % 2
            for (isb, bank, AC, rAC) in [(0, ba, ACA[sl], rACA[sl]), (1, bb, ACB[sl], rACB[sl])]:
                ch = j + 22 * isb
                wc = lambda k: CST[:, C_FCW + ch * 3 + k:C_FCW + ch * 3 + k + 1]
                ACT(AC[:, 0:N], PS[bank][:, 2:NW], AF.Identity, psr(bank) + [rC], [rAC], scale=wc(2),
                    bias=CST[:, C_FCB + ch:C_FCB + ch + 1])
                STT(AC[:, 0:N], PS[bank][:, 1:N + 1], wc(1), AC[:, 0:N], ALU.mult, ALU.add, psr(bank) + [rC, rAC], [rAC])
                STT(AC[:, 0:N], PS[bank][:, 0:N], wc(0), AC[:, 0:N], ALU.mult, ALU.add, psr(bank) + [rC, rAC], [rAC])
                if save_state:
                    CP("act", UH[0][:, ch, 0, :], PS[bank][:, N:NW], psr(bank), [rUH[0]])

        def e2(j):
            sl = j % 2
            ACT(GA[0][:, 0:N], ACA[sl][:, 0:N], AF.Gelu_apprx_tanh, [rACA[sl]], [rGA[0]])
            TT("pool", HT[sl][:, 0:N], GA[0][:, 0:N], ACB[sl][:, 0:N], ALU.mult, [rGA[0], rACB[sl]], [rHT[sl]])

        def down(j):
            sl = wdc[0] % 2
            wdc[0] += 1
            hs = j % 2
            DMA("sp", WD[sl][:], wdn_s[:, j, :], [rWD], [rWDs[sl]], rWDs[sl])
            for s in range(nsub):
                for n in range(2):
                    ab = 2 + s * 2 + n
                    MM(PS[ab][:Pn, :], HT[hs][:, s * Pn:(s + 1) * Pn], WD[sl][:, n * 512:(n + 1) * 512],
                       j == 0, j == NPAIR - 1, [rHT[hs], rWDs[sl]], psr(ab))

        up(0)
        up(1)
        e1(0)
        for j in range(NPAIR):
            if j + 2 < NPAIR:
                up(j + 2)
            if j + 1 < NPAIR:
                e1(j + 1)
            e2(j)
            down(j)
        for s in range(nsub):
            for n in range(2):
                ab = 2 + s * 2 + n
                TT("dve", XB[:Pn, s, n * 512:(n + 1) * 512], PS[ab][:Pn, :], XB[:Pn, s, n * 512:(n + 1) * 512],
                   ALU.add, psr(ab) + [rXB[s]], [rXB[s]])
            rs, rr = rstd_from(Pn, XB[:Pn, s, :], [rXB[s]], D, XN[:Pn, :], rXN)
            STT(XB[:Pn, s, :], XB[:Pn, s, :], rs, GFIN[:Pn, :], ALU.mult, ALU.mult, [rXB[s], rr, rC], [rXB[s]])
            DMA("sp", out_rows(s), XB[:Pn, s, :], [rXB[s]], (), rXB[s])

    def ffn(N, nseg, L, nsub, Pn, uh_prev, uh_next, out_rows, only_halo=False):
        Wd = 2 + L
        xr = [rXN2T[s] for s in range(nsub)]
        up_banks = [(0, 1), (6, 7)]
        wus = {}

        def up(j):
            sl = wuc[0] % 2
            wuc[0] += 1
            wus[j] = sl
            DMA("sp", WU[sl][:], wup_s[:, j, :, :], [rWU], [rWUs[sl]], rWUs[sl])
            (ba, bb) = up_banks[j % 2]
            for k in range(8):
                MM(PS[ba][:, 0:N], WU[sl][:, k, 0:128], XN2T[:, k, 0:N], k == 0, k == 7, [rWUs[sl]] + xr, psr(ba, 0, N))
            for k in range(8):
                MM(PS[bb][:, 0:N], WU[sl][:, k, 128:256], XN2T[:, k, 0:N], k == 0, k == 7, [rWUs[sl]] + xr,
                   psr(bb, 0, N))

        def elem(j):
            (ba, bb) = up_banks[j % 2]
            for (isb, bank, U_, rU, AC, rAC) in [(0, ba, UA[0], rUA[0], ACA[0], rACA[0]),
                                                 (1, bb, UBf[0], rUB[0], ACB[0], rACB[0])]:
                ch = j + 22 * isb
                uv = U_[:, 0:nseg * Wd].rearrange("p (g w) -> p g w", g=nseg)
                CP("act", uv[:, :, 2:2 + L], PS[bank][:, 0:N].rearrange("p (g l) -> p g l", g=nseg),
                   psr(bank, 0, N), [rU])
                CP("pool", uv[:, :, 0:2], UH[uh_prev][:, ch, 0:nseg, :], [rUH[uh_prev]], [rU])
                CP("pool", UH[uh_next][:, ch, 0:nseg, :], uv[:, :, L:L + 2], [rU], [rUH[uh_next]])
                if only_halo:
                    continue
                av = AC[:, 0:N].rearrange("p (g l) -> p g l", g=nseg)
                wc = lambda k: CST[:, C_FCW + ch * 3 + k:C_FCW + ch * 3 + k + 1]
                TS("dve", av, uv[:, :, 2:2 + L], wc(2), CST[:, C_FCB + ch:C_FCB + ch + 1], ALU.mult, ALU.add,
                   [rU, rC], [rAC])
                STT(av, uv[:, :, 1:1 + L], wc(1), av, ALU.mult, ALU.add, [rU, rC, rAC], [rAC])
                STT(av, uv[:, :, 0:L], wc(0), av, ALU.mult, ALU.add, [rU, rC, rAC], [rAC])
            if only_halo:
                return
            ACT(GA[0][:, 0:N], ACA[0][:, 0:N], AF.Gelu_apprx_tanh, [rACA[0]], [rGA[0]])
            hs = j % 2
            TT("pool", HT[hs][:, 0:N], GA[0][:, 0:N], ACB[0][:, 0:N], ALU.mult, [rGA[0], rACB[0]], [rHT[hs]])

        def down(j):
            sl = wdc[0] % 2
            wdc[0] += 1
            hs = j % 2
            DMA("sp", WD[sl][:], wdn_s[:, j, :], [rWD], [rWDs[sl]], rWDs[sl])
            for s in range(nsub):
                for n in range(2):
                    ab = 2 + s * 2 + n
                    MM(PS[ab][:Pn, :], HT[hs][:, s * Pn:(s + 1) * Pn], WD[sl][:, n * 512:(n + 1) * 512],
                       j == 0, j == NPAIR - 1, [rHT[hs], rWDs[sl]], psr(ab))

        if only_halo:
            for j in range(NPAIR):
                up(j)
                elem(j)
            return
        up(0)
        for j in range(NPAIR):
            if j + 1 < NPAIR:
                up(j + 1)
            elem(j)
            down(j)
        for s in range(nsub):
            for n in range(2):
                ab = 2 + s * 2 + n
                TT("dve", XB[:Pn, s, n * 512:(n + 1) * 512], PS[ab][:Pn, :], XB[:Pn, s, n * 512:(n + 1) * 512],
                   ALU.add, psr(ab) + [rXB[s]], [rXB[s]])
            rs, rr = rstd_from(Pn, XB[:Pn, s, :], [rXB[s]], D, XN[:Pn, :], rXN)
            STT(XB[:Pn, s, :], XB[:Pn, s, :], rs, GFIN[:Pn, :], ALU.mult, ALU.mult, [rXB[s], rr, rC], [rXB[s]])
            DMA("sp", out_rows(s), XB[:Pn, s, :], [rXB[s]], (), rXB[s])

    try:
        _ck('setup')
        def xrows(i):
            return xp[i * 128:(i + 1) * 128, :]

        def kvdst(i):
            return (KTC[:, :, i * 128:(i + 1) * 128], KTR[0:32, i * 128:(i + 1) * 128], V[:, i, :], [KVR[i]])

        def tab(s, Pn=128):
            return (TAB[:Pn, s, 0:16], TAB[:Pn, s, 16:32], [rTAB[s]])

        def pre_stage(tiles):
            for s, f in enumerate(tiles):
                stage_a(xrows(f), 128, s, tabd[f], use_tmp=True)

        def pre_rg(par, other=False):
            if other:
                rg_branch(256, 1, 256, False, [0, 1], par=par, halo_gate=True)
            else:
                rg_branch(256, 1, 256, True, [0, 1], par=par)

        def pre_pair(tiles, par, other=False):
            for s, f in enumerate(tiles):
                stage_a(xrows(f), 128, s, tabd[f], use_tmp=True)
            if other:
                rg_branch(256, 1, 256, False, [0, 1], par=par, halo_gate=True)
            else:
                rg_branch(256, 1, 256, True, [0, 1], par=par)

        def kside_pair(tiles):
            for s, f in enumerate(tiles):
                c_, s_, tr_ = tab(s)
                kv_side(128, s, c_, s_, tr_, False, kvdst(f))

        def front(f, s, qi, oi):
            c_, s_, tr_ = tab(s)
            kv_side(128, s, c_, s_, tr_, True, kvdst(f),
                    pckv_d[oi * 128:(oi + 1) * 128, :], pkr_d[oi * 128:(oi + 1) * 128, :])
            q_path(128, s, c_, s_, tr_, qi)

        def xload(tiles):
            for s, f in enumerate(tiles):
                DMA("sp", XB[:, s, :], xrows(f), (), [rXB[s]], rXB[s])

        def back(s, qi, par):
            tok = slice(s * 128, (s + 1) * 128)
            o_mla_and_out(128, OT2[qi], rOT2[qi], XB[:, s, :], rXB[s], CATM[:, :, tok], rCAT[s],
                          CATR[par][:, :, tok], rCATR[par], XN2T[:, :, 2 + s * 128:2 + (s + 1) * 128], [rXN2T[s]])

        rPTH = res("pth")
        rRCPH = res("rcph")
        rONH = res("onh")
        rOTH = res("oth")
        rCATMH = res("catmh")
        rXH2 = res("xh2")

        def halo(f):
            stage_a(xp[f * 128 + 126:f * 128 + 128, :], 2, 0, tabd[f][126:128, :], use_tmp=True)
            c_, s_, tr_ = tab(0, 2)
            kv_side(2, 0, c_, s_, tr_, True, None)
            q_path(2, 0, c_, s_, tr_, "h")
            groups = [[0, 1]] + [list(range(a, min(a + 4, f + 1))) for a in range(2, f + 1, 4)]
            for gi, grp_ in enumerate(groups):
                for j, kt in enumerate(grp_):
                    sv = PS[6][:, 16 * j:16 * j + 16]
                    MM(sv, KTC[:, 0, kt * 128:(kt + 1) * 128], QTCH[:, 0, :, :], True, False, [KVR[kt], rQTH], psr(6))
                    MM(sv, KTC[:, 1, kt * 128:(kt + 1) * 128], QTCH[:, 1, :, :], False, False, [KVR[kt], rQTH], psr(6))
                    MM(sv, KTR[:, kt * 128:(kt + 1) * 128], QTRH[:, :, :], False, True, [KVR[kt], rQTH], psr(6))
                ncol = 16 * len(grp_)
                if gi == 0:
                    ACT(PTH[:, 0:ncol], PS[6][:, 0:ncol], AF.Exp, psr(6) + [rC], [rPTH], scale=SCALE,
                        bias=CST[:, C_KB:C_KB + 1])
                else:
                    ACT(PTH[:, 0:ncol], PS[6][:, 0:ncol], AF.Exp, psr(6), [rPTH], scale=SCALE, bias=0.0)
                for j, kt in enumerate(grp_):
                    MM(PS[7][:16, 0:257], PTH[:, 16 * j:16 * j + 16], V[:, kt, :], kt == 0, kt == f,
                       [rPTH, KVR[kt]], psr(7))
            TS("dve", RCPH[:16, 0:1], PS[7][:16, 256:257], 1e-30, None, ALU.add, None, psr(7), [rRCPH])
            P.add("dve", (lambda: nc.vector.reciprocal(out=RCPH[:16, 0:1], in_=RCPH[:16, 0:1])), [rRCPH], [rRCPH])
            TS("dve", ONH[:16, :], PS[7][:16, 0:256], RCPH[:16, 0:1], None, ALU.mult, None, psr(7) + [rRCPH], [rONH])
            pb = psb(6)
            for c in range(2):
                TR(pb[:, c * 16:(c + 1) * 16], ONH[:16, c * 128:(c + 1) * 128], IDB[:16, :16], [rONH, rW], psr(6))
            CP("act", OTH[:, :, :, :], pb[:, 0:32].rearrange("p (c h q) -> p c h q", c=2, h=8), psr(6), [rOTH])
            o_mla_and_out(2, OTH, rOTH, XTMP[:2, :], rXTMP, CATMH[:, :, :], rCATMH, CATRH[:, :, :], rCATRH,
                          XH2[:, :, :], [rXH2])

        def catrh_from(par):
            CP("pool", CATRH[:, :, :], CATR[par][:, :, 254:256], [rCATR[par]], [rCATRH])

        pre_pair((0, 1), 1, other=True)
        kside_pair((0, 1))
        TS("pool", HST[:, :, 0:1], HST[:, :, 0:1], CST[:, C_FLAG:C_FLAG + 1], None, ALU.mult, None,
           rHSTc + [rC], rHSTc)
        halo(1)
        TS("pool", XN2T[:, :, 0:2], XH2[:, :, :], CST[:, C_FLAG:C_FLAG + 1], None, ALU.mult, None,
           [rXH2, rC], [rXH])
        pre_pair((2, 3), 0)
        xload((2, 3))
        front(2, 0, 0, 0)
        _ck('prefix')
        for k in range(16):
            f0, f1, g0, g1 = 2 + 4 * k, 3 + 4 * k, 4 + 4 * k, 5 + 4 * k
            par, parn = k % 2, (k + 1) % 2
            last = (k == 15)

            def l0a():
                front(f1, 1, 1, 2 * k + 1)

            def l0b():
                pre_pair((g0, g1), parn, other=True)
                if not last:
                    kside_pair((g0, g1))
            la = P.record(l0a)
            ida = set(id(t) for t in la)
            lst = la + P.record(l0b)
            attention_prompt(f0, 0, lst, drain=False)
            while any(id(t) in ida for t in lst):
                P.replay_sched(lst, 10)

            def l1():
                back(0, 0, par)
                if not last:
                    halo(g1)
                    pre_stage((f0 + 4, f1 + 4))
                    front(f0 + 4, 0, 0, 2 * k + 2)
                    pre_rg(parn)
            lst = lst + P.record(l1)
            attention_prompt(f1, 1, lst, drain=True)
            back(1, 1, par)
            base = 2 * k * 128
            ffn_prompt(2, lambda s: y_d[base + s * 128:base + (s + 1) * 128, :], last)
            if not last:
                CP("pool", XN2T[:, :, 0:2], XH2[:, :, :], [rXH2], [rXH])
                xload((f0 + 4, f1 + 4))
            _ck('own1')
        _ck('own')

        DMA("sp", prh_d, HST[:, :, 0], rHSTc, (), res("o_prh"), slow=True)
        DMA("sp", prc_d, RX[:, :, 0:3], rRXc, (), res("o_prc"), slow=True)
        DMA("sp", pfc_d, UH[0][:, :, 0, :], [rUH[0]], (), res("o_pfc"), slow=True)
        uh_cur = 0

        DMA("sp", HST[:, :, :], srh_d, (), rHSTc, res("l_srh"), slow=True)
        rxs = RX[:, :, 0:4 * 19].rearrange("p c (g w) -> p c g w", g=4)
        for c in range(4):
            DMA("sp", rxs[:, c, :, 0:3], src_d[:, c, :, :], (), [rRXc[c]], res("l_src%d" % c), slow=True)
        DMA("sp", UH[uh_cur][:, :, :, :], sfc_d, (), [rUH[uh_cur]], res("l_sfc"), slow=True)
        stg = [(XTMP[:, :].rearrange("p (t d) -> p t d", t=4), rXTMP),
               (XB[:, 0, :].rearrange("p (t d) -> p t d", t=4), rXB[0]),
               (XB[:, 1, :].rearrange("p (t d) -> p t d", t=4), rXB[1])]
        krs = [(T1[:, 0:4, :], rT1), (T2[:, 0:4, :], rT2)]
        rKRB4 = [res("krb4_0"), res("krb4_1")]
        cbanks = [4, 5, 6, 7]
        for g4 in range(NTILE // 4):
            (sv, sr) = stg[g4 % 3]
            (kv_, kr_) = krs[g4 % 2]
            kl = g4 % 2
            DMA("sp", sv, cckv_d[4 * g4:4 * g4 + 4].rearrange("t p d -> p t d"), (), [sr], sr)
            DMA("sp", kv_, ckr_d[4 * g4:4 * g4 + 4].rearrange("t p e -> p t e"), (), [kr_], kr_)
            CP("pool", KRB4[:, kl, :, :], kv_, [kr_], [rKRB4[kl]])
            for j in range(4):
                kt = 4 * g4 + j
                bnk = cbanks[kt % 4]
                CP("pool" if kt % 2 == 0 else "dve", V[:, kt, 0:NKV], sv[:, j, :], [sr], [KVR[kt]])
                pb = psb(bnk)
                for c in range(2):
                    TR(pb[:, c * 128:(c + 1) * 128], V[:, kt, c * 128:(c + 1) * 128], IDB[:, :], [KVR[kt], rW],
                       psr(bnk))
                TR(pb[0:32, 256:384], KRB4[:, kl, j, :], IDB[:, :], [rKRB4[kl], rW], psr(bnk))
                CP("act" if kt % 2 == 0 else "dve", KTC[:, :, kt * 128:(kt + 1) * 128],
                   pb[:, 0:256].rearrange("p (c t) -> p c t", c=2), psr(bnk), [KVR[kt]])
                CP("act", KTR[0:32, kt * 128:(kt + 1) * 128], pb[0:32, 256:384], psr(bnk), [KVR[kt]])

        stage_a(xs_d, 64, 0, None)
        rg_branch(64, 4, 16, True, [0], par=0)
        kv_side(64, 0, COSS[:, :], SINS[:, :], [rC], True, (KTCN[:, :, :], KTRN[0:32, :], VN[:, :], [rVN]),
                sckv_d, skr_d)
        q_path(64, 0, COSS[:, :], SINS[:, :], [rC])
        for sq in range(4):
            kts = []
            for j in range(16):
                kt = sq * 16 + j
                kc = (lambda c, kt=kt: KTC[:, c, kt * 128:(kt + 1) * 128])
                kts.append((kc, KTR[:, kt * 128:(kt + 1) * 128], V[:, kt, :], 128, [KVR[kt]], 0.0, False))
            kts.append(((lambda c: KTCN[:, c, :]), KTRN[:, :], VN[:, :], 64, [rVN],
                        CST[0:64, C_SMASK + sq:C_SMASK + sq + 1], False))
            qc = (lambda c, sq=sq: QTC2[0][:, c, :, sq * 16:(sq + 1) * 16])
            qr = QTR2[0][:, :, sq * 16:(sq + 1) * 16]
            attn_group(qc, qr, 128, kts, [(2 + sq, 0, 128)], [ON[:, sq, :]], rQT2[0])
        for sq in range(4):
            pb = psb(6)
            for c in range(2):
                TR(pb[:, c * 128:(c + 1) * 128], ON[:, sq, c * 128:(c + 1) * 128], IDB[:, :], [rON, rW], psr(6))
            CP("act", OT2[0][:, :, :, sq * 16:(sq + 1) * 16],
               pb[:, 0:256].rearrange("p (c h q) -> p c h q", c=2, h=8), psr(6), [rOT2[0]])
        o_mla_and_out(64, OT2[0], rOT2[0], XB[:64, 0, :], rXB[0], CATM[:, :, 0:64], rCAT[0],
                      CATR[0][:, :, 0:64], rCATR[0], XN2T[:, :, 0:64], [rXN2T[0], rXH])
        ffn(64, 4, 16, 1, 64, uh_cur, 1 - uh_cur, lambda s: ys_d)
        uh_cur = 1 - uh_cur
        DMA("sp", srh_o, HST[:, :, :], rHSTc, (), res("o_srh"), slow=True)
        for c in range(4):
            DMA("sp", src_o[:, c, :, :], rxs[:, c, :, 0:3], [rRXc[c]], (), res("o_src%d" % c), slow=True)
        DMA("sp", sfc_o, UH[uh_cur][:, :, :, :], [rUH[uh_cur]], (), res("o_sfc"), slow=True)
    except _Stop:
        pass

    P.emit()
    return nc, es, P


_CACHE = {}


def _prep_inputs(inp):
    f32 = np.float32
    g = lambda k: np.asarray(inp[k], dtype=f32)
    x_prompt = g("x_prompt")
    x_sample = g("x_sample")
    cache_ckv = g("cache_ckv")[0]
    cache_krope = g("cache_krope")[0]
    st_h = g("state_rg_h")[0]
    st_c = g("state_rg_conv")[0]
    st_f = g("state_ffn_conv")[0]
    w_in = g("w_in")[0]
    w_uq = g("w_uq")[0]
    w_uk = g("w_uk")[0]
    w_uv = g("w_uv")[0]
    w_out = g("w_out")[0]
    w_up = g("w_ffn_up")[0]
    w_dn = g("w_ffn_down")[0]

    def pm(a, k):
        return np.ascontiguousarray(a.reshape(k, 128, -1).transpose(1, 0, 2))

    def fm(v, k):
        return np.ascontiguousarray(v.reshape(k, 128).T)

    shared = {}
    shared["w_in"] = pm(w_in, 8)
    uq = np.concatenate([w_uq[:, :, :64].reshape(NQ, 512), w_uq[:, :, 64:].reshape(NQ, 256)], axis=1)
    shared["w_uq"] = pm(uq, 3)
    shared["w_out"] = pm(w_out, 8)
    ukt = np.zeros((128, 8, 256), f32)
    for h in range(8):
        ukt[(h % 2) * 64:(h % 2) * 64 + 64, h, :] = w_uk[:, h, :].T
    shared["w_ukt"] = ukt
    shared["w_uv"] = pm(w_uv.reshape(256, 512), 2)
    for nm, key in [("w_rga", "w_rg_a"), ("w_rgi", "w_rg_i")]:
        w = g(key)[0]
        bd = np.zeros((128, 4, 128), f32)
        for c in range(4):
            for b in range(2):
                bd[b * 64:(b + 1) * 64, c, b * 64:(b + 1) * 64] = w[2 * c + b]
        shared[nm] = bd
    upi = np.concatenate([w_up[:, :FF].reshape(D, NPAIR, 128), w_up[:, FF:].reshape(D, NPAIR, 128)], axis=2)
    shared["w_up"] = pm(upi.reshape(D, 2 * FF), 8)
    shared["w_dn"] = pm(w_dn, NPAIR)
    shared["gfin"] = np.ascontiguousarray(np.broadcast_to(g("final_norm_g")[None, :], (128, D)))
    shared["gkv"] = np.ascontiguousarray(np.broadcast_to(g("kv_norm_g")[0][None, :], (128, NKV)))
    shared["ident"] = np.eye(128, dtype=f32)

    cst = np.zeros((128, C_NCOL), f32)
    rgw = g("w_rg_conv")[0]
    cst[:, C_RGW:C_RGW + 16] = rgw.reshape(4, 4, 128).transpose(2, 1, 0).reshape(128, 16)
    cst[:, C_RGB:C_RGB + 4] = fm(g("b_rg_conv")[0], 4)
    cst[:, C_BA:C_BA + 4] = fm(g("b_rg_a")[0], 4)
    cst[:, C_BI:C_BI + 4] = fm(g("b_rg_i")[0], 4)
    cst[:, C_LAM:C_LAM + 4] = fm(g("rg_lambda")[0], 4)
    fcw = g("w_ffn_conv")[0]
    cst[:, C_FCW:C_FCW + 132] = fcw.reshape(3, 44, 128).transpose(2, 1, 0).reshape(128, 132)
    cst[:, C_FCB:C_FCB + 44] = fm(g("b_ffn_conv")[0], 44)
    cst[:, C_GMIX:C_GMIX + 8] = fm(g("norm_mix_g")[0], 8)
    cst[:, C_GFFN:C_GFFN + 8] = fm(g("norm_ffn_g")[0], 8)
    cst[:, C_GQ:C_GQ + 3] = fm(g("q_norm_g")[0], 3)
    for sq in range(4):
        cst[:, C_SMASK + sq] = NEGB
        cst[sq * 16:(sq + 1) * 16, C_SMASK + sq] = 0.0

    half_ = NRO // 2
    inv = (10000.0 ** (-np.arange(half_, dtype=f32) / half_)).astype(f32)

    def tables(pos):
        ang = pos.astype(f32)[:, None] * inv[None, :]
        return np.cos(ang).astype(f32), np.sin(ang).astype(f32)

    pos_s = (2048 + (np.arange(64) % 16)).astype(np.int64)
    cs_s, sn_s = tables(pos_s)

    in_maps = []
    for core in range(8):
        b, half = core // 2, core % 2
        m = dict(shared)
        zpad = np.zeros((256, D), f32)
        if half == 1:
            m["xp"] = np.concatenate([x_prompt[b], zpad], axis=0)
            pos = np.arange(NF * 128, dtype=np.int64)
        else:
            m["xp"] = np.concatenate([zpad, x_prompt[b]], axis=0)
            pos = np.arange(NF * 128, dtype=np.int64) - 256
        cs, sn = tables(np.abs(pos))
        sn = sn * np.sign(pos)[:, None].astype(f32)
        m["tabd"] = np.ascontiguousarray(np.concatenate([cs, sn], axis=1).reshape(NF, 128, 32))
        m["cossd"] = cs_s
        m["sinsd"] = sn_s
        c2 = cst.copy()
        c2[:, C_KB] = 0.0 if half == 1 else NEGB
        c2[:, C_FLAG] = float(half)
        m["cst"] = c2
        sl = slice(4 * core, 4 * core + 4)
        m["xs"] = np.ascontiguousarray(x_sample[sl].reshape(64, D))
        m["cckv"] = np.ascontiguousarray(cache_ckv[sl].reshape(NTILE, 128, NKV))
        m["ckr"] = np.ascontiguousarray(cache_krope[sl].reshape(NTILE, 128, NRO))
        m["srh"] = np.ascontiguousarray(st_h[sl].reshape(4, 4, 128).transpose(2, 1, 0))
        m["src"] = np.ascontiguousarray(st_c[sl].reshape(4, 3, 4, 128).transpose(3, 2, 0, 1))
        m["sfc"] = np.ascontiguousarray(st_f[sl].reshape(4, 2, 44, 128).transpose(3, 2, 0, 1))
        in_maps.append(m)
    return in_maps


def kernel(**inputs):
    if "prog" not in _CACHE:
        _CACHE["prog"] = build_program()
    nc, es, P = _CACHE["prog"]
    in_maps = _prep_inputs(inputs)
    res = run_bass_kernel_spmd(nc, in_maps, core_ids=list(range(8)))
    r = res.results
    f32 = np.float32
    B = 4
    y_prompt = np.zeros((B, SEQ, D), f32)
    p_ckv = np.zeros((1, B, SEQ, NKV), f32)
    p_krope = np.zeros((1, B, SEQ, NRO), f32)
    p_rg_h = np.zeros((1, B, RGW), f32)
    p_rg_conv = np.zeros((1, B, 3, RGW), f32)
    p_ffn_conv = np.zeros((1, B, 2, 2 * FF), f32)
    y_sample = np.zeros((32, 16, D), f32)
    s_ckv = np.zeros((1, 32, 16, NKV), f32)
    s_krope = np.zeros((1, 32, 16, NRO), f32)
    s_rg_h = np.zeros((1, 32, RGW), f32)
    s_rg_conv = np.zeros((1, 32, 3, RGW), f32)
    s_ffn_conv = np.zeros((1, 32, 2, 2 * FF), f32)
    for core in range(8):
        b, half = core // 2, core % 2
        o = r[core]
        for oi in range(32):
            st = 4 * (oi // 2) + 2 * half + (oi % 2)
            ts = slice(st * 128, (st + 1) * 128)
            os_ = slice(oi * 128, (oi + 1) * 128)
            y_prompt[b, ts] = o["y"][os_]
            p_ckv[0, b, ts] = o["pckv"][os_]
            p_krope[0, b, ts] = o["pkr"][os_]
        if half == 0:
            p_rg_h[0, b] = o["prh"].T.reshape(RGW)
            p_rg_conv[0, b] = o["prc"].transpose(2, 1, 0).reshape(3, RGW)
        else:
            p_ffn_conv[0, b] = o["pfc"].transpose(2, 1, 0).reshape(2, 2 * FF)
        sl = slice(4 * core, 4 * core + 4)
        y_sample[sl] = o["ys"].reshape(4, 16, D)
        s_ckv[0, sl] = o["sckv"].reshape(4, 16, NKV)
        s_krope[0, sl] = o["skr"].reshape(4, 16, NRO)
        s_rg_h[0, sl] = o["srho"].transpose(2, 1, 0).reshape(4, RGW)
        s_rg_conv[0, sl] = o["srco"].transpose(2, 3, 1, 0).reshape(4, 3, RGW)
        s_ffn_conv[0, sl] = o["sfco"].transpose(2, 3, 1, 0).reshape(4, 2, 2 * FF)
    return (y_prompt, y_sample, p_ckv, p_krope, p_rg_h, p_rg_conv, p_ffn_conv,
            s_ckv, s_krope, s_rg_h, s_rg_conv, s_ffn_conv)
```
